# Optimizing a Trainium2 kernel written in Bass

```python
import jax
import jax.numpy as jnp
from jax import lax
import numpy as np

D_MODEL = 1024
BATCH = 8
SEQ = 2048
DEPTH = 1

N_HEADS = 16
HEAD_DIM = 64
N_KV = 4
GROUP = N_HEADS // N_KV
ROT_DIM = HEAD_DIM // 4
ROPE_THETA = 500000.0
CMP_LEN = 32
CMP_STRIDE = 16
CMP_HIDDEN = 2 * HEAD_DIM
SEL_LEN = 64
N_SEL = 8
WINDOW = 512
Q_BLOCK = 128
SEL_Q_BLOCK = 64
POOL_WINDOWS = (2, 4, 8, 16)
POOL_WIDTH = D_MODEL // 2
POOL_GROUP = POOL_WIDTH // len(POOL_WINDOWS)
D_FF = 2816
EPS = 1e-6
NEG_INF = -1e30
FORCE_SCORE = 1e4
Q_WIDTH = N_HEADS * HEAD_DIM
KV_WIDTH = N_KV * HEAD_DIM
IN_WIDTHS = (Q_WIDTH, KV_WIDTH, KV_WIDTH, KV_WIDTH, KV_WIDTH, KV_WIDTH, KV_WIDTH, 3 * N_HEADS, POOL_WIDTH, 2 * D_MODEL)
IN_TOTAL = sum(IN_WIDTHS)

kernel_name = 'hybrid_nsa_pool_macaron_layer'


def rms_norm(x, g):
    xf = x.astype(jnp.float32)
    y = xf * lax.rsqrt(jnp.mean(xf * xf, axis=-1, keepdims=True) + EPS)
    return (y * g.astype(jnp.float32)).astype(x.dtype)


def swiglu(h, w_gate, w_up, w_down):
    return (jax.nn.silu(h @ w_gate) * (h @ w_up)) @ w_down


def partial_rope(x, pos):
    half = ROT_DIM // 2
    inv = ROPE_THETA ** (-jnp.arange(half, dtype=jnp.float32) * (2.0 / ROT_DIM))
    ang = pos.astype(jnp.float32)[..., None] * inv
    ang = ang.reshape(ang.shape[:2] + (1,) * (x.ndim - 3) + (half,))
    cos, sin = jnp.cos(ang), jnp.sin(ang)
    xr = x[..., :ROT_DIM].astype(jnp.float32)
    x1, x2 = xr[..., :half], xr[..., half:]
    rot = jnp.concatenate([x1 * cos - x2 * sin, x2 * cos + x1 * sin], axis=-1).astype(x.dtype)
    return jnp.concatenate([rot, x[..., ROT_DIM:]], axis=-1)


def compress_blocks(kv, pe, w1, w2):
    B, S = kv.shape[:2]
    n_cmp = (S - CMP_LEN) // CMP_STRIDE + 1
    idx = jnp.arange(n_cmp)[:, None] * CMP_STRIDE + jnp.arange(CMP_LEN)[None, :]
    blk = kv[:, idx] + pe[:, None, :]
    blk = jnp.moveaxis(blk, 3, 2).reshape(B, n_cmp, N_KV, CMP_LEN * HEAD_DIM)
    return jax.nn.gelu(blk @ w1) @ w2


def compressed_attention(q, kc, vc):
    S = q.shape[1]
    n = kc.shape[1]
    s = jnp.einsum('btgrd,bngd->btgrn', q, kc).astype(jnp.float32) * HEAD_DIM ** -0.5
    blk_end = jnp.arange(n) * CMP_STRIDE + CMP_LEN - 1
    valid = (blk_end[None, :] <= jnp.arange(S)[:, None])[None, :, None, None, :]
    p = jax.nn.softmax(jnp.where(valid, s, NEG_INF), axis=-1)
    p = jnp.where(valid, p, 0.0)
    o = jnp.einsum('btgrn,bngd->btgrd', p.astype(vc.dtype), vc)
    return o, p


def select_blocks(p_cmp, S):
    n_cmp = p_cmp.shape[-1]
    n_slc = S // SEL_LEN
    c0 = jnp.arange(n_cmp) * CMP_STRIDE
    s0 = jnp.arange(n_slc) * SEL_LEN
    ov = jnp.minimum(c0[:, None] + CMP_LEN, s0[None, :] + SEL_LEN) - jnp.maximum(c0[:, None], s0[None, :])
    overlap = jnp.clip(ov, 0).astype(jnp.float32) / CMP_LEN
    imp = jnp.einsum('btgrn,nj->btgj', p_cmp, overlap)
    t = jnp.arange(S)[:, None]
    blk = jnp.arange(n_slc)[None, :]
    forced = ((blk == t // SEL_LEN) | (blk == 0))[None, :, None, :]
    causal = (blk * SEL_LEN <= t)[None, :, None, :]
    imp = jnp.where(causal, jnp.where(forced, FORCE_SCORE, imp), NEG_INF)
    _, idx = lax.top_k(imp, min(N_SEL, n_slc))
    return idx


def selected_attention(q, ks, vs, idx):
    B, S, G, R, hd = q.shape
    n_slc = S // SEL_LEN
    k_top = idx.shape[-1]
    kb = ks.reshape(B, n_slc, SEL_LEN, G, hd).transpose(0, 3, 1, 2, 4)
    vb = vs.reshape(B, n_slc, SEL_LEN, G, hd).transpose(0, 3, 1, 2, 4)
    nq = S // SEL_Q_BLOCK
    qc = q.reshape(B, nq, SEL_Q_BLOCK, G, R, hd).transpose(1, 0, 2, 3, 4, 5)
    ic = idx.reshape(B, nq, SEL_Q_BLOCK, G, k_top).transpose(1, 0, 2, 3, 4)
    b_ix = jnp.arange(B)[:, None, None, None]
    g_ix = jnp.arange(G)[None, None, :, None]

    def chunk(args):
        qi, ii, c = args
        kg = kb[b_ix, g_ix, ii]
        vg = vb[b_ix, g_ix, ii]
        s = jnp.einsum('btgrd,btgkld->btgrkl', qi, kg).astype(jnp.float32) * HEAD_DIM ** -0.5
        t = c * SEL_Q_BLOCK + jnp.arange(SEL_Q_BLOCK)
        kpos = ii[..., None] * SEL_LEN + jnp.arange(SEL_LEN)
        mask = (kpos <= t[None, :, None, None, None])[:, :, :, None]
        s = jnp.where(mask, s, NEG_INF).reshape(B, SEL_Q_BLOCK, G, R, k_top * SEL_LEN)
        p = jax.nn.softmax(s, axis=-1).reshape(B, SEL_Q_BLOCK, G, R, k_top, SEL_LEN)
        return jnp.einsum('btgrkl,btgkld->btgrd', p.astype(vg.dtype), vg)

    o = lax.map(chunk, (qc, ic, jnp.arange(nq)))
    return o.transpose(1, 0, 2, 3, 4, 5).reshape(B, S, G, R, hd)


def window_attention(q, kw, vw):
    B, S, G, R, hd = q.shape
    nq = S // Q_BLOCK
    band = Q_BLOCK + WINDOW
    kp = jnp.pad(kw, ((0, 0), (WINDOW, 0), (0, 0), (0, 0)))
    vp = jnp.pad(vw, ((0, 0), (WINDOW, 0), (0, 0), (0, 0)))
    qc = q.reshape(B, nq, Q_BLOCK, G, R, hd).transpose(1, 0, 2, 3, 4, 5)

    def blk(args):
        qi, c = args
        start = c * Q_BLOCK
        kk = lax.dynamic_slice_in_dim(kp, start, band, axis=1)
        vv = lax.dynamic_slice_in_dim(vp, start, band, axis=1)
        s = jnp.einsum('btgrd,bsgd->btgrs', qi, kk).astype(jnp.float32) * HEAD_DIM ** -0.5
        t = start + jnp.arange(Q_BLOCK)
        kpos = start - WINDOW + jnp.arange(band)
        mask = (kpos[None, :] <= t[:, None]) & (kpos[None, :] > t[:, None] - WINDOW) & (kpos[None, :] >= 0)
        p = jax.nn.softmax(jnp.where(mask[None, :, None, None, :], s, NEG_INF), axis=-1)
        return jnp.einsum('btgrs,bsgd->btgrd', p.astype(vv.dtype), vv)

    o = lax.map(blk, (qc, jnp.arange(nq)))
    return o.transpose(1, 0, 2, 3, 4, 5).reshape(B, S, G, R, hd)


def pool_mixer(u, w_grp, scale):
    S = u.shape[1]
    uf = u.astype(jnp.float32)
    c = jnp.pad(jnp.cumsum(uf, axis=1), ((0, 0), (1, 0), (0, 0)))
    t1 = jnp.arange(1, S + 1)
    outs = []
    for gi, w in enumerate(POOL_WINDOWS):
        sl = slice(gi * POOL_GROUP, (gi + 1) * POOL_GROUP)
        cg = c[..., sl]
        lo = jnp.maximum(t1 - w, 0)
        cnt = jnp.minimum(t1, w).astype(jnp.float32)
        outs.append((cg[:, 1:] - cg[:, lo]) / cnt[None, :, None] - uf[..., sl])
    pooled = jnp.stack(outs, axis=2).astype(u.dtype)
    y = jnp.einsum('bsgc,gcd->bsgd', pooled, w_grp).reshape(u.shape)
    return y * scale


def token_mixer(h, positions, w_in, cmp_pe_k, cmp_w1_k, cmp_w2_k, cmp_pe_v, cmp_w1_v, cmp_w2_v,
                w_attn_branch, pool_w, pool_scale, w_pool_branch, w_out):
    B, S, _ = h.shape
    offs = [int(v) for v in np.cumsum(IN_WIDTHS)[:-1]]
    q, kc, vc, ks, vs, kw, vw, nsa_g, pool_in, merge_g = jnp.split(h @ w_in, offs, axis=-1)
    q = partial_rope(q.reshape(B, S, N_KV, GROUP, HEAD_DIM), positions)
    kv_shape = (B, S, N_KV, HEAD_DIM)
    ks = partial_rope(ks.reshape(kv_shape), positions)
    kw = partial_rope(kw.reshape(kv_shape), positions)
    n_cmp = (S - CMP_LEN) // CMP_STRIDE + 1
    pos_cmp = positions[:, jnp.arange(n_cmp) * CMP_STRIDE + CMP_LEN - 1]
    kcmp = partial_rope(compress_blocks(kc.reshape(kv_shape), cmp_pe_k, cmp_w1_k, cmp_w2_k), pos_cmp)
    vcmp = compress_blocks(vc.reshape(kv_shape), cmp_pe_v, cmp_w1_v, cmp_w2_v)
    o_cmp, p_cmp = compressed_attention(q, kcmp, vcmp)
    idx = select_blocks(p_cmp, S)
    o_sel = selected_attention(q, ks, vs.reshape(kv_shape), idx)
    o_win = window_attention(q, kw, vw.reshape(kv_shape))
    g = jax.nn.sigmoid(nsa_g.reshape(B, S, 3, N_KV, GROUP, 1))
    o_nsa = (g[:, :, 0] * o_cmp + g[:, :, 1] * o_sel + g[:, :, 2] * o_win).reshape(B, S, Q_WIDTH)
    o_pool = pool_mixer(pool_in, pool_w, pool_scale)
    g_attn, g_pool = jnp.split(jax.nn.sigmoid(merge_g), 2, axis=-1)
    y = g_attn * (o_nsa @ w_attn_branch) + g_pool * (o_pool @ w_pool_branch)
    return y @ w_out


def setup_inputs(seed: int = 0) -> dict:
    key = jax.random.key(seed)
    k = jax.random.split(key, 32)
    f32 = jnp.float32

    def w(kk, shape, fan_in):
        return jax.random.normal(kk, (DEPTH,) + shape, f32) * fan_in ** -0.5

    def gain(kk, n):
        return 1.0 + 0.05 * jax.random.normal(kk, (DEPTH, n), f32)

    x = jax.random.normal(k[0], (BATCH, SEQ, D_MODEL), f32)
    offsets = jax.random.randint(k[1], (BATCH, 1), 0, 4096, dtype=jnp.int32)
    positions = offsets + jnp.arange(SEQ, dtype=jnp.int32)[None, :]
    cmp_in = CMP_LEN * HEAD_DIM
    return {
        'x': x,
        'positions': positions,
        'g_ffn1_pre': gain(k[2], D_MODEL),
        'w_ffn1_gate': w(k[3], (D_MODEL, D_FF), D_MODEL),
        'w_ffn1_up': w(k[4], (D_MODEL, D_FF), D_MODEL),
        'w_ffn1_down': w(k[5], (D_FF, D_MODEL), D_FF),
        'g_ffn1_post': gain(k[6], D_MODEL),
        'g_mix_pre': gain(k[7], D_MODEL),
        'w_in': w(k[8], (D_MODEL, IN_TOTAL), D_MODEL),
        'cmp_pe_k': 0.5 * jax.random.normal(k[9], (DEPTH, CMP_LEN, HEAD_DIM), f32),
        'cmp_w1_k': w(k[10], (cmp_in, CMP_HIDDEN), cmp_in),
        'cmp_w2_k': w(k[11], (CMP_HIDDEN, HEAD_DIM), CMP_HIDDEN),
        'cmp_pe_v': 0.5 * jax.random.normal(k[12], (DEPTH, CMP_LEN, HEAD_DIM), f32),
        'cmp_w1_v': w(k[13], (cmp_in, CMP_HIDDEN), cmp_in),
        'cmp_w2_v': w(k[14], (CMP_HIDDEN, HEAD_DIM), CMP_HIDDEN),
        'w_attn_branch': w(k[15], (Q_WIDTH, D_MODEL), Q_WIDTH),
        'pool_w': w(k[16], (len(POOL_WINDOWS), POOL_GROUP, POOL_GROUP), POOL_GROUP),
        'pool_scale': 1.0 + 0.1 * jax.random.normal(k[17], (DEPTH, POOL_WIDTH), f32),
        'w_pool_branch': w(k[18], (POOL_WIDTH, D_MODEL), POOL_WIDTH),
        'w_out': w(k[19], (D_MODEL, D_MODEL), D_MODEL),
        'g_mix_post': gain(k[20], D_MODEL),
        'g_ffn2_pre': gain(k[21], D_MODEL),
        'w_ffn2_gate': w(k[22], (D_MODEL, D_FF), D_MODEL),
        'w_ffn2_up': w(k[23], (D_MODEL, D_FF), D_MODEL),
        'w_ffn2_down': w(k[24], (D_FF, D_MODEL), D_FF),
        'g_ffn2_post': gain(k[25], D_MODEL),
    }


def reference(x, positions, g_ffn1_pre, w_ffn1_gate, w_ffn1_up, w_ffn1_down, g_ffn1_post,
              g_mix_pre, w_in, cmp_pe_k, cmp_w1_k, cmp_w2_k, cmp_pe_v, cmp_w1_v, cmp_w2_v,
              w_attn_branch, pool_w, pool_scale, w_pool_branch, w_out, g_mix_post,
              g_ffn2_pre, w_ffn2_gate, w_ffn2_up, w_ffn2_down, g_ffn2_post):
    for l in range(DEPTH):
        h = swiglu(rms_norm(x, g_ffn1_pre[l]), w_ffn1_gate[l], w_ffn1_up[l], w_ffn1_down[l])
        x = x + 0.5 * rms_norm(h, g_ffn1_post[l])
        h = token_mixer(rms_norm(x, g_mix_pre[l]), positions, w_in[l],
                        cmp_pe_k[l], cmp_w1_k[l], cmp_w2_k[l], cmp_pe_v[l], cmp_w1_v[l], cmp_w2_v[l],
                        w_attn_branch[l], pool_w[l], pool_scale[l], w_pool_branch[l], w_out[l])
        x = x + rms_norm(h, g_mix_post[l])
        h = swiglu(rms_norm(x, g_ffn2_pre[l]), w_ffn2_gate[l], w_ffn2_up[l], w_ffn2_down[l])
        x = x + 0.5 * rms_norm(h, g_ffn2_post[l])
    return x
```

```python
import numpy as np
from contextlib import ExitStack, contextmanager
import concourse.bass as bass
import concourse.mybir as mybir
from concourse.bass_utils import run_bass_kernel_spmd

F32 = mybir.dt.float32
BF16 = mybir.dt.bfloat16
I32 = mybir.dt.int32
ALU = mybir.AluOpType
AF = mybir.ActivationFunctionType

ENGINES = ("pe", "act", "dve", "pool", "sp")
SEM_ROLL = 30000

D = 1024
S = 2048
DFF = 2816
NT = S // 128
NF = DFF // 128
KC = D // 128
NH = 16
HD = 64
NKV = 4
IN_TOTAL = 5168
EPS = 1e-6
NEG = -30000.0


class Buf:
    __slots__ = ("name", "w", "r")

    def __init__(self, name=""):
        self.name = name
        self.w = None
        self.r = []


class Ring:
    def __init__(self, items):
        self.items = items
        self.i = 0

    def next(self):
        it = self.items[self.i]
        self.i = (self.i + 1) % len(self.items)
        return it


class Prog:
    def __init__(self, nc, n_dma_sems=24):
        self.nc = nc
        self.es = ExitStack()
        self.scopes = [self.es]
        self.streams = {e: [] for e in ENGINES}
        self.sem = {}
        self.cnt = {}
        self.nsem = 0
        for e in ENGINES:
            self._new_engine_sem(e)
        self.known = {e: {} for e in ENGINES}
        self.dma_sems = [self.es.enter_context(nc.semaphore(f"dq{i}")) for i in range(n_dma_sems)]
        self.dma_cnt = [0] * n_dma_sems
        self.dma_rr = 0
        self.ninstr = 0
        self.uid = 0

    def _new_engine_sem(self, e):
        self.nsem += 1
        self.sem[e] = self.es.enter_context(self.nc.semaphore(f"s_{e}_{self.nsem}"))
        self.cnt[e] = 0

    @contextmanager
    def scope(self):
        es = ExitStack()
        self.scopes.append(es)
        try:
            yield
        finally:
            self.barrier()
            self.scopes.pop()
            es.close()

    def sb(self, name, shape, dt):
        self.uid += 1
        return self.scopes[-1].enter_context(self.nc.sbuf_tensor(f"{name}_{self.uid}", list(shape), dt))

    def ps(self, name, shape, dt):
        self.uid += 1
        return self.scopes[-1].enter_context(self.nc.psum_tensor(f"{name}_{self.uid}", list(shape), dt))

    def sb_ring(self, name, shape, dt, n):
        return Ring([(self.sb(f"{name}{i}", shape, dt), Buf(f"{name}{i}")) for i in range(n)])

    def _need(self, eng, tok, waits):
        if tok is None:
            return
        sem, val, teng = tok
        if teng == "pe" and eng == "pe":
            return
        sid = id(sem)
        if self.known[eng].get(sid, 0) >= val:
            return
        cur = waits.get(sid)
        if cur is None or cur[1] < val:
            waits[sid] = (sem, val)

    def _collect(self, eng, reads, writes):
        waits = {}
        for b in reads:
            self._need(eng, b.w, waits)
        for b in writes:
            self._need(eng, b.w, waits)
            for t in b.r:
                self._need(eng, t, waits)
        return waits

    def _emit_waits(self, eng, waits):
        st = self.streams[eng]
        for sid, (sem, val) in waits.items():
            st.append(lambda e, sem=sem, val=val: e.wait_ge(sem, val))
            self.known[eng][sid] = val

    def _commit(self, tok, reads, writes):
        for b in reads:
            b.r.append(tok)
            if len(b.r) > 48:
                best = {}
                for t in b.r:
                    k = id(t[0])
                    if k not in best or best[k][1] < t[1]:
                        best[k] = t
                b.r = list(best.values())
        for b in writes:
            b.w = tok
            b.r = []

    def op(self, eng, fn, reads=(), writes=(), inc=True):
        waits = self._collect(eng, reads, writes)
        self._emit_waits(eng, waits)
        self.ninstr += 1
        if inc:
            if self.cnt[eng] >= SEM_ROLL:
                self._new_engine_sem(eng)
            self.cnt[eng] += 1
            sem, val = self.sem[eng], self.cnt[eng]
            self.streams[eng].append(lambda e, fn=fn, sem=sem: fn(e).then_inc(sem, 1))
            tok = (sem, val, eng)
        else:
            self.streams[eng].append(lambda e, fn=fn: fn(e))
            tok = (self.sem[eng], self.cnt[eng] + 1, eng)
        self._commit(tok, reads, writes)
        return tok

    def dma(self, eng, fn, reads=(), writes=()):
        i = self.dma_rr
        self.dma_rr = (self.dma_rr + 1) % len(self.dma_sems)
        sem = self.dma_sems[i]
        waits = self._collect(eng, reads, writes)
        if self.dma_cnt[i] > 0:
            self._need(eng, (sem, self.dma_cnt[i], "dma"), waits)
        self._emit_waits(eng, waits)
        self.dma_cnt[i] += 16
        val = self.dma_cnt[i]
        self.streams[eng].append(lambda e, fn=fn, sem=sem: fn(e).then_inc(sem, 16))
        tok = (sem, val, "dma")
        self._commit(tok, reads, writes)
        self.ninstr += 1
        return tok

    def barrier(self):
        for eng in ENGINES:
            waits = {}
            for e2 in ENGINES:
                if e2 != eng and self.cnt[e2] > 0:
                    self._need(eng, (self.sem[e2], self.cnt[e2], e2), waits)
            if self.cnt[eng] > 0 and eng != "pe":
                self._need(eng, (self.sem[eng], self.cnt[eng], eng + "_self"), waits)
            for i, sem in enumerate(self.dma_sems):
                if self.dma_cnt[i] > 0:
                    self._need(eng, (sem, self.dma_cnt[i], "dma"), waits)
            self._emit_waits(eng, waits)

    def finish(self):
        nc = self.nc
        streams = self.streams
        self.barrier()
        with nc.Block() as block:
            @block.tensor
            def _(e):
                for f in streams["pe"]:
                    f(e)

            @block.scalar
            def _(e):
                for f in streams["act"]:
                    f(e)

            @block.vector
            def _(e):
                for f in streams["dve"]:
                    f(e)

            @block.gpsimd
            def _(e):
                for f in streams["pool"]:
                    f(e)

            @block.sync
            def _(e):
                for f in streams["sp"]:
                    f(e)
        self.es.close()

    def mm(self, out, lhsT, rhs, start, stop, reads, writes, inc=None):
        if inc is None:
            inc = stop
        return self.op("pe", lambda e: e.matmul(out, lhsT, rhs, start=start, stop=stop), reads, writes, inc=inc)

    def tr(self, out, in_, ident, reads, writes, inc=True):
        return self.op("pe", lambda e: e.transpose(out, in_, ident), reads, writes, inc=inc)

    def act(self, out, in_, func, reads, writes, **kw):
        return self.op("act", lambda e: e.activation(out=out, in_=in_, func=func, **kw), reads, writes)

    def tt(self, out, in0, in1, op, reads, writes, eng="dve"):
        return self.op(eng, lambda e: e.tensor_tensor(out=out, in0=in0, in1=in1, op=op), reads, writes)

    def ts(self, out, in0, s1, s2, op0, op1, reads, writes, eng="dve"):
        if op1 is None:
            return self.op(eng, lambda e: e.tensor_scalar(out=out, in0=in0, scalar1=s1, scalar2=None, op0=op0), reads, writes)
        return self.op(eng, lambda e: e.tensor_scalar(out=out, in0=in0, scalar1=s1, scalar2=s2, op0=op0, op1=op1), reads, writes)

    def stt(self, out, in0, scalar, in1, op0, op1, reads, writes):
        return self.op("dve", lambda e: e.scalar_tensor_tensor(out=out, in0=in0, scalar=scalar, in1=in1, op0=op0, op1=op1), reads, writes)

    def cp(self, out, in_, reads, writes, eng="dve"):
        return self.op(eng, lambda e: e.tensor_copy(out=out, in_=in_), reads, writes)

    def recip(self, out, in_, reads, writes):
        return self.op("dve", lambda e: e.reciprocal(out=out, in_=in_), reads, writes)

    def memset(self, out, val, writes, eng="pool"):
        return self.op(eng, lambda e: e.memset(out, val), (), writes)

    def load(self, out, in_, reads, writes, eng="sp"):
        return self.dma(eng, lambda e: e.dma_start(out=out, in_=in_), reads, writes)

    def load_nc(self, out, in_, reads, writes, eng="sp"):
        return self.dma(eng, lambda e: e.dma_start(out=out, in_=in_, allow_slow_non_contiguous=True), reads, writes)


class Ctx:
    pass


def norm_transpose_phase(P, C, src, Bsrc, g_dram, xnT, BxnT, tag):
    gbc = P.sb("gbc", [128, D], F32)
    Bg = Buf()
    P.load(gbc[:], g_dram.partition_broadcast(128), (), [Bg])
    xt_ring = P.sb_ring("xt", [128, D], F32, 2)
    xn_ring = P.sb_ring("xn", [128, D], BF16, 2)
    st_ring = P.sb_ring("st", [128, 4], F32, 3)
    for t in range(NT):
        xt, Bxt = xt_ring.next()
        xn, Bxn = xn_ring.next()
        st, Bst = st_ring.next()
        P.load(xt[:], src[t * 128:(t + 1) * 128, :], [Bsrc[t]], [Bxt])
        P.act(xn[:], xt[:], AF.Square, [Bxt], [Bxn, Bst], accum_out=st[:, 0:1])
        P.act(st[:, 1:2], st[:, 0:1], AF.Sqrt, [Bst], [Bst], scale=1.0 / D, bias=EPS)
        P.recip(st[:, 2:3], st[:, 1:2], [Bst], [Bst])
        P.stt(xn[:], xt[:], st[:, 2:3], gbc[:], ALU.mult, ALU.mult, [Bxt, Bst, Bg], [Bxn])
        pt, Bpt = C.pst.next()
        for k in range(KC):
            P.tr(pt[:, k * 128:(k + 1) * 128], xn[:, k * 128:(k + 1) * 128], C.ident[:], [Bxn, C.Bident], [Bpt], inc=(k == KC - 1))
        P.cp(xnT[:, :, t * 128:(t + 1) * 128], pt[:].rearrange("p (k c) -> p k c", k=KC), [Bpt], [BxnT[t]])


def post_norm_residual(P, C, po_list, Bpo_list, gbc, Bg, res_t, Bres, half_scale, out_dram, Bout, stt_ring, tmp_ring):
    st, Bst = stt_ring.next()
    junk, Bj = tmp_ring.next()
    P.act(junk[:], po_list[0][:], AF.Square, [Bpo_list[0]], [Bj, Bst], accum_out=st[:, 0:1])
    junk2, Bj2 = tmp_ring.next()
    P.act(junk2[:], po_list[1][:], AF.Square, [Bpo_list[1]], [Bj2, Bst], accum_out=st[:, 1:2])
    P.tt(st[:, 2:3], st[:, 0:1], st[:, 1:2], ALU.add, [Bst], [Bst])
    sc = 1.0 / (half_scale * half_scale)
    P.act(st[:, 3:4], st[:, 2:3], AF.Sqrt, [Bst], [Bst], scale=sc / D, bias=EPS * sc)
    P.recip(st[:, 4:5], st[:, 3:4], [Bst], [Bst])
    for hh in range(2):
        tmp, Bt = tmp_ring.next()
        P.stt(tmp[:], po_list[hh][:], st[:, 4:5], gbc[:, hh * 512:(hh + 1) * 512], ALU.mult, ALU.mult,
              [Bpo_list[hh], Bst, Bg], [Bt])
        P.tt(res_t[:, hh * 512:(hh + 1) * 512], res_t[:, hh * 512:(hh + 1) * 512], tmp[:], ALU.add, [Bres, Bt], [Bres])
    P.load(out_dram, res_t[:], [Bres], [Bout])


def ffn_block(P, C, src, Bsrc, dst, Bdst, g_pre, wg, wu, wd, g_post):
    FG = 2
    with P.scope():
        hT = P.sb("hT", [128, NF, S], BF16)
        BhT = [[Buf() for _ in range(4)] for _ in range(NF)]
        NA = 10
        WG = 2
        wd_v = wd.rearrange("(fc p) d -> p fc d", p=128)
        wdA = P.sb("wdA", [128, NA, D], BF16)
        Bwd = [Buf() for _ in range(NF // WG)]
        with P.scope():
            xnT = P.sb("xnT", [128, KC, S], BF16)
            BxnT = [Buf() for _ in range(NT)]
            norm_transpose_phase(P, C, src, Bsrc, g_pre, xnT, BxnT, "f")
            wg_ring = P.sb_ring("wg", [128, KC, FG * 128], BF16, 2)
            wu_ring = P.sb_ring("wu", [128, KC, FG * 128], BF16, 2)
            sg_ring = P.sb_ring("sg", [128, 512], F32, 2)
            wd_pending = list(range(NA // WG))
            wg_v = wg.rearrange("(kc p) f -> p kc f", p=128)
            wu_v = wu.rearrange("(kc p) f -> p kc f", p=128)
            for fg in range(NF // FG):
                wgt, Bwg = wg_ring.next()
                wut, Bwu = wu_ring.next()
                P.load(wgt[:], wg_v[:, :, fg * FG * 128:(fg + 1) * FG * 128], (), [Bwg], eng="pool")
                P.load(wut[:], wu_v[:, :, fg * FG * 128:(fg + 1) * FG * 128], (), [Bwu], eng="pool")
                if fg >= 2 and wd_pending:
                    i_ = wd_pending.pop(0)
                    P.load(wdA[:, i_ * WG:(i_ + 1) * WG, :], wd_v[:, i_ * WG:(i_ + 1) * WG, :], (), [Bwd[i_]], eng="pool")
                for tb in range(4):
                    for fc in range(FG):
                        f = fg * FG + fc
                        pg, Bpg = C.psA.next()
                        pu, Bpu = C.psA.next()
                        rd = [BxnT[tb * 4 + i] for i in range(4)]
                        for k in range(KC):
                            P.mm(pg[:], wgt[:, k, fc * 128:(fc + 1) * 128], xnT[:, k, tb * 512:(tb + 1) * 512],
                                 k == 0, k == KC - 1, rd + [Bwg], [Bpg])
                        for k in range(KC):
                            P.mm(pu[:], wut[:, k, fc * 128:(fc + 1) * 128], xnT[:, k, tb * 512:(tb + 1) * 512],
                                 k == 0, k == KC - 1, rd + [Bwu], [Bpu])
                        sg, Bsg = sg_ring.next()
                        P.act(sg[:], pg[:], AF.Silu, [Bpg], [Bsg])
                        P.tt(hT[:, f, tb * 512:(tb + 1) * 512], pu[:], sg[:], ALU.mult, [Bpu, Bsg], [BhT[f][tb]])
        with P.scope():
            wdB = P.sb("wdB", [128, NF - NA, D], BF16)
            for i in range(NA // WG, NF // WG):
                P.load(wdB[:, i * WG - NA:(i + 1) * WG - NA, :], wd_v[:, i * WG:(i + 1) * WG, :], (), [Bwd[i]], eng="pool")
            gbc = P.sb("gbc2", [128, D], F32)
            Bg = Buf()
            P.load(gbc[:], g_post.partition_broadcast(128), (), [Bg])
            xt_ring = P.sb_ring("xr", [128, D], F32, 2)
            st_ring = P.sb_ring("st2", [128, 8], F32, 3)
            tmp_ring = P.sb_ring("tmp", [128, 512], F32, 3)
            for t in range(NT):
                xt, Bxt = xt_ring.next()
                P.load(xt[:], src[t * 128:(t + 1) * 128, :], [Bsrc[t]], [Bxt])
                pos, Bpos = [], []
                for dh in range(2):
                    po, Bpo = C.psA.next()
                    for f in range(NF):
                        P.mm(po[:], hT[:, f, t * 128:(t + 1) * 128],
                             (wdA[:, f, dh * 512:(dh + 1) * 512] if f < NA else wdB[:, f - NA, dh * 512:(dh + 1) * 512]),
                             f == 0, f == NF - 1, [BhT[f][t // 4], Bwd[f // WG]], [Bpo])
                    pos.append(po)
                    Bpos.append(Bpo)
                post_norm_residual(P, C, pos, Bpos, gbc, Bg, xt, Bxt, 0.5, dst[t * 128:(t + 1) * 128, :], Bdst[t],
                                   st_ring, tmp_ring)


def setup_consts(P, C):
    C.ident = P.sb("ident", [128, 128], BF16)
    C.Bident = Buf()
    tmpi = P.sb("tmpi", [128, 128], F32)
    Bt = Buf()
    P.op("pool", lambda e: e.iota(tmpi[:], [[-1, 128]], base=0, channel_multiplier=1, allow_small_or_imprecise_dtypes=True), (), [Bt])
    P.op("dve", lambda e: e.tensor_single_scalar(out=C.ident[:], in_=tmpi[:], scalar=0.0, op=ALU.is_equal), [Bt], [C.Bident])
    C.psA = Ring([(P.ps(f"psA{i}", [128, 512], F32), Buf()) for i in range(4)])
    C.pst = Ring([(P.ps(f"pst{i}", [128, 1024], BF16), Buf()) for i in range(2)])
    C.psB = Ring([(P.ps(f"psB{i}", [128, 512], F32), Buf()) for i in range(2)])


STAGE = 3
DBG = False
PIPE = True
PIPE_DEPTH = 3
TRIM = True


def build_nc(stage=STAGE):
    nc = bass.Bass("TRN2", target_bir_lowering=False)

    def din(name, shape, dt=F32):
        return nc.dram_tensor(name, list(shape), dt, kind="ExternalInput").ap()

    x = din("x", [S, D])
    pos = din("positions", [1, S], I32)
    W = {}
    for n, shp in (("g_ffn1_pre", [1, D]), ("w_ffn1_gate", [D, DFF]), ("w_ffn1_up", [D, DFF]), ("w_ffn1_down", [DFF, D]),
                   ("g_ffn1_post", [1, D]), ("g_mix_pre", [1, D]), ("w_in", [D, IN_TOTAL]),
                   ("cmp_pe_k", [32, 64]), ("cmp_w1_k", [2048, 128]), ("cmp_w2_k", [128, 64]),
                   ("cmp_pe_v", [32, 64]), ("cmp_w1_v", [2048, 128]), ("cmp_w2_v", [128, 64]),
                   ("w_attn_branch", [D, D]), ("pool_w", [4, 128, 128]), ("pool_scale", [1, 512]),
                   ("w_pool_branch", [512, D]), ("w_out", [D, D]), ("g_mix_post", [1, D]),
                   ("g_ffn2_pre", [1, D]), ("w_ffn2_gate", [D, DFF]), ("w_ffn2_up", [D, DFF]), ("w_ffn2_down", [DFF, D]),
                   ("g_ffn2_post", [1, D])):
        W[n] = din(n, shp)
    consts = din("kconsts", [128, C_N])
    masks = din("kmasks", [128, 12, 512])
    emat = din("kemat", [128, 16, 128])
    out = nc.dram_tensor("out", [S, D], F32, kind="ExternalOutput").ap()
    x1 = nc.dram_tensor("x1_scratch", [S, D], F32).ap()
    x2 = nc.dram_tensor("x2_scratch", [S, D], F32).ap()
    onsa_dram = nc.dram_tensor("onsa_scratch", [128, KC, S], BF16).ap()

    P = Prog(nc)
    C = Ctx()
    C.dbg = None
    if DBG:
        C.dbg = nc.dram_tensor("dbg", [3, S, D], F32, kind="ExternalOutput").ap()
        C.Bdbg = Buf()
    setup_consts(P, C)
    C.onsa_d = onsa_dram
    Bx = [Buf() for _ in range(NT)]
    Bx1 = [Buf() for _ in range(NT)]
    Bx2 = [Buf() for _ in range(NT)]
    Bout = [Buf() for _ in range(NT)]
    if stage == 1:
        ffn_block(P, C, x, Bx, out, Bout, W["g_ffn1_pre"], W["w_ffn1_gate"], W["w_ffn1_up"], W["w_ffn1_down"], W["g_ffn1_post"])
    else:
        ffn_block(P, C, x, Bx, x1, Bx1, W["g_ffn1_pre"], W["w_ffn1_gate"], W["w_ffn1_up"], W["w_ffn1_down"], W["g_ffn1_post"])
        mixer_block(P, C, x1, Bx1, (out if stage == 2 else x2), (Bout if stage == 2 else Bx2), pos, W, consts, masks, emat)
        if stage >= 3:
            ffn_block(P, C, x2, Bx2, out, Bout, W["g_ffn2_pre"], W["w_ffn2_gate"], W["w_ffn2_up"], W["w_ffn2_down"], W["g_ffn2_post"])
    P.finish()
    return nc, P


OFF_Q, OFF_KC, OFF_VC, OFF_KS, OFF_VS, OFF_KW, OFF_VW, OFF_G, OFF_POOL, OFF_MG = 0, 1024, 1280, 1536, 1792, 2048, 2304, 2560, 2608, 3120
C_INV, C_RC, C_A, C_B, C_OV, C_FORCE, C_N = 0, 1, 17, 81, 145, 177, 178
PI = float(np.pi)
PI_SAFE = 3.1415


def host_consts():
    c = np.zeros((128, C_N), np.float32)
    p = np.arange(128)
    d = p % 64
    inv = 500000.0 ** (-(np.arange(8, dtype=np.float32)) * (2.0 / 16.0))
    c[:, C_INV] = np.where(d < 16, inv[d % 8], 0.0)
    c[:, C_RC:C_RC + 16] = 1.0 / (np.arange(16, dtype=np.float32) + 1.0)
    cur = (p // 64)[:, None]
    rel = (np.arange(64) - 32)[None, :]
    c[:, C_A:C_A + 64] = (rel < cur).astype(np.float32)
    c[:, C_B:C_B + 64] = np.where(rel == cur, 1e4, np.where(rel > cur, -1e30, 0.0))
    n = np.arange(128)[:, None]
    j = np.arange(32)[None, :]
    ov = np.minimum(16 * n + 32, 64 * j + 64) - np.maximum(16 * n, 64 * j)
    ov = np.clip(ov, 0, None).astype(np.float32) / 32.0
    ov[127, :] = 0.0
    c[:, C_OV:C_OV + 32] = ov
    c[:, C_FORCE] = 1e4
    masks = np.zeros((128, 12, 512), np.float32)
    s_ = np.arange(128)[:, None]
    tl = np.arange(512)[None, :]
    for jj in range(4):
        masks[:, jj, :] = np.where(128 * jj + s_ <= tl, 0.0, NEG)
        masks[:, 4 + jj, :] = np.where(128 * jj + s_ > tl, 0.0, NEG)
        masks[:, 8 + jj, :] = np.where(16 * s_ + 31 <= 512 * jj + tl, 0.0, NEG)
    em = np.zeros((128, 16, 128), np.float32)
    for jt in range(16):
        for s in range(128):
            em[2 * jt + s // 64, jt, s] = 1.0
    return c, masks, em


def mixer_block(P, C, src, Bsrc, dst, Bdst, pos, W, consts, masks, emat):
    win_v = W["w_in"].rearrange("(kc p) c -> p kc c", p=128)
    ident = C.ident
    Bid = C.Bident
    with P.scope():
        xnT = P.sb("mxnT", [128, KC, S], BF16)
        BxnT = [Buf() for _ in range(NT)]
        with P.scope():
            norm_transpose_phase(P, C, src, Bsrc, W["g_mix_pre"], xnT, BxnT, "m")
        onsa_d = C.onsa_d
        BonD = [[Buf() for _ in range(NT)] for _ in range(4)]
        cst = P.sb("cst", [128, C_N], F32)
        Bcst = Buf()
        P.load(cst[:], consts[:, :], (), [Bcst])

        def xr(tb):
            return [BxnT[tb * 4 + i] for i in range(4)]

        with P.scope():
            gs = P.sb("gs", [128, NT, 48], F32)
            Bgs = [Buf() for _ in range(NT)]
            mk = P.sb("mk", [128, 12, 512], BF16)
            Bmk = Buf()
            P.load(mk[:], masks[:, :, :], (), [Bmk], eng="pool")
            em = P.sb("em", [128, 16, 128], BF16)
            Bem = Buf()
            P.load(em[:], emat[:, :, :], (), [Bem], eng="pool")
            Ct = P.sb("Ct", [128, S], F32)
            St = P.sb("St", [128, S], F32)
            Btab = Buf()
            with P.scope():
                posi = P.sb("posi", [128, S], I32)
                Bp = Buf()
                P.load(posi[:], pos.partition_broadcast(128), (), [Bp])
                ang = P.sb("ang", [128, S], F32)
                Ba = Buf()
                P.cp(ang[:], posi[:], [Bp], [Ba])
                P.ts(ang[:], ang[:], cst[:, C_INV:C_INV + 1], None, ALU.mult, None, [Ba, Bcst], [Ba])
                kf = P.sb("kf", [128, S], F32)
                Bk = Buf()
                for tab, shift, bias in ((St, 0.0, 0.0), (Ct, 0.25, PI / 2)):
                    P.ts(posi[:], ang[:], 1.0 / (2 * PI), shift, ALU.mult, ALU.add, [Ba], [Bp])
                    P.cp(kf[:], posi[:], [Bp], [Bk])
                    P.stt(kf[:], kf[:], -2 * PI, ang[:], ALU.mult, ALU.add, [Bk, Ba], [Bk])
                    P.ts(kf[:], kf[:], PI_SAFE - bias, -PI_SAFE - bias, ALU.min, ALU.max, [Bk], [Bk])
                    if bias == 0.0:
                        P.act(tab[:], kf[:], AF.Sin, [Bk], [Btab])
                    else:
                        hp = P.sb("hp", [128, 1], F32)
                        Bhp = Buf()
                        P.memset(hp[:], bias, [Bhp])
                        P.act(tab[:], kf[:], AF.Sin, [Bk, Bhp], [Btab], bias=hp[:, 0:1])
            wgate = P.sb("wgate", [128, KC, 48], BF16)
            Bwgt = Buf()
            P.load(wgate[:], win_v[:, :, OFF_G:OFF_G + 48], (), [Bwgt], eng="pool")
            for t in range(NT):
                ps, Bps = C.psB.next()
                for k in range(KC):
                    P.mm(ps[:, 0:48], xnT[:, k, t * 128:(t + 1) * 128], wgate[:, k, :], k == 0, k == KC - 1, [BxnT[t], Bwgt], [Bps])
                P.act(gs[:, t, :], ps[:, 0:48], AF.Sigmoid, [Bps], [Bgs[t]])
            W1 = {}
            BW1 = Buf()
            cbias = {}
            Bcb = Buf()
            pe16 = P.sb("pe16", [16, 128], F32)
            pe16b = P.sb("pe16b", [16, 128], BF16)
            peb = P.sb("peb", [128, 16], BF16)
            Bpe = Buf()
            for kv in ("k", "v"):
                W1[kv] = P.sb("W1" + kv, [128, 16, 128], BF16)
                P.load(W1[kv][:], W[f"cmp_w1_{kv}"].rearrange("(m p) h -> p m h", p=128), (), [BW1], eng="pool")
                P.load(pe16[:], W[f"cmp_pe_{kv}"].rearrange("(m lp) d -> m (lp d)", lp=2), (), [Bpe])
                P.cp(pe16b[:], pe16[:], [Bpe], [Bpe])
                ptr, Bptr = C.pst.next()
                P.tr(ptr[:, 0:16], pe16b[:, :], ident[0:16, 0:16], [Bpe, Bid], [Bptr])
                P.cp(peb[:], ptr[:, 0:16], [Bptr], [Bpe])
                ps, Bps = C.psB.next()
                for m in range(16):
                    P.mm(ps[:, 0:1], W1[kv][:, m, :], peb[:, m:m + 1], m == 0, m == 15, [BW1, Bpe], [Bps])
                cbias[kv] = P.sb("cb" + kv, [128, 1], F32)
                P.cp(cbias[kv][:], ps[:, 0:1], [Bps], [Bcb])
            w2k = P.sb("w2k", [128, 128], BF16)
            w2kr = P.sb("w2kr", [128, 128], BF16)
            w2v = P.sb("w2v", [128, 64], BF16)
            Bw2 = Buf()
            P.load(w2k[:, 0:64], W["cmp_w2_k"][:, :], (), [Bw2], eng="pool")
            P.load(w2k[:, 64:128], W["cmp_w2_k"][:, :], (), [Bw2], eng="pool")
            P.load(w2v[:], W["cmp_w2_v"][:, :], (), [Bw2], eng="pool")
            P.memset(w2kr[:], 0.0, [Bw2], eng="dve")
            w2k3 = w2k[:].rearrange("p (r d) -> p r d", d=64)
            w2kr3 = w2kr[:].rearrange("p (r d) -> p r d", d=64)
            P.ts(w2kr3[:, :, 0:8], w2k3[:, :, 8:16], -1.0, None, ALU.mult, None, [Bw2], [Bw2])
            P.cp(w2kr3[:, :, 8:16], w2k3[:, :, 0:8], [Bw2], [Bw2])

            wq = P.sb("wq", [128, KC, 256], BF16)
            wqr = P.sb("wqr", [128, KC, 256], BF16)
            Bwq = Buf()
            Bwqr = Buf()
            P.memset(wqr[:], 0.0, [Bwqr], eng="dve")
            wdup = {}
            Bwdup = {}
            for nm in ("kc", "vc", "ks", "kw"):
                wdup[nm] = P.sb("wd_" + nm, [128, KC, 128], BF16)
                Bwdup[nm] = Buf()
            wrot = {}
            Bwrot = {}
            for nm in ("ks", "kw"):
                wrot[nm] = P.sb("wr_" + nm, [128, KC, 128], BF16)
                Bwrot[nm] = Buf()
                P.memset(wrot[nm][:], 0.0, [Bwrot[nm]], eng="dve")
            wv = P.sb("wv", [128, KC, 128], BF16)
            Bwv = Buf()
            qpad = P.sb("qpad", [128, 4, S], BF16)
            BqT = [[Buf() for _ in range(4)] for _ in range(4)]
            for hl_ in range(4):
                P.memset(qpad[:, hl_, :], 0.0, BqT[hl_], eng="dve")
            kT = {}
            BkT = {}
            for nm in ("ks", "kw"):
                kT[nm] = P.sb("kT_" + nm, [128, S], BF16)
                BkT[nm] = [Buf() for _ in range(4)]
            KK = {}
            BKK = {}
            for nm in ("kc", "vc"):
                KK[nm] = P.sb("KK_" + nm, [128, S], BF16)
                BKK[nm] = [Buf() for _ in range(4)]
                P.memset(KK[nm][:], 0.0, BKK[nm], eng="dve")
            Vs = P.sb("Vs", [128, NT, 65], BF16)
            Vw = P.sb("Vw", [128, NT, 65], BF16)
            BVs = [Buf() for _ in range(NT)]
            BVw = [Buf() for _ in range(NT)]
            P.memset(Vs[:], 1.0, BVs, eng="dve")
            P.memset(Vw[:], 1.0, BVw, eng="dve")
            HT = P.sb("HT", [128, 128], BF16)
            BHT = Buf()
            kcT = P.sb("kcT", [128, 128], BF16)
            BkcT = Buf()
            P.memset(kcT[:], 0.0, [BkcT], eng="dve")
            VC = P.sb("VC", [128, 97], BF16)
            BVC = Buf()
            P.memset(VC[:], 0.0, [BVC], eng="dve")
            P.memset(VC[:, 64:65], 1.0, [BVC], eng="dve")
            P.cp(VC[:, 65:97], cst[:, C_OV:C_OV + 32], [Bcst], [BVC])
            zer = P.sb("zer", [128, 128], BF16)
            Bzer = Buf()
            P.memset(zer[:], 0.0, [Bzer], eng="dve")
            oacc = P.sb("oacc", [128, NT, 256], F32)
            Boacc = [Buf() for _ in range(NT)]
            selbT = P.sb("selbT", [128, S], BF16)
            BselbT = [Buf() for _ in range(4)]
            P.memset(selbT[:], 0.0, BselbT, eng="dve")
            pT_ring = P.sb_ring("pT", [128, 512], BF16, 6)
            tmp_ring = P.sb_ring("rt", [128, 512], F32, 4)
            small = P.sb_ring("sm", [128, 8], F32, 4)
            imp_ring = P.sb_ring("imp", [128, 4, 32], F32, 2)
            imp2_ring = P.sb_ring("imp2", [128, 40], F32, 2)
            selb_ring = P.sb_ring("selb", [128, 32], BF16, 3)
            otmp_ring = P.sb_ring("otmp", [128, 4, 64], F32, 3)
            obf_ring = P.sb_ring("obf", [128, 256], BF16, 2)
            impall = P.sb("impall", [128, NT, 128], F32)
            Bimp = [Buf() for _ in range(NT)]
            ons_ring = P.sb_ring("ons", [128, 2, 128], BF16, 2)

            def make_rot(dst, Bd, srct, Bs):
                dv = dst[:].rearrange("p k (r d) -> p (k r) d", d=64)
                sv = srct[:].rearrange("p k (r d) -> p (k r) d", d=64)
                P.ts(dv[:, :, 0:8], sv[:, :, 8:16], -1.0, None, ALU.mult, None, [Bs], [Bd])
                P.cp(dv[:, :, 8:16], sv[:, :, 0:8], [Bs], [Bd])

            def rope_evac(ps1, B1, ps2, B2, c0, c1, out_ap, Bouts, cstep=None, split=None):
                t1, Bt1 = tmp_ring.next()
                t2, Bt2 = tmp_ring.next()
                n = (out_ap if split is None else split[0][0]).shape[-1]
                if cstep is None:
                    ca, sa = Ct[:, c0:c1], St[:, c0:c1]
                else:
                    ca, sa = Ct[:, c0:c1:cstep], St[:, c0:c1:cstep]
                P.tt(t1[:, 0:n], ps1, ca, ALU.mult, [B1, Btab], [Bt1])
                P.tt(t2[:, 0:n], ps2, sa, ALU.mult, [B2, Btab], [Bt2])
                if split is None:
                    P.tt(out_ap, t1[:, 0:n], t2[:, 0:n], ALU.add, [Bt1, Bt2], Bouts, eng="pool")
                else:
                    for (oap, Bo_), (r0, r1) in zip(split, ((0, 64), (64, 128))):
                        P.tt(oap, t1[r0:r1, 0:n], t2[r0:r1, 0:n], ALU.add, [Bt1, Bt2], [Bo_], eng="dve")

            for g in range(NKV):
                P.load(wq[:], win_v[:, :, OFF_Q + 256 * g:OFF_Q + 256 * g + 256], (), [Bwq], eng="pool")
                make_rot(wqr, Bwqr, wq, Bwq)
                for nm, off in (("kc", OFF_KC), ("vc", OFF_VC), ("ks", OFF_KS), ("kw", OFF_KW)):
                    for r in range(2):
                        P.load(wdup[nm][:, :, r * 64:(r + 1) * 64], win_v[:, :, off + 64 * g:off + 64 * g + 64], (), [Bwdup[nm]], eng="pool")
                for nm in ("ks", "kw"):
                    make_rot(wrot[nm], Bwrot[nm], wdup[nm], Bwdup[nm])
                P.load(wv[:, :, 0:64], win_v[:, :, OFF_VS + 64 * g:OFF_VS + 64 * g + 64], (), [Bwv], eng="pool")
                P.load(wv[:, :, 64:128], win_v[:, :, OFF_VW + 64 * g:OFF_VW + 64 * g + 64], (), [Bwv], eng="pool")
                for tb in range(4):
                    tok = slice(tb * 512, (tb + 1) * 512)
                    for cc in range(2):
                        ps1, B1 = C.psA.next()
                        ps2, B2 = C.psA.next()
                        for k in range(KC):
                            P.mm(ps1[:], wq[:, k, cc * 128:(cc + 1) * 128], xnT[:, k, tok], k == 0, k == KC - 1, xr(tb) + [Bwq], [B1])
                        for k in range(KC):
                            P.mm(ps2[:], wqr[:, k, cc * 128:(cc + 1) * 128], xnT[:, k, tok], k == 0, k == KC - 1, xr(tb) + [Bwqr], [B2])
                        rope_evac(ps1[:], B1, ps2[:], B2, tb * 512, (tb + 1) * 512, None, None,
                                  split=[(qpad[0:64, 2 * cc, tok], BqT[2 * cc][tb]), (qpad[64:128, 2 * cc + 1, tok], BqT[2 * cc + 1][tb])])
                    for nm in ("ks", "kw"):
                        ps1, B1 = C.psA.next()
                        ps2, B2 = C.psA.next()
                        for k in range(KC):
                            P.mm(ps1[:], wdup[nm][:, k, :], xnT[:, k, tok], k == 0, k == KC - 1, xr(tb) + [Bwdup[nm]], [B1])
                        for k in range(KC):
                            P.mm(ps2[:], wrot[nm][:, k, :], xnT[:, k, tok], k == 0, k == KC - 1, xr(tb) + [Bwrot[nm]], [B2])
                        rope_evac(ps1[:], B1, ps2[:], B2, tb * 512, (tb + 1) * 512, kT[nm][:, tok], [BkT[nm][tb]])
                    for nm in ("kc", "vc"):
                        ps1, B1 = C.psA.next()
                        for k in range(KC):
                            P.mm(ps1[:], wdup[nm][:, k, :], xnT[:, k, tok], k == 0, k == KC - 1, xr(tb) + [Bwdup[nm]], [B1])
                        P.act(KK[nm][0:64, tok], ps1[0:64, :], AF.Copy, [B1], [BKK[nm][tb]])
                        if tb == 0:
                            P.cp(KK[nm][64:128, 0:511], ps1[64:128, 1:512], [B1], [BKK[nm][0]])
                        else:
                            P.cp(KK[nm][64:128, tb * 512 - 1:tb * 512 + 511], ps1[64:128, :], [B1], [BKK[nm][tb], BKK[nm][tb - 1]])
                for t in range(NT):
                    ps, Bps = C.psB.next()
                    for k in range(KC):
                        P.mm(ps[:, 0:128], xnT[:, k, t * 128:(t + 1) * 128], wv[:, k, :], k == 0, k == KC - 1, [BxnT[t], Bwv], [Bps])
                    P.cp(Vs[:, t, 0:64], ps[:, 0:64], [Bps], [BVs[t]])
                    P.act(Vw[:, t, 0:64], ps[:, 64:128], AF.Copy, [Bps], [BVw[t]])
                for kv, nm in (("k", "kc"), ("v", "vc")):
                    psz, Bz = C.psA.next()
                    for m in range(16):
                        P.mm(psz[:, 0:127], W1[kv][:, m, :], KK[nm][:, 2 * m:2 * m + 2017:16], m == 0, m == 15, BKK[nm] + [BW1], [Bz])
                    P.act(HT[:, 0:127], psz[:, 0:127], AF.Gelu_apprx_tanh, [Bz, Bcb], [BHT], bias=cbias[kv][:, 0:1])
                    if kv == "k":
                        ps1, B1 = C.psA.next()
                        ps2, B2 = C.psA.next()
                        P.mm(ps1[:, 0:127], w2k[:, :], HT[:, 0:127], True, True, [Bw2, BHT], [B1])
                        P.mm(ps2[:, 0:127], w2kr[:, :], HT[:, 0:127], True, True, [Bw2, BHT], [B2])
                        rope_evac(ps1[:, 0:127], B1, ps2[:, 0:127], B2, 31, 2048, kcT[:, 0:127], [BkcT], cstep=16)
                    else:
                        ps, Bps = C.psB.next()
                        P.mm(ps[0:127, 0:64], HT[:, 0:127], w2v[:, :], True, True, [BHT, Bw2], [Bps])
                        P.cp(VC[0:127, 0:64], ps[0:127, 0:64], [Bps], [BVC])
                for tb in range(4):
                    tok = slice(tb * 512, (tb + 1) * 512)
                    PT = []
                    for hl in range(4):
                        cc, base = hl // 2, 64 * (hl % 2)
                        ps, Bps = C.psA.next()
                        P.mm(ps[:], kcT[:, :], qpad[:, hl, tok], True, False, [BkcT, BqT[hl][tb]], [Bps])
                        P.mm(ps[:], ident[:], mk[:, 8 + tb, :], False, True, [Bid, Bmk], [Bps])
                        pt, Bpt = pT_ring.next()
                        P.act(pt[:], ps[:], AF.Exp, [Bps], [Bpt], scale=0.125)
                        PT.append((pt, Bpt))
                    for qt in range(4):
                        t = 4 * tb + qt
                        pso, Bo = C.psB.next()
                        for hl in range(4):
                            P.mm(pso[:, hl * 97:(hl + 1) * 97], PT[hl][0][:, qt * 128:(qt + 1) * 128], VC[:, :], True, True,
                                 [PT[hl][1], BVC], [Bo], inc=(hl == 3))
                        pv = pso[:, 0:388].rearrange("p (h c) -> p h c", c=97)
                        sm, Bsm = small.next()
                        P.ts(sm[:, 0:4].unsqueeze(2), pv[:, :, 64:65], 1e-30, None, ALU.max, None, [Bo], [Bsm])
                        P.recip(sm[:, 0:4], sm[:, 0:4], [Bsm], [Bsm])
                        P.tt(sm[:, 4:8], sm[:, 0:4], gs[:, t, 4 * g:4 * g + 4], ALU.mult, [Bsm, Bgs[t]], [Bsm])
                        P.tt(oacc[:, t, :].rearrange("p (h d) -> p h d", d=64), pv[:, :, 0:64],
                             sm[:, 4:8].unsqueeze(2).broadcast_to([128, 4, 64]), ALU.mult, [Bo, Bsm], [Boacc[t]])
                        if C.dbg is not None:
                            P.load(C.dbg[0, t * 128:(t + 1) * 128, 256 * g:256 * g + 256], oacc[:, t, :], [Boacc[t]], [C.Bdbg])
                        P.tt(impall[:, t, :].rearrange("p (h j) -> p h j", j=32), pv[:, :, 65:97],
                             sm[:, 0:4].unsqueeze(2).broadcast_to([128, 4, 32]), ALU.mult, [Bo, Bsm], [Bimp[t]])

                def chain_A(t):
                    i2, Bi2 = imp2_ring.next()
                    P.op("dve", lambda e, o=i2[:, 0:32], i=impall[:, t, :].rearrange("p (h j) -> p j h", j=32): e.tensor_reduce(out=o, in_=i, axis=mybir.AxisListType.X, op=ALU.add),
                         [Bimp[t]], [Bi2])
                    a0 = C_A + 32 - 2 * t
                    b0 = C_B + 32 - 2 * t
                    P.tt(i2[:, 0:32], i2[:, 0:32], cst[:, a0:a0 + 32], ALU.mult, [Bi2, Bcst], [Bi2])
                    P.tt(i2[:, 0:32], i2[:, 0:32], cst[:, b0:b0 + 32], ALU.add, [Bi2, Bcst], [Bi2])
                    P.cp(i2[:, 0:1], cst[:, C_FORCE:C_FORCE + 1], [Bi2, Bcst], [Bi2])
                    P.op("dve", lambda e, o=i2[:, 32:40], i=i2[:, 0:32]: e.max(out=o, in_=i), [Bi2], [Bi2])
                    sb_, Bsb = selb_ring.next()
                    P.ts(sb_[:], i2[:, 0:32], i2[:, 39:40], NEG, ALU.is_lt, ALU.mult, [Bi2], [Bsb])
                    pendB[t] = (sb_, Bsb, t, t // 4)

                pendB = {}
                slots = [[("A", 0)]] + [[("A", u), ("B", u - 1)] for u in range(1, NT)] + [[("B", NT - 1)]]

                def run_slot(sl):
                    for kind, u in sl:
                        if kind == "A":
                            chain_A(u)
                        else:
                            deferred_B.append(pendB.pop(u))
                            flush_B()

                def flush_B():
                    if deferred_B:
                        sb_, Bsb, t, tb = deferred_B.pop(0)
                        ptr, Bptr = C.pst.next()
                        P.tr(ptr[0:32, 0:128], sb_[:, :], ident[:], [Bsb, Bid], [Bptr])
                        P.cp(selbT[0:32, t * 128:(t + 1) * 128], ptr[0:32, 0:128], [Bptr], [BselbT[tb]])

                deferred_B = []

                def flush_one():
                    if slots:
                        run_slot(slots.pop(0))

                deferred = slots
                items = []
                for branch in (2, 1):
                    for hl in range(4):
                        for tb in range(4):
                            j0 = 0 if branch == 1 else max(0, 4 * tb - 4)
                            for jt in range(j0, 4 * tb + 4):
                                items.append((hl, branch, tb, jt, jt == j0, jt == 4 * tb + 3))
                n_win = sum(1 for it_ in items if it_[1] == 2)
                state = {}

                def emit_scores(it):
                    hl, branch, tb, jt, first, last = it
                    cc, base = hl // 2, 64 * (hl % 2)
                    d = jt - 4 * tb
                    c0, c1 = 0, 512
                    if TRIM:
                        if d >= 0:
                            c0, c1 = 128 * d, 512
                        elif branch == 2:
                            c0, c1 = 0, 128 * (d + 4 + 1)
                    kt, Bkt = (kT["ks"], BkT["ks"]) if branch == 1 else (kT["kw"], BkT["kw"])
                    ps, Bps = C.psA.next()
                    q0 = tb * 512
                    P.mm(ps[:, c0:c1], kt[:, jt * 128:(jt + 1) * 128], qpad[:, hl, q0 + c0:q0 + c1], True, False,
                         [Bkt[jt // 4], BqT[hl][tb]], [Bps])
                    if branch == 1:
                        P.mm(ps[:, c0:c1], em[:, jt, :], selbT[:, q0 + c0:q0 + c1], False, d < 0, [Bem, BselbT[tb]], [Bps])
                        if d >= 0:
                            P.mm(ps[:, c0:c1], ident[:], mk[:, d, c0:c1], False, True, [Bid, Bmk], [Bps])
                    else:
                        mi = d if d >= 0 else (4 + d + 4)
                        P.mm(ps[:, c0:c1], ident[:], mk[:, mi, c0:c1], False, True, [Bid, Bmk], [Bps])
                    state[it] = (ps, Bps, c0, c1)

                def emit_rest(it):
                    hl, branch, tb, jt, first, last = it
                    h = 4 * g + hl
                    ps, Bps, c0, c1 = state.pop(it)
                    Vt, BVt = (Vs, BVs) if branch == 1 else (Vw, BVw)
                    if first:
                        pso, Bo = C.psB.next()
                        state[(hl, branch, tb)] = (pso, Bo)
                        P.mm(pso[:, 0:260], zer[:, :], mk[:, 0, 0:260], True, False, [Bzer, Bmk], [Bo], inc=False)
                    pso, Bo = state[(hl, branch, tb)]
                    pt, Bpt = pT_ring.next()
                    P.act(pt[:, c0:c1], ps[:, c0:c1], AF.Exp, [Bps], [Bpt], scale=0.125)
                    for qt in range(4):
                        T = 4 * tb + qt
                        lo = 0 if branch == 1 else max(0, T - 4)
                        if lo <= jt <= T:
                            assert c0 <= qt * 128 and (qt + 1) * 128 <= c1
                            P.mm(pso[:, qt * 65:(qt + 1) * 65], pt[:, qt * 128:(qt + 1) * 128], Vt[:, jt, :],
                                 False, (jt == 4 * tb + 3 and qt == 3), [Bpt, BVt[jt]], [Bo])
                    if last:
                        del state[(hl, branch, tb)]
                        pv = pso[:, 0:260].rearrange("p (q c) -> p q c", c=65)
                        sm, Bsm = small.next()
                        P.ts(sm[:, 0:4].unsqueeze(2), pv[:, :, 64:65], 1e-30, None, ALU.max, None, [Bo], [Bsm])
                        P.recip(sm[:, 0:4], sm[:, 0:4], [Bsm], [Bsm])
                        gcol = 16 * branch + h
                        P.tt(sm[:, 4:8].unsqueeze(2), sm[:, 0:4].unsqueeze(2), gs[:, 4 * tb:4 * tb + 4, gcol:gcol + 1], ALU.mult,
                             [Bsm] + [Bgs[4 * tb + i] for i in range(4)], [Bsm])
                        ot, Bot = otmp_ring.next()
                        P.tt(ot[:], pv[:, :, 0:64], sm[:, 4:8].unsqueeze(2).broadcast_to([128, 4, 64]), ALU.mult, [Bo, Bsm], [Bot])
                        ov = oacc[:, 4 * tb:4 * tb + 4, hl * 64:(hl + 1) * 64]
                        Bov = [Boacc[4 * tb + i] for i in range(4)]
                        P.tt(ov, ov, ot[:], ALU.add, Bov + [Bot], Bov, eng="pool")

                if PIPE:
                    for j in range(min(PIPE_DEPTH, len(items))):
                        emit_scores(items[j])
                    for i, it in enumerate(items):
                        if i + PIPE_DEPTH < len(items):
                            if i + PIPE_DEPTH >= n_win:
                                while deferred:
                                    flush_one()
                            emit_scores(items[i + PIPE_DEPTH])
                        emit_rest(it)
                        if i % 5 == 3:
                            flush_one()
                else:
                    while deferred:
                        flush_one()
                    for it in items:
                        emit_scores(it)
                        emit_rest(it)
                for t in range(NT):
                    ob, Bob = obf_ring.next()
                    P.act(ob[:], oacc[:, t, :], AF.Copy, [Boacc[t]], [Bob])
                    ptr, Bptr = C.pst.next()
                    for cc in range(2):
                        P.tr(ptr[:, cc * 128:(cc + 1) * 128], ob[:, cc * 128:(cc + 1) * 128], ident[:], [Bob, Bid], [Bptr], inc=(cc == 1))
                    ons, Bons = ons_ring.next()
                    P.cp(ons[:], ptr[:, 0:256].rearrange("p (c q) -> p c q", c=2), [Bptr], [Bons])
                    P.load(onsa_d[:, 2 * g:2 * g + 2, t * 128:(t + 1) * 128], ons[:], [Bons], [BonD[g][t]])

        o_poolT = P.sb("opoolT", [128, 4, S], BF16)
        BopT = [[Buf() for _ in range(4)] for _ in range(4)]
        with P.scope():
            wpin = P.sb("wpin", [128, KC, 512], BF16)
            Bwp = Buf()
            P.load(wpin[:], win_v[:, :, OFF_POOL:OFF_POOL + 512], (), [Bwp], eng="pool")
            pw = P.sb("pw", [128, 4, 128], BF16)
            Bpw = Buf()
            P.load(pw[:], W["pool_w"].rearrange("g c d -> c g d"), (), [Bpw], eng="pool")
            psc = P.sb("psc", [128, 4], F32)
            Bpsc = Buf()
            P.load_nc(psc[:], W["pool_scale"].rearrange("o (g p) -> p (o g)", p=128), (), [Bpsc])
            ub = [P.sb(f"ub{i}", [128, 16 + S], F32) for i in range(3)]
            Bub = [Buf() for _ in range(3)]
            for i in range(3):
                P.memset(ub[i][:, 0:16], 0.0, [Bub[i]], eng="dve")
            pl = P.sb("pl", [128, S], BF16)
            Bpl = Buf()
            ftmp = P.sb("ftmp", [128, 16], F32)
            Bft = Buf()
            for gi, w in enumerate((2, 4, 8, 16)):
                for tb in range(4):
                    ps, Bps = C.psA.next()
                    for k in range(KC):
                        P.mm(ps[:], wpin[:, k, gi * 128:(gi + 1) * 128], xnT[:, k, tb * 512:(tb + 1) * 512], k == 0, k == KC - 1, xr(tb) + [Bwp], [Bps])
                    P.act(ub[0][:, 16 + tb * 512:16 + (tb + 1) * 512], ps[:], AF.Copy, [Bps], [Bub[0]])
                cur = 0
                pp = [1, 2]
                step = 1
                for _ in range(gi + 1):
                    nxt = pp[0]
                    pp = pp[::-1]
                    P.tt(ub[nxt][:, 16:16 + S], ub[cur][:, 16:16 + S], ub[cur][:, 16 - step:16 - step + S], ALU.add, [Bub[cur]], [Bub[nxt]])
                    cur = nxt
                    step *= 2
                P.stt(pl[:], ub[cur][:, 16:16 + S], 1.0 / w, ub[0][:, 16:16 + S], ALU.mult, ALU.subtract, [Bub[cur], Bub[0]], [Bpl])
                P.tt(ftmp[:, 0:w - 1], ub[cur][:, 16:16 + w - 1], cst[:, C_RC:C_RC + w - 1], ALU.mult, [Bub[cur], Bcst], [Bft])
                P.tt(pl[:, 0:w - 1], ftmp[:, 0:w - 1], ub[0][:, 16:16 + w - 1], ALU.subtract, [Bft, Bub[0]], [Bpl])
                for tb in range(4):
                    ps, Bps = C.psA.next()
                    P.mm(ps[:], pw[:, gi, :], pl[:, tb * 512:(tb + 1) * 512], True, True, [Bpw, Bpl], [Bps])
                    P.act(o_poolT[:, gi, tb * 512:(tb + 1) * 512], ps[:], AF.Copy, [Bps, Bpsc], [BopT[gi][tb]], scale=psc[:, gi:gi + 1])

        with P.scope():
            o_nsaT = P.sb("onsaT", [128, KC, S], BF16)
            BonT = [Buf() for _ in range(4)]
            for tb_ in range(4):
                P.load(o_nsaT[:, :, tb_ * 512:(tb_ + 1) * 512], onsa_d[:, :, tb_ * 512:(tb_ + 1) * 512],
                       [BonD[g_][tb_ * 4 + i] for g_ in range(4) for i in range(4)], [BonT[tb_]])
            yT = P.sb("yT", [128, KC, S], BF16)
            ByT = [[Buf() for _ in range(4)] for _ in range(KC)]
            wab_v = W["w_attn_branch"].rearrange("(kc p) c -> p kc c", p=128)
            wpb_v = W["w_pool_branch"].rearrange("(kc p) c -> p kc c", p=128)
            with P.scope():
                wab_r = P.sb_ring("wab", [128, KC, 256], BF16, 2)
                wpb_r = P.sb_ring("wpb", [128, 4, 256], BF16, 2)
                wga_r = P.sb_ring("wga", [128, KC, 256], BF16, 2)
                wgp_r = P.sb_ring("wgp", [128, KC, 256], BF16, 2)
                sg_r = P.sb_ring("msg", [128, 512], F32, 4)
                t_r = P.sb_ring("mt", [128, 512], F32, 4)
                for dp in range(4):
                    wab, Bwab = wab_r.next()
                    wpb, Bwpb = wpb_r.next()
                    wga, Bwga = wga_r.next()
                    wgp, Bwgp = wgp_r.next()
                    cs = slice(dp * 256, (dp + 1) * 256)
                    P.load(wab[:], wab_v[:, :, cs], (), [Bwab], eng="pool")
                    P.load(wpb[:], wpb_v[:, :, cs], (), [Bwpb], eng="pool")
                    P.load(wga[:], win_v[:, :, OFF_MG + dp * 256:OFF_MG + (dp + 1) * 256], (), [Bwga], eng="pool")
                    P.load(wgp[:], win_v[:, :, OFF_MG + 1024 + dp * 256:OFF_MG + 1024 + (dp + 1) * 256], (), [Bwgp], eng="pool")
                    for dl in range(2):
                        dc = dp * 2 + dl
                        cl = slice(dl * 128, (dl + 1) * 128)
                        for tb in range(4):
                            tok = slice(tb * 512, (tb + 1) * 512)
                            pga, Bpga = C.psA.next()
                            pgp, Bpgp = C.psA.next()
                            pa, Bpa = C.psA.next()
                            pp_, Bpp = C.psA.next()
                            sa, Bsa = sg_r.next()
                            sp_, Bsp = sg_r.next()
                            for k in range(KC):
                                P.mm(pga[:], wga[:, k, cl], xnT[:, k, tok], k == 0, k == KC - 1, xr(tb) + [Bwga], [Bpga])
                            P.act(sa[:], pga[:], AF.Sigmoid, [Bpga], [Bsa])
                            for k in range(KC):
                                P.mm(pgp[:], wgp[:, k, cl], xnT[:, k, tok], k == 0, k == KC - 1, xr(tb) + [Bwgp], [Bpgp])
                            P.act(sp_[:], pgp[:], AF.Sigmoid, [Bpgp], [Bsp])
                            for k in range(KC):
                                P.mm(pa[:], wab[:, k, cl], o_nsaT[:, k, tok], k == 0, k == KC - 1,
                                     [Bwab, BonT[tb]], [Bpa])
                            for k in range(4):
                                P.mm(pp_[:], wpb[:, k, cl], o_poolT[:, k, tok], k == 0, k == 3, [Bwpb, BopT[k][tb]], [Bpp])
                            t1, Bt1 = t_r.next()
                            t2, Bt2 = t_r.next()
                            P.tt(t1[:], pa[:], sa[:], ALU.mult, [Bpa, Bsa], [Bt1])
                            P.tt(t2[:], pp_[:], sp_[:], ALU.mult, [Bpp, Bsp], [Bt2])
                            P.tt(yT[:, dc, tok], t1[:], t2[:], ALU.add, [Bt1, Bt2], [ByT[dc][tb]], eng="pool")
            with P.scope():
                wo = P.sb("wo", [128, KC, D], BF16)
                Bwo = Buf()
                P.load(wo[:], W["w_out"].rearrange("(kc p) c -> p kc c", p=128), (), [Bwo], eng="pool")
                gbc = P.sb("gbc3", [128, D], F32)
                Bg = Buf()
                P.load(gbc[:], W["g_mix_post"].partition_broadcast(128), (), [Bg])
                xt_ring = P.sb_ring("mxr", [128, D], F32, 2)
                st_ring = P.sb_ring("mst", [128, 8], F32, 3)
                tmp2_ring = P.sb_ring("mtmp", [128, 512], F32, 3)
                for t in range(NT):
                    xt, Bxt = xt_ring.next()
                    P.load(xt[:], src[t * 128:(t + 1) * 128, :], [Bsrc[t]], [Bxt])
                    pos_, Bpos = [], []
                    for dh in range(2):
                        po, Bpo = C.psA.next()
                        for k in range(KC):
                            P.mm(po[:], yT[:, k, t * 128:(t + 1) * 128], wo[:, k, dh * 512:(dh + 1) * 512], k == 0, k == KC - 1,
                                 [ByT[k][t // 4], Bwo], [Bpo])
                        pos_.append(po)
                        Bpos.append(Bpo)
                    post_norm_residual(P, C, pos_, Bpos, gbc, Bg, xt, Bxt, 1.0, dst[t * 128:(t + 1) * 128, :], Bdst[t], st_ring, tmp2_ring)


_NC_CACHE = {}


def kernel(**inputs):
    n = 8
    if "nc" not in _NC_CACHE:
        _NC_CACHE["nc"] = build_nc()[0]
    nc = _NC_CACHE["nc"]
    in_maps = []
    hc, hm, he = host_consts()
    for b in range(n):
        m = {"kconsts": hc, "kmasks": hm, "kemat": he}
        for k, v in inputs.items():
            v = np.asarray(v)
            if k == "x":
                m[k] = np.ascontiguousarray(v[b])
            elif k == "positions":
                m[k] = np.ascontiguousarray(v[b].reshape(1, S))
            else:
                a = v[0]
                if a.ndim == 1:
                    a = a.reshape(1, -1)
                m[k] = np.ascontiguousarray(a)
        in_maps.append(m)
    res = run_bass_kernel_spmd(nc, in_maps, core_ids=list(range(n)))
    return np.stack([np.asarray(r["out"]).reshape(S, D) for r in res.results], axis=0).astype(np.float32)
```

```python
import numpy as np
from contextlib import ExitStack, contextmanager
import concourse.bass as bass
import concourse.mybir as mybir
from concourse.bass_utils import run_bass_kernel_spmd

F32 = mybir.dt.float32
BF16 = mybir.dt.bfloat16
I32 = mybir.dt.int32
ALU = mybir.AluOpType
AF = mybir.ActivationFunctionType

ENGINES = ("pe", "act", "dve", "pool", "sp")
SEM_ROLL = 30000
N_SW_OUTSTANDING = 6

D = 1024
S = 2048
DFF = 2816
NT = S // 128
NF = DFF // 128
KC = D // 128
NH = 16
HD = 64
NKV = 4
IN_TOTAL = 5168
EPS = 1e-6
NEG = -30000.0


class Buf:
    __slots__ = ("name", "w", "r")

    def __init__(self, name=""):
        self.name = name
        self.w = None
        self.r = []


class Ring:
    def __init__(self, items):
        self.items = items
        self.i = 0

    def next(self):
        it = self.items[self.i]
        self.i = (self.i + 1) % len(self.items)
        return it


class Prog:
    def __init__(self, nc, n_dma_sems=24):
        self.nc = nc
        self.es = ExitStack()
        self.scopes = [self.es]
        self.streams = {e: [] for e in ENGINES}
        self.sem = {}
        self.cnt = {}
        self.nsem = 0
        for e in ENGINES:
            self._new_engine_sem(e)
        self.known = {e: {} for e in ENGINES}
        self.dma_sems = [self.es.enter_context(nc.semaphore(f"dq{i}")) for i in range(n_dma_sems)]
        self.dma_cnt = [0] * n_dma_sems
        self.n_sw = N_SW_OUTSTANDING
        self.dma_rr = self.n_sw
        self.dma_rr_sw = 0
        self.ninstr = 0
        self.uid = 0

    def _new_engine_sem(self, e):
        self.nsem += 1
        self.sem[e] = self.es.enter_context(self.nc.semaphore(f"s_{e}_{self.nsem}"))
        self.cnt[e] = 0

    @contextmanager
    def scope(self):
        es = ExitStack()
        self.scopes.append(es)
        try:
            yield
        finally:
            self.barrier()
            self.scopes.pop()
            es.close()

    def sb(self, name, shape, dt):
        self.uid += 1
        return self.scopes[-1].enter_context(self.nc.sbuf_tensor(f"{name}_{self.uid}", list(shape), dt))

    def ps(self, name, shape, dt):
        self.uid += 1
        return self.scopes[-1].enter_context(self.nc.psum_tensor(f"{name}_{self.uid}", list(shape), dt))

    def sb_ring(self, name, shape, dt, n):
        return Ring([(self.sb(f"{name}{i}", shape, dt), Buf(f"{name}{i}")) for i in range(n)])

    def _need(self, eng, tok, waits):
        if tok is None:
            return
        sem, val, teng = tok
        if teng == "pe" and eng == "pe":
            return
        sid = id(sem)
        if self.known[eng].get(sid, 0) >= val:
            return
        cur = waits.get(sid)
        if cur is None or cur[1] < val:
            waits[sid] = (sem, val)

    def _collect(self, eng, reads, writes):
        waits = {}
        for b in reads:
            self._need(eng, b.w, waits)
        for b in writes:
            self._need(eng, b.w, waits)
            for t in b.r:
                self._need(eng, t, waits)
        return waits

    def _emit_waits(self, eng, waits):
        st = self.streams[eng]
        for sid, (sem, val) in waits.items():
            st.append(lambda e, sem=sem, val=val: e.wait_ge(sem, val))
            self.known[eng][sid] = val

    def _commit(self, tok, reads, writes):
        for b in reads:
            b.r.append(tok)
            if len(b.r) > 48:
                best = {}
                for t in b.r:
                    k = id(t[0])
                    if k not in best or best[k][1] < t[1]:
                        best[k] = t
                b.r = list(best.values())
        for b in writes:
            b.w = tok
            b.r = []

    def op(self, eng, fn, reads=(), writes=(), inc=True):
        waits = self._collect(eng, reads, writes)
        self._emit_waits(eng, waits)
        self.ninstr += 1
        if inc:
            if self.cnt[eng] >= SEM_ROLL:
                self._new_engine_sem(eng)
            self.cnt[eng] += 1
            sem, val = self.sem[eng], self.cnt[eng]
            self.streams[eng].append(lambda e, fn=fn, sem=sem: fn(e).then_inc(sem, 1))
            tok = (sem, val, eng)
        else:
            self.streams[eng].append(lambda e, fn=fn: fn(e))
            tok = (self.sem[eng], self.cnt[eng] + 1, eng)
        self._commit(tok, reads, writes)
        return tok

    def dma(self, eng, fn, reads=(), writes=()):
        if eng == "pool":
            i = self.dma_rr_sw
            self.dma_rr_sw = (self.dma_rr_sw + 1) % self.n_sw
        else:
            i = self.dma_rr
            self.dma_rr += 1
            if self.dma_rr >= len(self.dma_sems):
                self.dma_rr = self.n_sw
        sem = self.dma_sems[i]
        waits = self._collect(eng, reads, writes)
        if self.dma_cnt[i] > 0:
            self._need(eng, (sem, self.dma_cnt[i], "dma"), waits)
        self._emit_waits(eng, waits)
        self.dma_cnt[i] += 16
        val = self.dma_cnt[i]
        self.streams[eng].append(lambda e, fn=fn, sem=sem: fn(e).then_inc(sem, 16))
        tok = (sem, val, "dma")
        self._commit(tok, reads, writes)
        self.ninstr += 1
        return tok

    def barrier(self):
        for eng in ENGINES:
            waits = {}
            for e2 in ENGINES:
                if e2 != eng and self.cnt[e2] > 0:
                    self._need(eng, (self.sem[e2], self.cnt[e2], e2), waits)
            if self.cnt[eng] > 0 and eng != "pe":
                self._need(eng, (self.sem[eng], self.cnt[eng], eng + "_self"), waits)
            for i, sem in enumerate(self.dma_sems):
                if self.dma_cnt[i] > 0:
                    self._need(eng, (sem, self.dma_cnt[i], "dma"), waits)
            self._emit_waits(eng, waits)

    def finish(self):
        nc = self.nc
        streams = self.streams
        self.barrier()
        with nc.Block() as block:
            @block.tensor
            def _(e):
                for f in streams["pe"]:
                    f(e)

            @block.scalar
            def _(e):
                for f in streams["act"]:
                    f(e)

            @block.vector
            def _(e):
                for f in streams["dve"]:
                    f(e)

            @block.gpsimd
            def _(e):
                for f in streams["pool"]:
                    f(e)

            @block.sync
            def _(e):
                for f in streams["sp"]:
                    f(e)
        self.es.close()

    def mm(self, out, lhsT, rhs, start, stop, reads, writes, inc=None):
        if inc is None:
            inc = stop
        return self.op("pe", lambda e: e.matmul(out, lhsT, rhs, start=start, stop=stop), reads, writes, inc=inc)

    def tr(self, out, in_, ident, reads, writes, inc=True):
        return self.op("pe", lambda e: e.transpose(out, in_, ident), reads, writes, inc=inc)

    def act(self, out, in_, func, reads, writes, **kw):
        return self.op("act", lambda e: e.activation(out=out, in_=in_, func=func, **kw), reads, writes)

    def tt(self, out, in0, in1, op, reads, writes, eng="dve"):
        return self.op(eng, lambda e: e.tensor_tensor(out=out, in0=in0, in1=in1, op=op), reads, writes)

    def ts(self, out, in0, s1, s2, op0, op1, reads, writes, eng="dve"):
        if op1 is None:
            return self.op(eng, lambda e: e.tensor_scalar(out=out, in0=in0, scalar1=s1, scalar2=None, op0=op0), reads, writes)
        return self.op(eng, lambda e: e.tensor_scalar(out=out, in0=in0, scalar1=s1, scalar2=s2, op0=op0, op1=op1), reads, writes)

    def stt(self, out, in0, scalar, in1, op0, op1, reads, writes):
        return self.op("dve", lambda e: e.scalar_tensor_tensor(out=out, in0=in0, scalar=scalar, in1=in1, op0=op0, op1=op1), reads, writes)

    def cp(self, out, in_, reads, writes, eng="dve"):
        return self.op(eng, lambda e: e.tensor_copy(out=out, in_=in_), reads, writes)

    def recip(self, out, in_, reads, writes):
        return self.op("dve", lambda e: e.reciprocal(out=out, in_=in_), reads, writes)

    def memset(self, out, val, writes, eng="pool"):
        return self.op(eng, lambda e: e.memset(out, val), (), writes)

    def load(self, out, in_, reads, writes, eng="sp"):
        return self.dma(eng, lambda e: e.dma_start(out=out, in_=in_), reads, writes)

    def load_nc(self, out, in_, reads, writes, eng="sp"):
        return self.dma(eng, lambda e: e.dma_start(out=out, in_=in_, allow_slow_non_contiguous=True), reads, writes)


class Ctx:
    pass


def norm_transpose_phase(P, C, src, Bsrc, g_dram, xnT, BxnT, tag):
    gbc = P.sb("gbc", [128, D], F32)
    Bg = Buf()
    P.load(gbc[:], g_dram.partition_broadcast(128), (), [Bg])
    xt_ring = P.sb_ring("xt", [128, D], F32, 2)
    xn_ring = P.sb_ring("xn", [128, D], BF16, 2)
    st_ring = P.sb_ring("st", [128, 4], F32, 3)
    for t in range(NT):
        xt, Bxt = xt_ring.next()
        xn, Bxn = xn_ring.next()
        st, Bst = st_ring.next()
        P.load(xt[:], src[t * 128:(t + 1) * 128, :], [Bsrc[t]], [Bxt])
        P.act(xn[:], xt[:], AF.Square, [Bxt], [Bxn, Bst], accum_out=st[:, 0:1])
        P.act(st[:, 1:2], st[:, 0:1], AF.Sqrt, [Bst], [Bst], scale=1.0 / D, bias=EPS)
        P.recip(st[:, 2:3], st[:, 1:2], [Bst], [Bst])
        P.stt(xn[:], xt[:], st[:, 2:3], gbc[:], ALU.mult, ALU.mult, [Bxt, Bst, Bg], [Bxn])
        pt, Bpt = C.pst.next()
        for k in range(KC):
            P.tr(pt[:, k * 128:(k + 1) * 128], xn[:, k * 128:(k + 1) * 128], C.ident[:], [Bxn, C.Bident], [Bpt], inc=(k == KC - 1))
        P.cp(xnT[:, :, t * 128:(t + 1) * 128], pt[:].rearrange("p (k c) -> p k c", k=KC), [Bpt], [BxnT[t]])


def post_norm_residual(P, C, po_list, Bpo_list, gbc, Bg, res_t, Bres, half_scale, out_dram, Bout, stt_ring, tmp_ring):
    st, Bst = stt_ring.next()
    junk, Bj = tmp_ring.next()
    P.act(junk[:], po_list[0][:], AF.Square, [Bpo_list[0]], [Bj, Bst], accum_out=st[:, 0:1])
    junk2, Bj2 = tmp_ring.next()
    P.act(junk2[:], po_list[1][:], AF.Square, [Bpo_list[1]], [Bj2, Bst], accum_out=st[:, 1:2])
    P.tt(st[:, 2:3], st[:, 0:1], st[:, 1:2], ALU.add, [Bst], [Bst])
    sc = 1.0 / (half_scale * half_scale)
    P.act(st[:, 3:4], st[:, 2:3], AF.Sqrt, [Bst], [Bst], scale=sc / D, bias=EPS * sc)
    P.recip(st[:, 4:5], st[:, 3:4], [Bst], [Bst])
    for hh in range(2):
        tmp, Bt = tmp_ring.next()
        P.stt(tmp[:], po_list[hh][:], st[:, 4:5], gbc[:, hh * 512:(hh + 1) * 512], ALU.mult, ALU.mult,
              [Bpo_list[hh], Bst, Bg], [Bt])
        P.tt(res_t[:, hh * 512:(hh + 1) * 512], res_t[:, hh * 512:(hh + 1) * 512], tmp[:], ALU.add, [Bres, Bt], [Bres])
    P.load(out_dram, res_t[:], [Bres], [Bout])


def ffn_block(P, C, src, Bsrc, dst, Bdst, g_pre, wg, wu, wd, g_post):
    FG = 2
    with P.scope():
        hT = P.sb("hT", [128, NF, S], BF16)
        BhT = [[Buf() for _ in range(4)] for _ in range(NF)]
        NA = 10
        WG = 2
        wd_v = wd.rearrange("(fc p) d -> p fc d", p=128)
        wdA = P.sb("wdA", [128, NA, D], BF16)
        Bwd = [Buf() for _ in range(NF // WG)]
        with P.scope():
            xnT = P.sb("xnT", [128, KC, S], BF16)
            BxnT = [Buf() for _ in range(NT)]
            norm_transpose_phase(P, C, src, Bsrc, g_pre, xnT, BxnT, "f")
            wg_ring = P.sb_ring("wg", [128, KC, FG * 128], BF16, 2)
            wu_ring = P.sb_ring("wu", [128, KC, FG * 128], BF16, 2)
            sg_ring = P.sb_ring("sg", [128, 512], F32, 2)
            wd_pending = list(range(NA // WG))
            wg_v = wg.rearrange("(kc p) f -> p kc f", p=128)
            wu_v = wu.rearrange("(kc p) f -> p kc f", p=128)
            for fg in range(NF // FG):
                wgt, Bwg = wg_ring.next()
                wut, Bwu = wu_ring.next()
                P.load(wgt[:], wg_v[:, :, fg * FG * 128:(fg + 1) * FG * 128], (), [Bwg], eng="pool")
                P.load(wut[:], wu_v[:, :, fg * FG * 128:(fg + 1) * FG * 128], (), [Bwu], eng="pool")
                if fg >= 2 and wd_pending:
                    i_ = wd_pending.pop(0)
                    P.load(wdA[:, i_ * WG:(i_ + 1) * WG, :], wd_v[:, i_ * WG:(i_ + 1) * WG, :], (), [Bwd[i_]], eng="pool")
                for tb in range(4):
                    for fc in range(FG):
                        f = fg * FG + fc
                        pg, Bpg = C.psA.next()
                        pu, Bpu = C.psA.next()
                        rd = [BxnT[tb * 4 + i] for i in range(4)]
                        for k in range(KC):
                            P.mm(pg[:], wgt[:, k, fc * 128:(fc + 1) * 128], xnT[:, k, tb * 512:(tb + 1) * 512],
                                 k == 0, k == KC - 1, rd + [Bwg], [Bpg])
                        for k in range(KC):
                            P.mm(pu[:], wut[:, k, fc * 128:(fc + 1) * 128], xnT[:, k, tb * 512:(tb + 1) * 512],
                                 k == 0, k == KC - 1, rd + [Bwu], [Bpu])
                        sg, Bsg = sg_ring.next()
                        P.act(sg[:], pg[:], AF.Silu, [Bpg], [Bsg])
                        P.tt(hT[:, f, tb * 512:(tb + 1) * 512], pu[:], sg[:], ALU.mult, [Bpu, Bsg], [BhT[f][tb]])
        with P.scope():
            wdB = P.sb("wdB", [128, NF - NA, D], BF16)
            for i in range(NA // WG, NF // WG):
                P.load(wdB[:, i * WG - NA:(i + 1) * WG - NA, :], wd_v[:, i * WG:(i + 1) * WG, :], (), [Bwd[i]], eng="pool")
            gbc = P.sb("gbc2", [128, D], F32)
            Bg = Buf()
            P.load(gbc[:], g_post.partition_broadcast(128), (), [Bg])
            xt_ring = P.sb_ring("xr", [128, D], F32, 2)
            st_ring = P.sb_ring("st2", [128, 8], F32, 3)
            tmp_ring = P.sb_ring("tmp", [128, 512], F32, 3)
            for t in range(NT):
                xt, Bxt = xt_ring.next()
                P.load(xt[:], src[t * 128:(t + 1) * 128, :], [Bsrc[t]], [Bxt])
                pos, Bpos = [], []
                for dh in range(2):
                    po, Bpo = C.psA.next()
                    for f in range(NF):
                        P.mm(po[:], hT[:, f, t * 128:(t + 1) * 128],
                             (wdA[:, f, dh * 512:(dh + 1) * 512] if f < NA else wdB[:, f - NA, dh * 512:(dh + 1) * 512]),
                             f == 0, f == NF - 1, [BhT[f][t // 4], Bwd[f // WG]], [Bpo])
                    pos.append(po)
                    Bpos.append(Bpo)
                post_norm_residual(P, C, pos, Bpos, gbc, Bg, xt, Bxt, 0.5, dst[t * 128:(t + 1) * 128, :], Bdst[t],
                                   st_ring, tmp_ring)


def setup_consts(P, C):
    C.ident = P.sb("ident", [128, 128], BF16)
    C.Bident = Buf()
    tmpi = P.sb("tmpi", [128, 128], F32)
    Bt = Buf()
    P.op("pool", lambda e: e.iota(tmpi[:], [[-1, 128]], base=0, channel_multiplier=1, allow_small_or_imprecise_dtypes=True), (), [Bt])
    P.op("dve", lambda e: e.tensor_single_scalar(out=C.ident[:], in_=tmpi[:], scalar=0.0, op=ALU.is_equal), [Bt], [C.Bident])
    C.psA = Ring([(P.ps(f"psA{i}", [128, 512], F32), Buf()) for i in range(4)])
    C.pst = Ring([(P.ps(f"pst{i}", [128, 1024], BF16), Buf()) for i in range(2)])
    C.psB = Ring([(P.ps(f"psB{i}", [128, 512], F32), Buf()) for i in range(2)])


STAGE = 3
DBG = False
PIPE = True
PIPE_DEPTH = 3
TRIM = True


def build_nc(stage=STAGE):
    nc = bass.Bass("TRN2", target_bir_lowering=False)

    def din(name, shape, dt=F32):
        return nc.dram_tensor(name, list(shape), dt, kind="ExternalInput").ap()

    x = din("x", [S, D])
    pos = din("positions", [1, S], I32)
    W = {}
    for n, shp in (("g_ffn1_pre", [1, D]), ("w_ffn1_gate", [D, DFF]), ("w_ffn1_up", [D, DFF]), ("w_ffn1_down", [DFF, D]),
                   ("g_ffn1_post", [1, D]), ("g_mix_pre", [1, D]), ("w_in", [D, IN_TOTAL]),
                   ("cmp_pe_k", [32, 64]), ("cmp_w1_k", [2048, 128]), ("cmp_w2_k", [128, 64]),
                   ("cmp_pe_v", [32, 64]), ("cmp_w1_v", [2048, 128]), ("cmp_w2_v", [128, 64]),
                   ("w_attn_branch", [D, D]), ("pool_w", [4, 128, 128]), ("pool_scale", [1, 512]),
                   ("w_pool_branch", [512, D]), ("w_out", [D, D]), ("g_mix_post", [1, D]),
                   ("g_ffn2_pre", [1, D]), ("w_ffn2_gate", [D, DFF]), ("w_ffn2_up", [D, DFF]), ("w_ffn2_down", [DFF, D]),
                   ("g_ffn2_post", [1, D])):
        W[n] = din(n, shp)
    consts = din("kconsts", [128, C_N])
    masks = din("kmasks", [128, 12, 512])
    emat = din("kemat", [128, 16, 128])
    out = nc.dram_tensor("out", [S, D], F32, kind="ExternalOutput").ap()
    x1 = nc.dram_tensor("x1_scratch", [S, D], F32).ap()
    x2 = nc.dram_tensor("x2_scratch", [S, D], F32).ap()
    onsa_dram = nc.dram_tensor("onsa_scratch", [128, KC, S], BF16).ap()

    P = Prog(nc)
    C = Ctx()
    C.dbg = None
    if DBG:
        C.dbg = nc.dram_tensor("dbg", [3, S, D], F32, kind="ExternalOutput").ap()
        C.Bdbg = Buf()
    setup_consts(P, C)
    C.onsa_d = onsa_dram
    Bx = [Buf() for _ in range(NT)]
    Bx1 = [Buf() for _ in range(NT)]
    Bx2 = [Buf() for _ in range(NT)]
    Bout = [Buf() for _ in range(NT)]
    if stage == 1:
        ffn_block(P, C, x, Bx, out, Bout, W["g_ffn1_pre"], W["w_ffn1_gate"], W["w_ffn1_up"], W["w_ffn1_down"], W["g_ffn1_post"])
    else:
        ffn_block(P, C, x, Bx, x1, Bx1, W["g_ffn1_pre"], W["w_ffn1_gate"], W["w_ffn1_up"], W["w_ffn1_down"], W["g_ffn1_post"])
        mixer_block(P, C, x1, Bx1, (out if stage == 2 else x2), (Bout if stage == 2 else Bx2), pos, W, consts, masks, emat)
        if stage >= 3:
            ffn_block(P, C, x2, Bx2, out, Bout, W["g_ffn2_pre"], W["w_ffn2_gate"], W["w_ffn2_up"], W["w_ffn2_down"], W["g_ffn2_post"])
    P.finish()
    return nc, P


OFF_Q, OFF_KC, OFF_VC, OFF_KS, OFF_VS, OFF_KW, OFF_VW, OFF_G, OFF_POOL, OFF_MG = 0, 1024, 1280, 1536, 1792, 2048, 2304, 2560, 2608, 3120
C_INV, C_RC, C_A, C_B, C_OV, C_FORCE, C_N = 0, 1, 17, 81, 145, 177, 178
PI = float(np.pi)
PI_SAFE = 3.1415


def host_consts():
    c = np.zeros((128, C_N), np.float32)
    p = np.arange(128)
    d = p % 64
    inv = 500000.0 ** (-(np.arange(8, dtype=np.float32)) * (2.0 / 16.0))
    c[:, C_INV] = np.where(d < 16, inv[d % 8], 0.0)
    c[:, C_RC:C_RC + 16] = 1.0 / (np.arange(16, dtype=np.float32) + 1.0)
    cur = (p // 64)[:, None]
    rel = (np.arange(64) - 32)[None, :]
    c[:, C_A:C_A + 64] = (rel < cur).astype(np.float32)
    c[:, C_B:C_B + 64] = np.where(rel == cur, 1e4, np.where(rel > cur, -1e30, 0.0))
    n = np.arange(128)[:, None]
    j = np.arange(32)[None, :]
    ov = np.minimum(16 * n + 32, 64 * j + 64) - np.maximum(16 * n, 64 * j)
    ov = np.clip(ov, 0, None).astype(np.float32) / 32.0
    ov[127, :] = 0.0
    c[:, C_OV:C_OV + 32] = ov
    c[:, C_FORCE] = 1e4
    masks = np.zeros((128, 12, 512), np.float32)
    s_ = np.arange(128)[:, None]
    tl = np.arange(512)[None, :]
    for jj in range(4):
        masks[:, jj, :] = np.where(128 * jj + s_ <= tl, 0.0, NEG)
        masks[:, 4 + jj, :] = np.where(128 * jj + s_ > tl, 0.0, NEG)
        masks[:, 8 + jj, :] = np.where(16 * s_ + 31 <= 512 * jj + tl, 0.0, NEG)
    em = np.zeros((128, 16, 128), np.float32)
    for jt in range(16):
        for s in range(128):
            em[2 * jt + s // 64, jt, s] = 1.0
    return c, masks, em


def mixer_block(P, C, src, Bsrc, dst, Bdst, pos, W, consts, masks, emat):
    win_v = W["w_in"].rearrange("(kc p) c -> p kc c", p=128)
    ident = C.ident
    Bid = C.Bident
    with P.scope():
        xnT = P.sb("mxnT", [128, KC, S], BF16)
        BxnT = [Buf() for _ in range(NT)]
        with P.scope():
            norm_transpose_phase(P, C, src, Bsrc, W["g_mix_pre"], xnT, BxnT, "m")
        onsa_d = C.onsa_d
        BonD = [[Buf() for _ in range(NT)] for _ in range(4)]
        cst = P.sb("cst", [128, C_N], F32)
        Bcst = Buf()
        P.load(cst[:], consts[:, :], (), [Bcst])

        def xr(tb):
            return [BxnT[tb * 4 + i] for i in range(4)]

        with P.scope():
            gs = P.sb("gs", [128, NT, 48], F32)
            Bgs = [Buf() for _ in range(NT)]
            mk = P.sb("mk", [128, 12, 512], BF16)
            Bmk = Buf()
            P.load(mk[:], masks[:, :, :], (), [Bmk], eng="pool")
            em = P.sb("em", [128, 16, 128], BF16)
            Bem = Buf()
            P.load(em[:], emat[:, :, :], (), [Bem], eng="pool")
            Ct = P.sb("Ct", [128, S], F32)
            St = P.sb("St", [128, S], F32)
            Btab = Buf()
            with P.scope():
                posi = P.sb("posi", [128, S], I32)
                Bp = Buf()
                P.load(posi[:], pos.partition_broadcast(128), (), [Bp])
                ang = P.sb("ang", [128, S], F32)
                Ba = Buf()
                P.cp(ang[:], posi[:], [Bp], [Ba])
                P.ts(ang[:], ang[:], cst[:, C_INV:C_INV + 1], None, ALU.mult, None, [Ba, Bcst], [Ba])
                kf = P.sb("kf", [128, S], F32)
                Bk = Buf()
                for tab, shift, bias in ((St, 0.0, 0.0), (Ct, 0.25, PI / 2)):
                    P.ts(posi[:], ang[:], 1.0 / (2 * PI), shift, ALU.mult, ALU.add, [Ba], [Bp])
                    P.cp(kf[:], posi[:], [Bp], [Bk])
                    P.stt(kf[:], kf[:], -2 * PI, ang[:], ALU.mult, ALU.add, [Bk, Ba], [Bk])
                    P.ts(kf[:], kf[:], PI_SAFE - bias, -PI_SAFE - bias, ALU.min, ALU.max, [Bk], [Bk])
                    if bias == 0.0:
                        P.act(tab[:], kf[:], AF.Sin, [Bk], [Btab])
                    else:
                        hp = P.sb("hp", [128, 1], F32)
                        Bhp = Buf()
                        P.memset(hp[:], bias, [Bhp])
                        P.act(tab[:], kf[:], AF.Sin, [Bk, Bhp], [Btab], bias=hp[:, 0:1])
            wgate = P.sb("wgate", [128, KC, 48], BF16)
            Bwgt = Buf()
            P.load(wgate[:], win_v[:, :, OFF_G:OFF_G + 48], (), [Bwgt], eng="pool")
            for t in range(NT):
                ps, Bps = C.psB.next()
                for k in range(KC):
                    P.mm(ps[:, 0:48], xnT[:, k, t * 128:(t + 1) * 128], wgate[:, k, :], k == 0, k == KC - 1, [BxnT[t], Bwgt], [Bps])
                P.act(gs[:, t, :], ps[:, 0:48], AF.Sigmoid, [Bps], [Bgs[t]])
            W1 = {}
            BW1 = Buf()
            cbias = {}
            Bcb = Buf()
            pe16 = P.sb("pe16", [16, 128], F32)
            pe16b = P.sb("pe16b", [16, 128], BF16)
            peb = P.sb("peb", [128, 16], BF16)
            Bpe = Buf()
            for kv in ("k", "v"):
                W1[kv] = P.sb("W1" + kv, [128, 16, 128], BF16)
                P.load(W1[kv][:], W[f"cmp_w1_{kv}"].rearrange("(m p) h -> p m h", p=128), (), [BW1], eng="pool")
                P.load(pe16[:], W[f"cmp_pe_{kv}"].rearrange("(m lp) d -> m (lp d)", lp=2), (), [Bpe])
                P.cp(pe16b[:], pe16[:], [Bpe], [Bpe])
                ptr, Bptr = C.pst.next()
                P.tr(ptr[:, 0:16], pe16b[:, :], ident[0:16, 0:16], [Bpe, Bid], [Bptr])
                P.cp(peb[:], ptr[:, 0:16], [Bptr], [Bpe])
                ps, Bps = C.psB.next()
                for m in range(16):
                    P.mm(ps[:, 0:1], W1[kv][:, m, :], peb[:, m:m + 1], m == 0, m == 15, [BW1, Bpe], [Bps])
                cbias[kv] = P.sb("cb" + kv, [128, 1], F32)
                P.cp(cbias[kv][:], ps[:, 0:1], [Bps], [Bcb])
            w2k = P.sb("w2k", [128, 128], BF16)
            w2kr = P.sb("w2kr", [128, 128], BF16)
            w2v = P.sb("w2v", [128, 64], BF16)
            Bw2 = Buf()
            P.load(w2k[:, 0:64], W["cmp_w2_k"][:, :], (), [Bw2], eng="pool")
            P.load(w2k[:, 64:128], W["cmp_w2_k"][:, :], (), [Bw2], eng="pool")
            P.load(w2v[:], W["cmp_w2_v"][:, :], (), [Bw2], eng="pool")
            P.memset(w2kr[:], 0.0, [Bw2], eng="dve")
            w2k3 = w2k[:].rearrange("p (r d) -> p r d", d=64)
            w2kr3 = w2kr[:].rearrange("p (r d) -> p r d", d=64)
            P.ts(w2kr3[:, :, 0:8], w2k3[:, :, 8:16], -1.0, None, ALU.mult, None, [Bw2], [Bw2])
            P.cp(w2kr3[:, :, 8:16], w2k3[:, :, 0:8], [Bw2], [Bw2])

            wq = P.sb("wq", [128, KC, 256], BF16)
            wqr = P.sb("wqr", [128, KC, 256], BF16)
            Bwq = Buf()
            Bwqr = Buf()
            P.memset(wqr[:], 0.0, [Bwqr], eng="dve")
            wdup = {}
            Bwdup = {}
            for nm in ("kc", "vc", "ks", "kw"):
                wdup[nm] = P.sb("wd_" + nm, [128, KC, 128], BF16)
                Bwdup[nm] = Buf()
            wrot = {}
            Bwrot = {}
            for nm in ("ks", "kw"):
                wrot[nm] = P.sb("wr_" + nm, [128, KC, 128], BF16)
                Bwrot[nm] = Buf()
                P.memset(wrot[nm][:], 0.0, [Bwrot[nm]], eng="dve")
            wv = P.sb("wv", [128, KC, 128], BF16)
            Bwv = Buf()
            qpad = P.sb("qpad", [128, 4, S], BF16)
            BqT = [[Buf() for _ in range(4)] for _ in range(4)]
            for hl_ in range(4):
                P.memset(qpad[:, hl_, :], 0.0, BqT[hl_], eng="dve")
            kT = {}
            BkT = {}
            for nm in ("ks", "kw"):
                kT[nm] = P.sb("kT_" + nm, [128, S], BF16)
                BkT[nm] = [Buf() for _ in range(4)]
            KK = {}
            BKK = {}
            for nm in ("kc", "vc"):
                KK[nm] = P.sb("KK_" + nm, [128, S], BF16)
                BKK[nm] = [Buf() for _ in range(4)]
                P.memset(KK[nm][:], 0.0, BKK[nm], eng="dve")
            Vs = P.sb("Vs", [128, NT, 65], BF16)
            Vw = P.sb("Vw", [128, NT, 65], BF16)
            BVs = [Buf() for _ in range(NT)]
            BVw = [Buf() for _ in range(NT)]
            P.memset(Vs[:], 1.0, BVs, eng="dve")
            P.memset(Vw[:], 1.0, BVw, eng="dve")
            HT = P.sb("HT", [128, 128], BF16)
            BHT = Buf()
            kcT = P.sb("kcT", [128, 128], BF16)
            BkcT = Buf()
            P.memset(kcT[:], 0.0, [BkcT], eng="dve")
            VC = P.sb("VC", [128, 97], BF16)
            BVC = Buf()
            P.memset(VC[:], 0.0, [BVC], eng="dve")
            P.memset(VC[:, 64:65], 1.0, [BVC], eng="dve")
            P.cp(VC[:, 65:97], cst[:, C_OV:C_OV + 32], [Bcst], [BVC])
            zer = P.sb("zer", [128, 128], BF16)
            Bzer = Buf()
            P.memset(zer[:], 0.0, [Bzer], eng="dve")
            oacc = P.sb("oacc", [128, NT, 256], F32)
            Boacc = [Buf() for _ in range(NT)]
            selbT = P.sb("selbT", [128, S], BF16)
            BselbT = [Buf() for _ in range(4)]
            P.memset(selbT[:], 0.0, BselbT, eng="dve")
            pT_ring = P.sb_ring("pT", [128, 512], BF16, 6)
            tmp_ring = P.sb_ring("rt", [128, 512], F32, 4)
            small = P.sb_ring("sm", [128, 8], F32, 4)
            imp_ring = P.sb_ring("imp", [128, 4, 32], F32, 2)
            imp2_ring = P.sb_ring("imp2", [128, 40], F32, 2)
            selb_ring = P.sb_ring("selb", [128, 32], BF16, 3)
            otmp_ring = P.sb_ring("otmp", [128, 4, 64], F32, 3)
            obf_ring = P.sb_ring("obf", [128, 256], BF16, 2)
            impall = P.sb("impall", [128, NT, 128], F32)
            Bimp = [Buf() for _ in range(NT)]
            ons_ring = P.sb_ring("ons", [128, 2, 128], BF16, 2)

            def make_rot(dst, Bd, srct, Bs):
                dv = dst[:].rearrange("p k (r d) -> p (k r) d", d=64)
                sv = srct[:].rearrange("p k (r d) -> p (k r) d", d=64)
                P.ts(dv[:, :, 0:8], sv[:, :, 8:16], -1.0, None, ALU.mult, None, [Bs], [Bd])
                P.cp(dv[:, :, 8:16], sv[:, :, 0:8], [Bs], [Bd])

            def rope_evac(ps1, B1, ps2, B2, c0, c1, out_ap, Bouts, cstep=None, split=None):
                t1, Bt1 = tmp_ring.next()
                t2, Bt2 = tmp_ring.next()
                n = (out_ap if split is None else split[0][0]).shape[-1]
                if cstep is None:
                    ca, sa = Ct[:, c0:c1], St[:, c0:c1]
                else:
                    ca, sa = Ct[:, c0:c1:cstep], St[:, c0:c1:cstep]
                P.tt(t1[:, 0:n], ps1, ca, ALU.mult, [B1, Btab], [Bt1])
                P.tt(t2[:, 0:n], ps2, sa, ALU.mult, [B2, Btab], [Bt2])
                if split is None:
                    P.tt(out_ap, t1[:, 0:n], t2[:, 0:n], ALU.add, [Bt1, Bt2], Bouts, eng="pool")
                else:
                    for (oap, Bo_), (r0, r1) in zip(split, ((0, 64), (64, 128))):
                        P.tt(oap, t1[r0:r1, 0:n], t2[r0:r1, 0:n], ALU.add, [Bt1, Bt2], [Bo_], eng="dve")

            for g in range(NKV):
                P.load(wq[:], win_v[:, :, OFF_Q + 256 * g:OFF_Q + 256 * g + 256], (), [Bwq], eng="pool")
                make_rot(wqr, Bwqr, wq, Bwq)
                for nm, off in (("kc", OFF_KC), ("vc", OFF_VC), ("ks", OFF_KS), ("kw", OFF_KW)):
                    for r in range(2):
                        P.load(wdup[nm][:, :, r * 64:(r + 1) * 64], win_v[:, :, off + 64 * g:off + 64 * g + 64], (), [Bwdup[nm]], eng="pool")
                for nm in ("ks", "kw"):
                    make_rot(wrot[nm], Bwrot[nm], wdup[nm], Bwdup[nm])
                P.load(wv[:, :, 0:64], win_v[:, :, OFF_VS + 64 * g:OFF_VS + 64 * g + 64], (), [Bwv], eng="pool")
                P.load(wv[:, :, 64:128], win_v[:, :, OFF_VW + 64 * g:OFF_VW + 64 * g + 64], (), [Bwv], eng="pool")
                for tb in range(4):
                    tok = slice(tb * 512, (tb + 1) * 512)
                    for cc in range(2):
                        ps1, B1 = C.psA.next()
                        ps2, B2 = C.psA.next()
                        for k in range(KC):
                            P.mm(ps1[:], wq[:, k, cc * 128:(cc + 1) * 128], xnT[:, k, tok], k == 0, k == KC - 1, xr(tb) + [Bwq], [B1])
                        for k in range(KC):
                            P.mm(ps2[:], wqr[:, k, cc * 128:(cc + 1) * 128], xnT[:, k, tok], k == 0, k == KC - 1, xr(tb) + [Bwqr], [B2])
                        rope_evac(ps1[:], B1, ps2[:], B2, tb * 512, (tb + 1) * 512, None, None,
                                  split=[(qpad[0:64, 2 * cc, tok], BqT[2 * cc][tb]), (qpad[64:128, 2 * cc + 1, tok], BqT[2 * cc + 1][tb])])
                    for nm in ("ks", "kw"):
                        ps1, B1 = C.psA.next()
                        ps2, B2 = C.psA.next()
                        for k in range(KC):
                            P.mm(ps1[:], wdup[nm][:, k, :], xnT[:, k, tok], k == 0, k == KC - 1, xr(tb) + [Bwdup[nm]], [B1])
                        for k in range(KC):
                            P.mm(ps2[:], wrot[nm][:, k, :], xnT[:, k, tok], k == 0, k == KC - 1, xr(tb) + [Bwrot[nm]], [B2])
                        rope_evac(ps1[:], B1, ps2[:], B2, tb * 512, (tb + 1) * 512, kT[nm][:, tok], [BkT[nm][tb]])
                    for nm in ("kc", "vc"):
                        ps1, B1 = C.psA.next()
                        for k in range(KC):
                            P.mm(ps1[:], wdup[nm][:, k, :], xnT[:, k, tok], k == 0, k == KC - 1, xr(tb) + [Bwdup[nm]], [B1])
                        P.act(KK[nm][0:64, tok], ps1[0:64, :], AF.Copy, [B1], [BKK[nm][tb]])
                        if tb == 0:
                            P.cp(KK[nm][64:128, 0:511], ps1[64:128, 1:512], [B1], [BKK[nm][0]])
                        else:
                            P.cp(KK[nm][64:128, tb * 512 - 1:tb * 512 + 511], ps1[64:128, :], [B1], [BKK[nm][tb], BKK[nm][tb - 1]])
                for t in range(NT):
                    ps, Bps = C.psB.next()
                    for k in range(KC):
                        P.mm(ps[:, 0:128], xnT[:, k, t * 128:(t + 1) * 128], wv[:, k, :], k == 0, k == KC - 1, [BxnT[t], Bwv], [Bps])
                    P.cp(Vs[:, t, 0:64], ps[:, 0:64], [Bps], [BVs[t]])
                    P.act(Vw[:, t, 0:64], ps[:, 64:128], AF.Copy, [Bps], [BVw[t]])
                for kv, nm in (("k", "kc"), ("v", "vc")):
                    psz, Bz = C.psA.next()
                    for m in range(16):
                        P.mm(psz[:, 0:127], W1[kv][:, m, :], KK[nm][:, 2 * m:2 * m + 2017:16], m == 0, m == 15, BKK[nm] + [BW1], [Bz])
                    P.act(HT[:, 0:127], psz[:, 0:127], AF.Gelu_apprx_tanh, [Bz, Bcb], [BHT], bias=cbias[kv][:, 0:1])
                    if kv == "k":
                        ps1, B1 = C.psA.next()
                        ps2, B2 = C.psA.next()
                        P.mm(ps1[:, 0:127], w2k[:, :], HT[:, 0:127], True, True, [Bw2, BHT], [B1])
                        P.mm(ps2[:, 0:127], w2kr[:, :], HT[:, 0:127], True, True, [Bw2, BHT], [B2])
                        rope_evac(ps1[:, 0:127], B1, ps2[:, 0:127], B2, 31, 2048, kcT[:, 0:127], [BkcT], cstep=16)
                    else:
                        ps, Bps = C.psB.next()
                        P.mm(ps[0:127, 0:64], HT[:, 0:127], w2v[:, :], True, True, [BHT, Bw2], [Bps])
                        P.cp(VC[0:127, 0:64], ps[0:127, 0:64], [Bps], [BVC])
                for tb in range(4):
                    tok = slice(tb * 512, (tb + 1) * 512)
                    PT = []
                    for hl in range(4):
                        cc, base = hl // 2, 64 * (hl % 2)
                        ps, Bps = C.psA.next()
                        P.mm(ps[:], kcT[:, :], qpad[:, hl, tok], True, False, [BkcT, BqT[hl][tb]], [Bps])
                        P.mm(ps[:], ident[:], mk[:, 8 + tb, :], False, True, [Bid, Bmk], [Bps])
                        pt, Bpt = pT_ring.next()
                        P.act(pt[:], ps[:], AF.Exp, [Bps], [Bpt], scale=0.125)
                        PT.append((pt, Bpt))
                    for qt in range(4):
                        t = 4 * tb + qt
                        pso, Bo = C.psB.next()
                        for hl in range(4):
                            P.mm(pso[:, hl * 97:(hl + 1) * 97], PT[hl][0][:, qt * 128:(qt + 1) * 128], VC[:, :], True, True,
                                 [PT[hl][1], BVC], [Bo], inc=(hl == 3))
                        pv = pso[:, 0:388].rearrange("p (h c) -> p h c", c=97)
                        sm, Bsm = small.next()
                        P.ts(sm[:, 0:4].unsqueeze(2), pv[:, :, 64:65], 1e-30, None, ALU.max, None, [Bo], [Bsm])
                        P.recip(sm[:, 0:4], sm[:, 0:4], [Bsm], [Bsm])
                        P.tt(sm[:, 4:8], sm[:, 0:4], gs[:, t, 4 * g:4 * g + 4], ALU.mult, [Bsm, Bgs[t]], [Bsm])
                        P.tt(oacc[:, t, :].rearrange("p (h d) -> p h d", d=64), pv[:, :, 0:64],
                             sm[:, 4:8].unsqueeze(2).broadcast_to([128, 4, 64]), ALU.mult, [Bo, Bsm], [Boacc[t]])
                        if C.dbg is not None:
                            P.load(C.dbg[0, t * 128:(t + 1) * 128, 256 * g:256 * g + 256], oacc[:, t, :], [Boacc[t]], [C.Bdbg])
                        P.tt(impall[:, t, :].rearrange("p (h j) -> p h j", j=32), pv[:, :, 65:97],
                             sm[:, 0:4].unsqueeze(2).broadcast_to([128, 4, 32]), ALU.mult, [Bo, Bsm], [Bimp[t]])

                def chain_A(t):
                    i2, Bi2 = imp2_ring.next()
                    P.op("dve", lambda e, o=i2[:, 0:32], i=impall[:, t, :].rearrange("p (h j) -> p j h", j=32): e.tensor_reduce(out=o, in_=i, axis=mybir.AxisListType.X, op=ALU.add),
                         [Bimp[t]], [Bi2])
                    a0 = C_A + 32 - 2 * t
                    b0 = C_B + 32 - 2 * t
                    P.tt(i2[:, 0:32], i2[:, 0:32], cst[:, a0:a0 + 32], ALU.mult, [Bi2, Bcst], [Bi2])
                    P.tt(i2[:, 0:32], i2[:, 0:32], cst[:, b0:b0 + 32], ALU.add, [Bi2, Bcst], [Bi2])
                    P.cp(i2[:, 0:1], cst[:, C_FORCE:C_FORCE + 1], [Bi2, Bcst], [Bi2])
                    P.op("dve", lambda e, o=i2[:, 32:40], i=i2[:, 0:32]: e.max(out=o, in_=i), [Bi2], [Bi2])
                    sb_, Bsb = selb_ring.next()
                    P.ts(sb_[:], i2[:, 0:32], i2[:, 39:40], NEG, ALU.is_lt, ALU.mult, [Bi2], [Bsb])
                    pendB[t] = (sb_, Bsb, t, t // 4)

                pendB = {}
                slots = [[("A", 0)]] + [[("A", u), ("B", u - 1)] for u in range(1, NT)] + [[("B", NT - 1)]]

                def run_slot(sl):
                    for kind, u in sl:
                        if kind == "A":
                            chain_A(u)
                        else:
                            deferred_B.append(pendB.pop(u))
                            flush_B()

                def flush_B():
                    if deferred_B:
                        sb_, Bsb, t, tb = deferred_B.pop(0)
                        ptr, Bptr = C.pst.next()
                        P.tr(ptr[0:32, 0:128], sb_[:, :], ident[:], [Bsb, Bid], [Bptr])
                        P.cp(selbT[0:32, t * 128:(t + 1) * 128], ptr[0:32, 0:128], [Bptr], [BselbT[tb]])

                deferred_B = []

                def flush_one():
                    if slots:
                        run_slot(slots.pop(0))

                deferred = slots
                items = []
                for branch in (2, 1):
                    for hl in range(4):
                        for tb in range(4):
                            j0 = 0 if branch == 1 else max(0, 4 * tb - 4)
                            for jt in range(j0, 4 * tb + 4):
                                items.append((hl, branch, tb, jt, jt == j0, jt == 4 * tb + 3))
                n_win = sum(1 for it_ in items if it_[1] == 2)
                state = {}

                def emit_scores(it):
                    hl, branch, tb, jt, first, last = it
                    cc, base = hl // 2, 64 * (hl % 2)
                    d = jt - 4 * tb
                    c0, c1 = 0, 512
                    if TRIM:
                        if d >= 0:
                            c0, c1 = 128 * d, 512
                        elif branch == 2:
                            c0, c1 = 0, 128 * (d + 4 + 1)
                    kt, Bkt = (kT["ks"], BkT["ks"]) if branch == 1 else (kT["kw"], BkT["kw"])
                    ps, Bps = C.psA.next()
                    q0 = tb * 512
                    P.mm(ps[:, c0:c1], kt[:, jt * 128:(jt + 1) * 128], qpad[:, hl, q0 + c0:q0 + c1], True, False,
                         [Bkt[jt // 4], BqT[hl][tb]], [Bps])
                    if branch == 1:
                        P.mm(ps[:, c0:c1], em[:, jt, :], selbT[:, q0 + c0:q0 + c1], False, d < 0, [Bem, BselbT[tb]], [Bps])
                        if d >= 0:
                            P.mm(ps[:, c0:c1], ident[:], mk[:, d, c0:c1], False, True, [Bid, Bmk], [Bps])
                    else:
                        mi = d if d >= 0 else (4 + d + 4)
                        P.mm(ps[:, c0:c1], ident[:], mk[:, mi, c0:c1], False, True, [Bid, Bmk], [Bps])
                    state[it] = (ps, Bps, c0, c1)

                def emit_rest(it):
                    hl, branch, tb, jt, first, last = it
                    h = 4 * g + hl
                    ps, Bps, c0, c1 = state.pop(it)
                    Vt, BVt = (Vs, BVs) if branch == 1 else (Vw, BVw)
                    if first:
                        pso, Bo = C.psB.next()
                        state[(hl, branch, tb)] = (pso, Bo)
                        P.mm(pso[:, 0:260], zer[:, :], mk[:, 0, 0:260], True, False, [Bzer, Bmk], [Bo], inc=False)
                    pso, Bo = state[(hl, branch, tb)]
                    pt, Bpt = pT_ring.next()
                    P.act(pt[:, c0:c1], ps[:, c0:c1], AF.Exp, [Bps], [Bpt], scale=0.125)
                    for qt in range(4):
                        T = 4 * tb + qt
                        lo = 0 if branch == 1 else max(0, T - 4)
                        if lo <= jt <= T:
                            assert c0 <= qt * 128 and (qt + 1) * 128 <= c1
                            P.mm(pso[:, qt * 65:(qt + 1) * 65], pt[:, qt * 128:(qt + 1) * 128], Vt[:, jt, :],
                                 False, (jt == 4 * tb + 3 and qt == 3), [Bpt, BVt[jt]], [Bo])
                    if last:
                        del state[(hl, branch, tb)]
                        pv = pso[:, 0:260].rearrange("p (q c) -> p q c", c=65)
                        sm, Bsm = small.next()
                        P.ts(sm[:, 0:4].unsqueeze(2), pv[:, :, 64:65], 1e-30, None, ALU.max, None, [Bo], [Bsm])
                        P.recip(sm[:, 0:4], sm[:, 0:4], [Bsm], [Bsm])
                        gcol = 16 * branch + h
                        P.tt(sm[:, 4:8].unsqueeze(2), sm[:, 0:4].unsqueeze(2), gs[:, 4 * tb:4 * tb + 4, gcol:gcol + 1], ALU.mult,
                             [Bsm] + [Bgs[4 * tb + i] for i in range(4)], [Bsm])
                        ot, Bot = otmp_ring.next()
                        P.tt(ot[:], pv[:, :, 0:64], sm[:, 4:8].unsqueeze(2).broadcast_to([128, 4, 64]), ALU.mult, [Bo, Bsm], [Bot])
                        ov = oacc[:, 4 * tb:4 * tb + 4, hl * 64:(hl + 1) * 64]
                        Bov = [Boacc[4 * tb + i] for i in range(4)]
                        P.tt(ov, ov, ot[:], ALU.add, Bov + [Bot], Bov, eng="pool")

                if PIPE:
                    for j in range(min(PIPE_DEPTH, len(items))):
                        emit_scores(items[j])
                    for i, it in enumerate(items):
                        if i + PIPE_DEPTH < len(items):
                            if i + PIPE_DEPTH >= n_win:
                                while deferred:
                                    flush_one()
                            emit_scores(items[i + PIPE_DEPTH])
                        emit_rest(it)
                        if i % 5 == 3:
                            flush_one()
                else:
                    while deferred:
                        flush_one()
                    for it in items:
                        emit_scores(it)
                        emit_rest(it)
                for t in range(NT):
                    ob, Bob = obf_ring.next()
                    P.act(ob[:], oacc[:, t, :], AF.Copy, [Boacc[t]], [Bob])
                    ptr, Bptr = C.pst.next()
                    for cc in range(2):
                        P.tr(ptr[:, cc * 128:(cc + 1) * 128], ob[:, cc * 128:(cc + 1) * 128], ident[:], [Bob, Bid], [Bptr], inc=(cc == 1))
                    ons, Bons = ons_ring.next()
                    P.cp(ons[:], ptr[:, 0:256].rearrange("p (c q) -> p c q", c=2), [Bptr], [Bons])
                    P.load(onsa_d[:, 2 * g:2 * g + 2, t * 128:(t + 1) * 128], ons[:], [Bons], [BonD[g][t]])

        o_poolT = P.sb("opoolT", [128, 4, S], BF16)
        BopT = [[Buf() for _ in range(4)] for _ in range(4)]
        with P.scope():
            wpin = P.sb("wpin", [128, KC, 512], BF16)
            Bwp = Buf()
            P.load(wpin[:], win_v[:, :, OFF_POOL:OFF_POOL + 512], (), [Bwp], eng="pool")
            pw = P.sb("pw", [128, 4, 128], BF16)
            Bpw = Buf()
            P.load(pw[:], W["pool_w"].rearrange("g c d -> c g d"), (), [Bpw], eng="pool")
            psc = P.sb("psc", [128, 4], F32)
            Bpsc = Buf()
            P.load_nc(psc[:], W["pool_scale"].rearrange("o (g p) -> p (o g)", p=128), (), [Bpsc])
            ub = [P.sb(f"ub{i}", [128, 16 + S], F32) for i in range(3)]
            Bub = [Buf() for _ in range(3)]
            for i in range(3):
                P.memset(ub[i][:, 0:16], 0.0, [Bub[i]], eng="dve")
            pl = P.sb("pl", [128, S], BF16)
            Bpl = Buf()
            ftmp = P.sb("ftmp", [128, 16], F32)
            Bft = Buf()
            for gi, w in enumerate((2, 4, 8, 16)):
                for tb in range(4):
                    ps, Bps = C.psA.next()
                    for k in range(KC):
                        P.mm(ps[:], wpin[:, k, gi * 128:(gi + 1) * 128], xnT[:, k, tb * 512:(tb + 1) * 512], k == 0, k == KC - 1, xr(tb) + [Bwp], [Bps])
                    P.act(ub[0][:, 16 + tb * 512:16 + (tb + 1) * 512], ps[:], AF.Copy, [Bps], [Bub[0]])
                cur = 0
                pp = [1, 2]
                step = 1
                for _ in range(gi + 1):
                    nxt = pp[0]
                    pp = pp[::-1]
                    P.tt(ub[nxt][:, 16:16 + S], ub[cur][:, 16:16 + S], ub[cur][:, 16 - step:16 - step + S], ALU.add, [Bub[cur]], [Bub[nxt]])
                    cur = nxt
                    step *= 2
                P.stt(pl[:], ub[cur][:, 16:16 + S], 1.0 / w, ub[0][:, 16:16 + S], ALU.mult, ALU.subtract, [Bub[cur], Bub[0]], [Bpl])
                P.tt(ftmp[:, 0:w - 1], ub[cur][:, 16:16 + w - 1], cst[:, C_RC:C_RC + w - 1], ALU.mult, [Bub[cur], Bcst], [Bft])
                P.tt(pl[:, 0:w - 1], ftmp[:, 0:w - 1], ub[0][:, 16:16 + w - 1], ALU.subtract, [Bft, Bub[0]], [Bpl])
                for tb in range(4):
                    ps, Bps = C.psA.next()
                    P.mm(ps[:], pw[:, gi, :], pl[:, tb * 512:(tb + 1) * 512], True, True, [Bpw, Bpl], [Bps])
                    P.act(o_poolT[:, gi, tb * 512:(tb + 1) * 512], ps[:], AF.Copy, [Bps, Bpsc], [BopT[gi][tb]], scale=psc[:, gi:gi + 1])

        with P.scope():
            o_nsaT = P.sb("onsaT", [128, KC, S], BF16)
            BonT = [Buf() for _ in range(4)]
            for tb_ in range(4):
                P.load(o_nsaT[:, :, tb_ * 512:(tb_ + 1) * 512], onsa_d[:, :, tb_ * 512:(tb_ + 1) * 512],
                       [BonD[g_][tb_ * 4 + i] for g_ in range(4) for i in range(4)], [BonT[tb_]])
            yT = P.sb("yT", [128, KC, S], BF16)
            ByT = [[Buf() for _ in range(4)] for _ in range(KC)]
            wab_v = W["w_attn_branch"].rearrange("(kc p) c -> p kc c", p=128)
            wpb_v = W["w_pool_branch"].rearrange("(kc p) c -> p kc c", p=128)
            with P.scope():
                wab_r = P.sb_ring("wab", [128, KC, 256], BF16, 2)
                wpb_r = P.sb_ring("wpb", [128, 4, 256], BF16, 2)
                wga_r = P.sb_ring("wga", [128, KC, 256], BF16, 2)
                wgp_r = P.sb_ring("wgp", [128, KC, 256], BF16, 2)
                sg_r = P.sb_ring("msg", [128, 512], F32, 4)
                t_r = P.sb_ring("mt", [128, 512], F32, 4)
                for dp in range(4):
                    wab, Bwab = wab_r.next()
                    wpb, Bwpb = wpb_r.next()
                    wga, Bwga = wga_r.next()
                    wgp, Bwgp = wgp_r.next()
                    cs = slice(dp * 256, (dp + 1) * 256)
                    P.load(wab[:], wab_v[:, :, cs], (), [Bwab], eng="pool")
                    P.load(wpb[:], wpb_v[:, :, cs], (), [Bwpb], eng="pool")
                    P.load(wga[:], win_v[:, :, OFF_MG + dp * 256:OFF_MG + (dp + 1) * 256], (), [Bwga], eng="pool")
                    P.load(wgp[:], win_v[:, :, OFF_MG + 1024 + dp * 256:OFF_MG + 1024 + (dp + 1) * 256], (), [Bwgp], eng="pool")
                    for dl in range(2):
                        dc = dp * 2 + dl
                        cl = slice(dl * 128, (dl + 1) * 128)
                        for tb in range(4):
                            tok = slice(tb * 512, (tb + 1) * 512)
                            pa, Bpa = C.psA.next()
                            pp_, Bpp = C.psA.next()
                            pga, Bpga = C.psA.next()
                            pgp, Bpgp = C.psA.next()
                            for k in range(KC):
                                P.mm(pa[:], wab[:, k, cl], o_nsaT[:, k, tok], k == 0, k == KC - 1,
                                     [Bwab, BonT[tb]], [Bpa])
                            for k in range(4):
                                P.mm(pp_[:], wpb[:, k, cl], o_poolT[:, k, tok], k == 0, k == 3, [Bwpb, BopT[k][tb]], [Bpp])
                            for k in range(KC):
                                P.mm(pga[:], wga[:, k, cl], xnT[:, k, tok], k == 0, k == KC - 1, xr(tb) + [Bwga], [Bpga])
                            for k in range(KC):
                                P.mm(pgp[:], wgp[:, k, cl], xnT[:, k, tok], k == 0, k == KC - 1, xr(tb) + [Bwgp], [Bpgp])
                            sa, Bsa = sg_r.next()
                            sp_, Bsp = sg_r.next()
                            P.act(sa[:], pga[:], AF.Sigmoid, [Bpga], [Bsa])
                            P.act(sp_[:], pgp[:], AF.Sigmoid, [Bpgp], [Bsp])
                            t1, Bt1 = t_r.next()
                            t2, Bt2 = t_r.next()
                            P.tt(t1[:], pa[:], sa[:], ALU.mult, [Bpa, Bsa], [Bt1])
                            P.tt(t2[:], pp_[:], sp_[:], ALU.mult, [Bpp, Bsp], [Bt2])
                            P.tt(yT[:, dc, tok], t1[:], t2[:], ALU.add, [Bt1, Bt2], [ByT[dc][tb]], eng="pool")
            with P.scope():
                wo = P.sb("wo", [128, KC, D], BF16)
                Bwo = Buf()
                P.load(wo[:], W["w_out"].rearrange("(kc p) c -> p kc c", p=128), (), [Bwo], eng="pool")
                gbc = P.sb("gbc3", [128, D], F32)
                Bg = Buf()
                P.load(gbc[:], W["g_mix_post"].partition_broadcast(128), (), [Bg])
                xt_ring = P.sb_ring("mxr", [128, D], F32, 2)
                st_ring = P.sb_ring("mst", [128, 8], F32, 3)
                tmp2_ring = P.sb_ring("mtmp", [128, 512], F32, 3)
                for t in range(NT):
                    xt, Bxt = xt_ring.next()
                    P.load(xt[:], src[t * 128:(t + 1) * 128, :], [Bsrc[t]], [Bxt])
                    pos_, Bpos = [], []
                    for dh in range(2):
                        po, Bpo = C.psA.next()
                        for k in range(KC):
                            P.mm(po[:], yT[:, k, t * 128:(t + 1) * 128], wo[:, k, dh * 512:(dh + 1) * 512], k == 0, k == KC - 1,
                                 [ByT[k][t // 4], Bwo], [Bpo])
                        pos_.append(po)
                        Bpos.append(Bpo)
                    post_norm_residual(P, C, pos_, Bpos, gbc, Bg, xt, Bxt, 1.0, dst[t * 128:(t + 1) * 128, :], Bdst[t], st_ring, tmp2_ring)


_NC_CACHE = {}


def kernel(**inputs):
    n = 8
    if "nc" not in _NC_CACHE:
        _NC_CACHE["nc"] = build_nc()[0]
    nc = _NC_CACHE["nc"]
    in_maps = []
    hc, hm, he = host_consts()
    for b in range(n):
        m = {"kconsts": hc, "kmasks": hm, "kemat": he}
        for k, v in inputs.items():
            v = np.asarray(v)
            if k == "x":
                m[k] = np.ascontiguousarray(v[b])
            elif k == "positions":
                m[k] = np.ascontiguousarray(v[b].reshape(1, S))
            else:
                a = v[0]
                if a.ndim == 1:
                    a = a.reshape(1, -1)
                m[k] = np.ascontiguousarray(a)
        in_maps.append(m)
    res = run_bass_kernel_spmd(nc, in_maps, core_ids=list(range(n)))
    return np.stack([np.asarray(r["out"]).reshape(S, D) for r in res.results], axis=0).astype(np.float32)
```

```python
import numpy as np
from contextlib import ExitStack, contextmanager
import concourse.bass as bass
import concourse.mybir as mybir
from concourse.bass_utils import run_bass_kernel_spmd

F32 = mybir.dt.float32
BF16 = mybir.dt.bfloat16
I32 = mybir.dt.int32
ALU = mybir.AluOpType
AF = mybir.ActivationFunctionType

ENGINES = ("pe", "act", "dve", "pool", "sp")
SEM_ROLL = 30000

D = 1024
S = 2048
DFF = 2816
NT = S // 128
NF = DFF // 128
KC = D // 128
NH = 16
HD = 64
NKV = 4
IN_TOTAL = 5168
EPS = 1e-6
NEG = -30000.0


class Buf:
    __slots__ = ("name", "w", "r")

    def __init__(self, name=""):
        self.name = name
        self.w = None
        self.r = []


class Ring:
    def __init__(self, items):
        self.items = items
        self.i = 0

    def next(self):
        it = self.items[self.i]
        self.i = (self.i + 1) % len(self.items)
        return it


class Prog:
    def __init__(self, nc, n_dma_sems=24):
        self.nc = nc
        self.es = ExitStack()
        self.scopes = [self.es]
        self.streams = {e: [] for e in ENGINES}
        self.sem = {}
        self.cnt = {}
        self.nsem = 0
        for e in ENGINES:
            self._new_engine_sem(e)
        self.known = {e: {} for e in ENGINES}
        self.dma_sems = [self.es.enter_context(nc.semaphore(f"dq{i}")) for i in range(n_dma_sems)]
        self.dma_cnt = [0] * n_dma_sems
        self.dma_rr = 0
        self.ninstr = 0
        self.uid = 0

    def _new_engine_sem(self, e):
        self.nsem += 1
        self.sem[e] = self.es.enter_context(self.nc.semaphore(f"s_{e}_{self.nsem}"))
        self.cnt[e] = 0

    @contextmanager
    def scope(self):
        es = ExitStack()
        self.scopes.append(es)
        try:
            yield
        finally:
            self.barrier()
            self.scopes.pop()
            es.close()

    def sb(self, name, shape, dt):
        self.uid += 1
        return self.scopes[-1].enter_context(self.nc.sbuf_tensor(f"{name}_{self.uid}", list(shape), dt))

    def ps(self, name, shape, dt):
        self.uid += 1
        return self.scopes[-1].enter_context(self.nc.psum_tensor(f"{name}_{self.uid}", list(shape), dt))

    def sb_ring(self, name, shape, dt, n):
        return Ring([(self.sb(f"{name}{i}", shape, dt), Buf(f"{name}{i}")) for i in range(n)])

    def _need(self, eng, tok, waits):
        if tok is None:
            return
        sem, val, teng = tok
        if teng == "pe" and eng == "pe":
            return
        sid = id(sem)
        if self.known[eng].get(sid, 0) >= val:
            return
        cur = waits.get(sid)
        if cur is None or cur[1] < val:
            waits[sid] = (sem, val)

    def _collect(self, eng, reads, writes):
        waits = {}
        for b in reads:
            self._need(eng, b.w, waits)
        for b in writes:
            self._need(eng, b.w, waits)
            for t in b.r:
                self._need(eng, t, waits)
        return waits

    def _emit_waits(self, eng, waits):
        st = self.streams[eng]
        for sid, (sem, val) in waits.items():
            st.append(lambda e, sem=sem, val=val: e.wait_ge(sem, val))
            self.known[eng][sid] = val

    def _commit(self, tok, reads, writes):
        for b in reads:
            b.r.append(tok)
            if len(b.r) > 48:
                best = {}
                for t in b.r:
                    k = id(t[0])
                    if k not in best or best[k][1] < t[1]:
                        best[k] = t
                b.r = list(best.values())
        for b in writes:
            b.w = tok
            b.r = []

    def op(self, eng, fn, reads=(), writes=(), inc=True):
        waits = self._collect(eng, reads, writes)
        self._emit_waits(eng, waits)
        self.ninstr += 1
        if inc:
            if self.cnt[eng] >= SEM_ROLL:
                self._new_engine_sem(eng)
            self.cnt[eng] += 1
            sem, val = self.sem[eng], self.cnt[eng]
            self.streams[eng].append(lambda e, fn=fn, sem=sem: fn(e).then_inc(sem, 1))
            tok = (sem, val, eng)
        else:
            self.streams[eng].append(lambda e, fn=fn: fn(e))
            tok = (self.sem[eng], self.cnt[eng] + 1, eng)
        self._commit(tok, reads, writes)
        return tok

    def dma(self, eng, fn, reads=(), writes=()):
        i = self.dma_rr
        self.dma_rr = (self.dma_rr + 1) % len(self.dma_sems)
        sem = self.dma_sems[i]
        waits = self._collect(eng, reads, writes)
        if self.dma_cnt[i] > 0:
            self._need(eng, (sem, self.dma_cnt[i], "dma"), waits)
        self._emit_waits(eng, waits)
        self.dma_cnt[i] += 16
        val = self.dma_cnt[i]
        self.streams[eng].append(lambda e, fn=fn, sem=sem: fn(e).then_inc(sem, 16))
        tok = (sem, val, "dma")
        self._commit(tok, reads, writes)
        self.ninstr += 1
        return tok

    def barrier(self):
        for eng in ENGINES:
            waits = {}
            for e2 in ENGINES:
                if e2 != eng and self.cnt[e2] > 0:
                    self._need(eng, (self.sem[e2], self.cnt[e2], e2), waits)
            if self.cnt[eng] > 0 and eng != "pe":
                self._need(eng, (self.sem[eng], self.cnt[eng], eng + "_self"), waits)
            for i, sem in enumerate(self.dma_sems):
                if self.dma_cnt[i] > 0:
                    self._need(eng, (sem, self.dma_cnt[i], "dma"), waits)
            self._emit_waits(eng, waits)

    def finish(self):
        nc = self.nc
        streams = self.streams
        self.barrier()
        with nc.Block() as block:
            @block.tensor
            def _(e):
                for f in streams["pe"]:
                    f(e)

            @block.scalar
            def _(e):
                for f in streams["act"]:
                    f(e)

            @block.vector
            def _(e):
                for f in streams["dve"]:
                    f(e)

            @block.gpsimd
            def _(e):
                for f in streams["pool"]:
                    f(e)

            @block.sync
            def _(e):
                for f in streams["sp"]:
                    f(e)
        self.es.close()

    def mm(self, out, lhsT, rhs, start, stop, reads, writes, inc=None):
        if inc is None:
            inc = stop
        return self.op("pe", lambda e: e.matmul(out, lhsT, rhs, start=start, stop=stop), reads, writes, inc=inc)

    def tr(self, out, in_, ident, reads, writes, inc=True):
        return self.op("pe", lambda e: e.transpose(out, in_, ident), reads, writes, inc=inc)

    def act(self, out, in_, func, reads, writes, **kw):
        return self.op("act", lambda e: e.activation(out=out, in_=in_, func=func, **kw), reads, writes)

    def tt(self, out, in0, in1, op, reads, writes, eng="dve"):
        return self.op(eng, lambda e: e.tensor_tensor(out=out, in0=in0, in1=in1, op=op), reads, writes)

    def ts(self, out, in0, s1, s2, op0, op1, reads, writes, eng="dve"):
        if op1 is None:
            return self.op(eng, lambda e: e.tensor_scalar(out=out, in0=in0, scalar1=s1, scalar2=None, op0=op0), reads, writes)
        return self.op(eng, lambda e: e.tensor_scalar(out=out, in0=in0, scalar1=s1, scalar2=s2, op0=op0, op1=op1), reads, writes)

    def stt(self, out, in0, scalar, in1, op0, op1, reads, writes):
        return self.op("dve", lambda e: e.scalar_tensor_tensor(out=out, in0=in0, scalar=scalar, in1=in1, op0=op0, op1=op1), reads, writes)

    def cp(self, out, in_, reads, writes, eng="dve"):
        return self.op(eng, lambda e: e.tensor_copy(out=out, in_=in_), reads, writes)

    def recip(self, out, in_, reads, writes):
        return self.op("dve", lambda e: e.reciprocal(out=out, in_=in_), reads, writes)

    def memset(self, out, val, writes, eng="pool"):
        return self.op(eng, lambda e: e.memset(out, val), (), writes)

    def load(self, out, in_, reads, writes, eng="sp"):
        return self.dma(eng, lambda e: e.dma_start(out=out, in_=in_), reads, writes)

    def load_nc(self, out, in_, reads, writes, eng="sp"):
        return self.dma(eng, lambda e: e.dma_start(out=out, in_=in_, allow_slow_non_contiguous=True), reads, writes)


class Ctx:
    pass


def norm_transpose_phase(P, C, src, Bsrc, g_dram, xnT, BxnT, tag):
    gbc = P.sb("gbc", [128, D], F32)
    Bg = Buf()
    P.load(gbc[:], g_dram.partition_broadcast(128), (), [Bg])
    xt_ring = P.sb_ring("xt", [128, D], F32, 2)
    xn_ring = P.sb_ring("xn", [128, D], BF16, 2)
    st_ring = P.sb_ring("st", [128, 4], F32, 3)
    for t in range(NT):
        xt, Bxt = xt_ring.next()
        xn, Bxn = xn_ring.next()
        st, Bst = st_ring.next()
        P.load(xt[:], src[t * 128:(t + 1) * 128, :], [Bsrc[t]], [Bxt])
        P.act(xn[:], xt[:], AF.Square, [Bxt], [Bxn, Bst], accum_out=st[:, 0:1])
        P.act(st[:, 1:2], st[:, 0:1], AF.Sqrt, [Bst], [Bst], scale=1.0 / D, bias=EPS)
        P.recip(st[:, 2:3], st[:, 1:2], [Bst], [Bst])
        P.stt(xn[:], xt[:], st[:, 2:3], gbc[:], ALU.mult, ALU.mult, [Bxt, Bst, Bg], [Bxn])
        pt, Bpt = C.pst.next()
        for k in range(KC):
            P.tr(pt[:, k * 128:(k + 1) * 128], xn[:, k * 128:(k + 1) * 128], C.ident[:], [Bxn, C.Bident], [Bpt], inc=(k == KC - 1))
        P.cp(xnT[:, :, t * 128:(t + 1) * 128], pt[:].rearrange("p (k c) -> p k c", k=KC), [Bpt], [BxnT[t]])


def post_norm_residual(P, C, po_list, Bpo_list, gbc, Bg, res_t, Bres, half_scale, out_dram, Bout, stt_ring, tmp_ring):
    st, Bst = stt_ring.next()
    junk, Bj = tmp_ring.next()
    P.act(junk[:], po_list[0][:], AF.Square, [Bpo_list[0]], [Bj, Bst], accum_out=st[:, 0:1])
    junk2, Bj2 = tmp_ring.next()
    P.act(junk2[:], po_list[1][:], AF.Square, [Bpo_list[1]], [Bj2, Bst], accum_out=st[:, 1:2])
    P.tt(st[:, 2:3], st[:, 0:1], st[:, 1:2], ALU.add, [Bst], [Bst])
    sc = 1.0 / (half_scale * half_scale)
    P.act(st[:, 3:4], st[:, 2:3], AF.Sqrt, [Bst], [Bst], scale=sc / D, bias=EPS * sc)
    P.recip(st[:, 4:5], st[:, 3:4], [Bst], [Bst])
    for hh in range(2):
        tmp, Bt = tmp_ring.next()
        P.stt(tmp[:], po_list[hh][:], st[:, 4:5], gbc[:, hh * 512:(hh + 1) * 512], ALU.mult, ALU.mult,
              [Bpo_list[hh], Bst, Bg], [Bt])
        P.tt(res_t[:, hh * 512:(hh + 1) * 512], res_t[:, hh * 512:(hh + 1) * 512], tmp[:], ALU.add, [Bres, Bt], [Bres])
    P.load(out_dram, res_t[:], [Bres], [Bout])


def ffn_block(P, C, src, Bsrc, dst, Bdst, g_pre, wg, wu, wd, g_post):
    FG = 2
    with P.scope():
        hT = P.sb("hT", [128, NF, S], BF16)
        BhT = [[Buf() for _ in range(4)] for _ in range(NF)]
        NA = 10
        WG = 2
        wd_v = wd.rearrange("(fc p) d -> p fc d", p=128)
        wdA = P.sb("wdA", [128, NA, D], BF16)
        Bwd = [Buf() for _ in range(NF // WG)]
        with P.scope():
            xnT = P.sb("xnT", [128, KC, S], BF16)
            BxnT = [Buf() for _ in range(NT)]
            norm_transpose_phase(P, C, src, Bsrc, g_pre, xnT, BxnT, "f")
            wg_ring = P.sb_ring("wg", [128, KC, FG * 128], BF16, 2)
            wu_ring = P.sb_ring("wu", [128, KC, FG * 128], BF16, 2)
            sg_ring = P.sb_ring("sg", [128, 512], F32, 2)
            wd_pending = list(range(NA // WG))
            wg_v = wg.rearrange("(kc p) f -> p kc f", p=128)
            wu_v = wu.rearrange("(kc p) f -> p kc f", p=128)
            for fg in range(NF // FG):
                wgt, Bwg = wg_ring.next()
                wut, Bwu = wu_ring.next()
                P.load(wgt[:], wg_v[:, :, fg * FG * 128:(fg + 1) * FG * 128], (), [Bwg], eng="pool")
                P.load(wut[:], wu_v[:, :, fg * FG * 128:(fg + 1) * FG * 128], (), [Bwu], eng="pool")
                if fg >= 2 and wd_pending:
                    i_ = wd_pending.pop(0)
                    P.load(wdA[:, i_ * WG:(i_ + 1) * WG, :], wd_v[:, i_ * WG:(i_ + 1) * WG, :], (), [Bwd[i_]], eng="pool")
                for tb in range(4):
                    for fc in range(FG):
                        f = fg * FG + fc
                        pg, Bpg = C.psA.next()
                        pu, Bpu = C.psA.next()
                        rd = [BxnT[tb * 4 + i] for i in range(4)]
                        for k in range(KC):
                            P.mm(pg[:], wgt[:, k, fc * 128:(fc + 1) * 128], xnT[:, k, tb * 512:(tb + 1) * 512],
                                 k == 0, k == KC - 1, rd + [Bwg], [Bpg])
                        for k in range(KC):
                            P.mm(pu[:], wut[:, k, fc * 128:(fc + 1) * 128], xnT[:, k, tb * 512:(tb + 1) * 512],
                                 k == 0, k == KC - 1, rd + [Bwu], [Bpu])
                        sg, Bsg = sg_ring.next()
                        P.act(sg[:], pg[:], AF.Silu, [Bpg], [Bsg])
                        P.tt(hT[:, f, tb * 512:(tb + 1) * 512], pu[:], sg[:], ALU.mult, [Bpu, Bsg], [BhT[f][tb]])
        with P.scope():
            wdB = P.sb("wdB", [128, NF - NA, D], BF16)
            for i in range(NA // WG, NF // WG):
                P.load(wdB[:, i * WG - NA:(i + 1) * WG - NA, :], wd_v[:, i * WG:(i + 1) * WG, :], (), [Bwd[i]], eng="pool")
            gbc = P.sb("gbc2", [128, D], F32)
            Bg = Buf()
            P.load(gbc[:], g_post.partition_broadcast(128), (), [Bg])
            xt_ring = P.sb_ring("xr", [128, D], F32, 2)
            st_ring = P.sb_ring("st2", [128, 8], F32, 3)
            tmp_ring = P.sb_ring("tmp", [128, 512], F32, 3)
            for t in range(NT):
                xt, Bxt = xt_ring.next()
                P.load(xt[:], src[t * 128:(t + 1) * 128, :], [Bsrc[t]], [Bxt])
                pos, Bpos = [], []
                for dh in range(2):
                    po, Bpo = C.psA.next()
                    for f in range(NF):
                        P.mm(po[:], hT[:, f, t * 128:(t + 1) * 128],
                             (wdA[:, f, dh * 512:(dh + 1) * 512] if f < NA else wdB[:, f - NA, dh * 512:(dh + 1) * 512]),
                             f == 0, f == NF - 1, [BhT[f][t // 4], Bwd[f // WG]], [Bpo])
                    pos.append(po)
                    Bpos.append(Bpo)
                post_norm_residual(P, C, pos, Bpos, gbc, Bg, xt, Bxt, 0.5, dst[t * 128:(t + 1) * 128, :], Bdst[t],
                                   st_ring, tmp_ring)


def setup_consts(P, C):
    C.ident = P.sb("ident", [128, 128], BF16)
    C.Bident = Buf()
    tmpi = P.sb("tmpi", [128, 128], F32)
    Bt = Buf()
    P.op("pool", lambda e: e.iota(tmpi[:], [[-1, 128]], base=0, channel_multiplier=1, allow_small_or_imprecise_dtypes=True), (), [Bt])
    P.op("dve", lambda e: e.tensor_single_scalar(out=C.ident[:], in_=tmpi[:], scalar=0.0, op=ALU.is_equal), [Bt], [C.Bident])
    C.psA = Ring([(P.ps(f"psA{i}", [128, 512], F32), Buf()) for i in range(4)])
    C.pst = Ring([(P.ps(f"pst{i}", [128, 1024], BF16), Buf()) for i in range(2)])
    C.psB = Ring([(P.ps(f"psB{i}", [128, 512], F32), Buf()) for i in range(2)])


STAGE = 3
DBG = False
PIPE = True
PIPE_DEPTH = 3
TRIM = True


def build_nc(stage=STAGE):
    nc = bass.Bass("TRN2", target_bir_lowering=False)

    def din(name, shape, dt=F32):
        return nc.dram_tensor(name, list(shape), dt, kind="ExternalInput").ap()

    x = din("x", [S, D])
    pos = din("positions", [1, S], I32)
    W = {}
    for n, shp in (("g_ffn1_pre", [1, D]), ("w_ffn1_gate", [D, DFF]), ("w_ffn1_up", [D, DFF]), ("w_ffn1_down", [DFF, D]),
                   ("g_ffn1_post", [1, D]), ("g_mix_pre", [1, D]), ("w_in", [D, IN_TOTAL]),
                   ("cmp_pe_k", [32, 64]), ("cmp_w1_k", [2048, 128]), ("cmp_w2_k", [128, 64]),
                   ("cmp_pe_v", [32, 64]), ("cmp_w1_v", [2048, 128]), ("cmp_w2_v", [128, 64]),
                   ("w_attn_branch", [D, D]), ("pool_w", [4, 128, 128]), ("pool_scale", [1, 512]),
                   ("w_pool_branch", [512, D]), ("w_out", [D, D]), ("g_mix_post", [1, D]),
                   ("g_ffn2_pre", [1, D]), ("w_ffn2_gate", [D, DFF]), ("w_ffn2_up", [D, DFF]), ("w_ffn2_down", [DFF, D]),
                   ("g_ffn2_post", [1, D])):
        W[n] = din(n, shp)
    consts = din("kconsts", [128, C_N])
    masks = din("kmasks", [128, 12, 512])
    emat = din("kemat", [128, 16, 128])
    out = nc.dram_tensor("out", [S, D], F32, kind="ExternalOutput").ap()
    x1 = nc.dram_tensor("x1_scratch", [S, D], F32).ap()
    x2 = nc.dram_tensor("x2_scratch", [S, D], F32).ap()
    onsa_dram = nc.dram_tensor("onsa_scratch", [128, KC, S], BF16).ap()

    P = Prog(nc)
    C = Ctx()
    C.dbg = None
    if DBG:
        C.dbg = nc.dram_tensor("dbg", [3, S, D], F32, kind="ExternalOutput").ap()
        C.Bdbg = Buf()
    setup_consts(P, C)
    C.onsa_d = onsa_dram
    Bx = [Buf() for _ in range(NT)]
    Bx1 = [Buf() for _ in range(NT)]
    Bx2 = [Buf() for _ in range(NT)]
    Bout = [Buf() for _ in range(NT)]
    if stage == 1:
        ffn_block(P, C, x, Bx, out, Bout, W["g_ffn1_pre"], W["w_ffn1_gate"], W["w_ffn1_up"], W["w_ffn1_down"], W["g_ffn1_post"])
    else:
        ffn_block(P, C, x, Bx, x1, Bx1, W["g_ffn1_pre"], W["w_ffn1_gate"], W["w_ffn1_up"], W["w_ffn1_down"], W["g_ffn1_post"])
        mixer_block(P, C, x1, Bx1, (out if stage == 2 else x2), (Bout if stage == 2 else Bx2), pos, W, consts, masks, emat)
        if stage >= 3:
            ffn_block(P, C, x2, Bx2, out, Bout, W["g_ffn2_pre"], W["w_ffn2_gate"], W["w_ffn2_up"], W["w_ffn2_down"], W["g_ffn2_post"])
    P.finish()
    return nc, P


OFF_Q, OFF_KC, OFF_VC, OFF_KS, OFF_VS, OFF_KW, OFF_VW, OFF_G, OFF_POOL, OFF_MG = 0, 1024, 1280, 1536, 1792, 2048, 2304, 2560, 2608, 3120
C_INV, C_RC, C_A, C_B, C_OV, C_FORCE, C_N = 0, 1, 17, 81, 145, 177, 178
PI = float(np.pi)
PI_SAFE = 3.1415


def host_consts():
    c = np.zeros((128, C_N), np.float32)
    p = np.arange(128)
    d = p % 64
    inv = 500000.0 ** (-(np.arange(8, dtype=np.float32)) * (2.0 / 16.0))
    c[:, C_INV] = np.where(d < 16, inv[d % 8], 0.0)
    c[:, C_RC:C_RC + 16] = 1.0 / (np.arange(16, dtype=np.float32) + 1.0)
    cur = (p // 64)[:, None]
    rel = (np.arange(64) - 32)[None, :]
    c[:, C_A:C_A + 64] = (rel < cur).astype(np.float32)
    c[:, C_B:C_B + 64] = np.where(rel == cur, 1e4, np.where(rel > cur, -1e30, 0.0))
    n = np.arange(128)[:, None]
    j = np.arange(32)[None, :]
    ov = np.minimum(16 * n + 32, 64 * j + 64) - np.maximum(16 * n, 64 * j)
    ov = np.clip(ov, 0, None).astype(np.float32) / 32.0
    ov[127, :] = 0.0
    c[:, C_OV:C_OV + 32] = ov
    c[:, C_FORCE] = 1e4
    masks = np.zeros((128, 12, 512), np.float32)
    s_ = np.arange(128)[:, None]
    tl = np.arange(512)[None, :]
    for jj in range(4):
        masks[:, jj, :] = np.where(128 * jj + s_ <= tl, 0.0, NEG)
        masks[:, 4 + jj, :] = np.where(128 * jj + s_ > tl, 0.0, NEG)
        masks[:, 8 + jj, :] = np.where(16 * s_ + 31 <= 512 * jj + tl, 0.0, NEG)
    em = np.zeros((128, 16, 128), np.float32)
    for jt in range(16):
        for s in range(128):
            em[2 * jt + s // 64, jt, s] = 1.0
    return c, masks, em


def mixer_block(P, C, src, Bsrc, dst, Bdst, pos, W, consts, masks, emat):
    win_v = W["w_in"].rearrange("(kc p) c -> p kc c", p=128)
    ident = C.ident
    Bid = C.Bident
    with P.scope():
        xnT = P.sb("mxnT", [128, KC, S], BF16)
        BxnT = [Buf() for _ in range(NT)]
        with P.scope():
            norm_transpose_phase(P, C, src, Bsrc, W["g_mix_pre"], xnT, BxnT, "m")
        onsa_d = C.onsa_d
        BonD = [[Buf() for _ in range(NT)] for _ in range(4)]
        cst = P.sb("cst", [128, C_N], F32)
        Bcst = Buf()
        P.load(cst[:], consts[:, :], (), [Bcst])

        def xr(tb):
            return [BxnT[tb * 4 + i] for i in range(4)]

        with P.scope():
            gs = P.sb("gs", [128, NT, 48], F32)
            Bgs = [Buf() for _ in range(NT)]
            mk = P.sb("mk", [128, 12, 512], BF16)
            Bmk = Buf()
            P.load(mk[:], masks[:, :, :], (), [Bmk], eng="pool")
            em = P.sb("em", [128, 16, 128], BF16)
            Bem = Buf()
            P.load(em[:], emat[:, :, :], (), [Bem], eng="pool")
            Ct = P.sb("Ct", [128, S], F32)
            St = P.sb("St", [128, S], F32)
            Btab = Buf()
            with P.scope():
                posi = P.sb("posi", [128, S], I32)
                Bp = Buf()
                P.load(posi[:], pos.partition_broadcast(128), (), [Bp])
                ang = P.sb("ang", [128, S], F32)
                Ba = Buf()
                P.cp(ang[:], posi[:], [Bp], [Ba])
                P.ts(ang[:], ang[:], cst[:, C_INV:C_INV + 1], None, ALU.mult, None, [Ba, Bcst], [Ba])
                kf = P.sb("kf", [128, S], F32)
                Bk = Buf()
                for tab, shift, bias in ((St, 0.0, 0.0), (Ct, 0.25, PI / 2)):
                    P.ts(posi[:], ang[:], 1.0 / (2 * PI), shift, ALU.mult, ALU.add, [Ba], [Bp])
                    P.cp(kf[:], posi[:], [Bp], [Bk])
                    P.stt(kf[:], kf[:], -2 * PI, ang[:], ALU.mult, ALU.add, [Bk, Ba], [Bk])
                    P.ts(kf[:], kf[:], PI_SAFE - bias, -PI_SAFE - bias, ALU.min, ALU.max, [Bk], [Bk])
                    if bias == 0.0:
                        P.act(tab[:], kf[:], AF.Sin, [Bk], [Btab])
                    else:
                        hp = P.sb("hp", [128, 1], F32)
                        Bhp = Buf()
                        P.memset(hp[:], bias, [Bhp])
                        P.act(tab[:], kf[:], AF.Sin, [Bk, Bhp], [Btab], bias=hp[:, 0:1])
            wgate = P.sb("wgate", [128, KC, 48], BF16)
            Bwgt = Buf()
            P.load(wgate[:], win_v[:, :, OFF_G:OFF_G + 48], (), [Bwgt], eng="pool")
            for t in range(NT):
                ps, Bps = C.psB.next()
                for k in range(KC):
                    P.mm(ps[:, 0:48], xnT[:, k, t * 128:(t + 1) * 128], wgate[:, k, :], k == 0, k == KC - 1, [BxnT[t], Bwgt], [Bps])
                P.act(gs[:, t, :], ps[:, 0:48], AF.Sigmoid, [Bps], [Bgs[t]])
            W1 = {}
            BW1 = Buf()
            cbias = {}
            Bcb = Buf()
            pe16 = P.sb("pe16", [16, 128], F32)
            pe16b = P.sb("pe16b", [16, 128], BF16)
            peb = P.sb("peb", [128, 16], BF16)
            Bpe = Buf()
            for kv in ("k", "v"):
                W1[kv] = P.sb("W1" + kv, [128, 16, 128], BF16)
                P.load(W1[kv][:], W[f"cmp_w1_{kv}"].rearrange("(m p) h -> p m h", p=128), (), [BW1], eng="pool")
                P.load(pe16[:], W[f"cmp_pe_{kv}"].rearrange("(m lp) d -> m (lp d)", lp=2), (), [Bpe])
                P.cp(pe16b[:], pe16[:], [Bpe], [Bpe])
                ptr, Bptr = C.pst.next()
                P.tr(ptr[:, 0:16], pe16b[:, :], ident[0:16, 0:16], [Bpe, Bid], [Bptr])
                P.cp(peb[:], ptr[:, 0:16], [Bptr], [Bpe])
                ps, Bps = C.psB.next()
                for m in range(16):
                    P.mm(ps[:, 0:1], W1[kv][:, m, :], peb[:, m:m + 1], m == 0, m == 15, [BW1, Bpe], [Bps])
                cbias[kv] = P.sb("cb" + kv, [128, 1], F32)
                P.cp(cbias[kv][:], ps[:, 0:1], [Bps], [Bcb])
            w2k = P.sb("w2k", [128, 128], BF16)
            w2kr = P.sb("w2kr", [128, 128], BF16)
            w2v = P.sb("w2v", [128, 64], BF16)
            Bw2 = Buf()
            P.load(w2k[:, 0:64], W["cmp_w2_k"][:, :], (), [Bw2], eng="pool")
            P.load(w2k[:, 64:128], W["cmp_w2_k"][:, :], (), [Bw2], eng="pool")
            P.load(w2v[:], W["cmp_w2_v"][:, :], (), [Bw2], eng="pool")
            P.memset(w2kr[:], 0.0, [Bw2], eng="dve")
            w2k3 = w2k[:].rearrange("p (r d) -> p r d", d=64)
            w2kr3 = w2kr[:].rearrange("p (r d) -> p r d", d=64)
            P.ts(w2kr3[:, :, 0:8], w2k3[:, :, 8:16], -1.0, None, ALU.mult, None, [Bw2], [Bw2])
            P.cp(w2kr3[:, :, 8:16], w2k3[:, :, 0:8], [Bw2], [Bw2])

            wq = P.sb("wq", [128, KC, 256], BF16)
            wqr = P.sb("wqr", [128, KC, 256], BF16)
            Bwq = Buf()
            Bwqr = Buf()
            P.memset(wqr[:], 0.0, [Bwqr], eng="dve")
            wdup = {}
            Bwdup = {}
            for nm in ("kc", "vc", "ks", "kw"):
                wdup[nm] = P.sb("wd_" + nm, [128, KC, 128], BF16)
                Bwdup[nm] = Buf()
            wrot = {}
            Bwrot = {}
            for nm in ("ks", "kw"):
                wrot[nm] = P.sb("wr_" + nm, [128, KC, 128], BF16)
                Bwrot[nm] = Buf()
                P.memset(wrot[nm][:], 0.0, [Bwrot[nm]], eng="dve")
            wv = P.sb("wv", [128, KC, 128], BF16)
            Bwv = Buf()
            qpad = P.sb("qpad", [128, 4, S], BF16)
            BqT = [[Buf() for _ in range(4)] for _ in range(4)]
            for hl_ in range(4):
                P.memset(qpad[:, hl_, :], 0.0, BqT[hl_], eng="dve")
            kT = {}
            BkT = {}
            for nm in ("ks", "kw"):
                kT[nm] = P.sb("kT_" + nm, [128, S], BF16)
                BkT[nm] = [Buf() for _ in range(4)]
            KK = {}
            BKK = {}
            for nm in ("kc", "vc"):
                KK[nm] = P.sb("KK_" + nm, [128, S], BF16)
                BKK[nm] = [Buf() for _ in range(4)]
                P.memset(KK[nm][:], 0.0, BKK[nm], eng="dve")
            Vs = P.sb("Vs", [128, NT, 65], BF16)
            Vw = P.sb("Vw", [128, NT, 65], BF16)
            BVs = [Buf() for _ in range(NT)]
            BVw = [Buf() for _ in range(NT)]
            P.memset(Vs[:], 1.0, BVs, eng="dve")
            P.memset(Vw[:], 1.0, BVw, eng="dve")
            HT = P.sb("HT", [128, 128], BF16)
            BHT = Buf()
            kcT = P.sb("kcT", [128, 128], BF16)
            BkcT = Buf()
            P.memset(kcT[:], 0.0, [BkcT], eng="dve")
            VC = P.sb("VC", [128, 97], BF16)
            BVC = Buf()
            P.memset(VC[:], 0.0, [BVC], eng="dve")
            P.memset(VC[:, 64:65], 1.0, [BVC], eng="dve")
            P.cp(VC[:, 65:97], cst[:, C_OV:C_OV + 32], [Bcst], [BVC])
            zer = P.sb("zer", [128, 128], BF16)
            Bzer = Buf()
            P.memset(zer[:], 0.0, [Bzer], eng="dve")
            oacc = P.sb("oacc", [128, NT, 256], F32)
            Boacc = [Buf() for _ in range(NT)]
            selbT = P.sb("selbT", [128, S], BF16)
            BselbT = [Buf() for _ in range(4)]
            P.memset(selbT[:], 0.0, BselbT, eng="dve")
            pT_ring = P.sb_ring("pT", [128, 512], BF16, 6)
            tmp_ring = P.sb_ring("rt", [128, 512], F32, 4)
            small = P.sb_ring("sm", [128, 8], F32, 4)
            imp_ring = P.sb_ring("imp", [128, 4, 32], F32, 2)
            imp2_ring = P.sb_ring("imp2", [128, 40], F32, 2)
            selb_ring = P.sb_ring("selb", [128, 32], BF16, 3)
            otmp_ring = P.sb_ring("otmp", [128, 4, 64], F32, 3)
            obf_ring = P.sb_ring("obf", [128, 256], BF16, 2)
            impall = P.sb("impall", [128, NT, 128], F32)
            Bimp = [Buf() for _ in range(NT)]
            ons_ring = P.sb_ring("ons", [128, 2, 128], BF16, 2)

            def make_rot(dst, Bd, srct, Bs):
                dv = dst[:].rearrange("p k (r d) -> p (k r) d", d=64)
                sv = srct[:].rearrange("p k (r d) -> p (k r) d", d=64)
                P.ts(dv[:, :, 0:8], sv[:, :, 8:16], -1.0, None, ALU.mult, None, [Bs], [Bd])
                P.cp(dv[:, :, 8:16], sv[:, :, 0:8], [Bs], [Bd])

            def rope_evac(ps1, B1, ps2, B2, c0, c1, out_ap, Bouts, cstep=None, split=None):
                t1, Bt1 = tmp_ring.next()
                t2, Bt2 = tmp_ring.next()
                n = (out_ap if split is None else split[0][0]).shape[-1]
                if cstep is None:
                    ca, sa = Ct[:, c0:c1], St[:, c0:c1]
                else:
                    ca, sa = Ct[:, c0:c1:cstep], St[:, c0:c1:cstep]
                P.tt(t1[:, 0:n], ps1, ca, ALU.mult, [B1, Btab], [Bt1])
                P.tt(t2[:, 0:n], ps2, sa, ALU.mult, [B2, Btab], [Bt2])
                if split is None:
                    P.tt(out_ap, t1[:, 0:n], t2[:, 0:n], ALU.add, [Bt1, Bt2], Bouts, eng="pool")
                else:
                    for (oap, Bo_), (r0, r1) in zip(split, ((0, 64), (64, 128))):
                        P.tt(oap, t1[r0:r1, 0:n], t2[r0:r1, 0:n], ALU.add, [Bt1, Bt2], [Bo_], eng="dve")

            for g in range(NKV):
                P.load(wq[:], win_v[:, :, OFF_Q + 256 * g:OFF_Q + 256 * g + 256], (), [Bwq], eng="pool")
                make_rot(wqr, Bwqr, wq, Bwq)
                for nm, off in (("kc", OFF_KC), ("vc", OFF_VC), ("ks", OFF_KS), ("kw", OFF_KW)):
                    for r in range(2):
                        P.load(wdup[nm][:, :, r * 64:(r + 1) * 64], win_v[:, :, off + 64 * g:off + 64 * g + 64], (), [Bwdup[nm]], eng="pool")
                for nm in ("ks", "kw"):
                    make_rot(wrot[nm], Bwrot[nm], wdup[nm], Bwdup[nm])
                P.load(wv[:, :, 0:64], win_v[:, :, OFF_VS + 64 * g:OFF_VS + 64 * g + 64], (), [Bwv], eng="pool")
                P.load(wv[:, :, 64:128], win_v[:, :, OFF_VW + 64 * g:OFF_VW + 64 * g + 64], (), [Bwv], eng="pool")
                for tb in range(4):
                    tok = slice(tb * 512, (tb + 1) * 512)
                    for cc in range(2):
                        ps1, B1 = C.psA.next()
                        ps2, B2 = C.psA.next()
                        for k in range(KC):
                            P.mm(ps1[:], wq[:, k, cc * 128:(cc + 1) * 128], xnT[:, k, tok], k == 0, k == KC - 1, xr(tb) + [Bwq], [B1])
                        for k in range(KC):
                            P.mm(ps2[:], wqr[:, k, cc * 128:(cc + 1) * 128], xnT[:, k, tok], k == 0, k == KC - 1, xr(tb) + [Bwqr], [B2])
                        rope_evac(ps1[:], B1, ps2[:], B2, tb * 512, (tb + 1) * 512, None, None,
                                  split=[(qpad[0:64, 2 * cc, tok], BqT[2 * cc][tb]), (qpad[64:128, 2 * cc + 1, tok], BqT[2 * cc + 1][tb])])
                    for nm in ("ks", "kw"):
                        ps1, B1 = C.psA.next()
                        ps2, B2 = C.psA.next()
                        for k in range(KC):
                            P.mm(ps1[:], wdup[nm][:, k, :], xnT[:, k, tok], k == 0, k == KC - 1, xr(tb) + [Bwdup[nm]], [B1])
                        for k in range(KC):
                            P.mm(ps2[:], wrot[nm][:, k, :], xnT[:, k, tok], k == 0, k == KC - 1, xr(tb) + [Bwrot[nm]], [B2])
                        rope_evac(ps1[:], B1, ps2[:], B2, tb * 512, (tb + 1) * 512, kT[nm][:, tok], [BkT[nm][tb]])
                    for nm in ("kc", "vc"):
                        ps1, B1 = C.psA.next()
                        for k in range(KC):
                            P.mm(ps1[:], wdup[nm][:, k, :], xnT[:, k, tok], k == 0, k == KC - 1, xr(tb) + [Bwdup[nm]], [B1])
                        P.act(KK[nm][0:64, tok], ps1[0:64, :], AF.Copy, [B1], [BKK[nm][tb]])
                        if tb == 0:
                            P.cp(KK[nm][64:128, 0:511], ps1[64:128, 1:512], [B1], [BKK[nm][0]])
                        else:
                            P.cp(KK[nm][64:128, tb * 512 - 1:tb * 512 + 511], ps1[64:128, :], [B1], [BKK[nm][tb], BKK[nm][tb - 1]])
                for t in range(NT):
                    ps, Bps = C.psB.next()
                    for k in range(KC):
                        P.mm(ps[:, 0:128], xnT[:, k, t * 128:(t + 1) * 128], wv[:, k, :], k == 0, k == KC - 1, [BxnT[t], Bwv], [Bps])
                    P.cp(Vs[:, t, 0:64], ps[:, 0:64], [Bps], [BVs[t]])
                    P.act(Vw[:, t, 0:64], ps[:, 64:128], AF.Copy, [Bps], [BVw[t]])
                for kv, nm in (("k", "kc"), ("v", "vc")):
                    psz, Bz = C.psA.next()
                    for m in range(16):
                        P.mm(psz[:, 0:127], W1[kv][:, m, :], KK[nm][:, 2 * m:2 * m + 2017:16], m == 0, m == 15, BKK[nm] + [BW1], [Bz])
                    P.act(HT[:, 0:127], psz[:, 0:127], AF.Gelu_apprx_tanh, [Bz, Bcb], [BHT], bias=cbias[kv][:, 0:1])
                    if kv == "k":
                        ps1, B1 = C.psA.next()
                        ps2, B2 = C.psA.next()
                        P.mm(ps1[:, 0:127], w2k[:, :], HT[:, 0:127], True, True, [Bw2, BHT], [B1])
                        P.mm(ps2[:, 0:127], w2kr[:, :], HT[:, 0:127], True, True, [Bw2, BHT], [B2])
                        rope_evac(ps1[:, 0:127], B1, ps2[:, 0:127], B2, 31, 2048, kcT[:, 0:127], [BkcT], cstep=16)
                    else:
                        ps, Bps = C.psB.next()
                        P.mm(ps[0:127, 0:64], HT[:, 0:127], w2v[:, :], True, True, [BHT, Bw2], [Bps])
                        P.cp(VC[0:127, 0:64], ps[0:127, 0:64], [Bps], [BVC])
                for tb in range(4):
                    tok = slice(tb * 512, (tb + 1) * 512)
                    PT = []
                    for hl in range(4):
                        cc, base = hl // 2, 64 * (hl % 2)
                        ps, Bps = C.psA.next()
                        P.mm(ps[:], kcT[:, :], qpad[:, hl, tok], True, False, [BkcT, BqT[hl][tb]], [Bps])
                        P.mm(ps[:], ident[:], mk[:, 8 + tb, :], False, True, [Bid, Bmk], [Bps])
                        pt, Bpt = pT_ring.next()
                        P.act(pt[:], ps[:], AF.Exp, [Bps], [Bpt], scale=0.125)
                        PT.append((pt, Bpt))
                    for qt in range(4):
                        t = 4 * tb + qt
                        pso, Bo = C.psB.next()
                        for hl in range(4):
                            P.mm(pso[:, hl * 97:(hl + 1) * 97], PT[hl][0][:, qt * 128:(qt + 1) * 128], VC[:, :], True, True,
                                 [PT[hl][1], BVC], [Bo], inc=(hl == 3))
                        pv = pso[:, 0:388].rearrange("p (h c) -> p h c", c=97)
                        sm, Bsm = small.next()
                        P.ts(sm[:, 0:4].unsqueeze(2), pv[:, :, 64:65], 1e-30, None, ALU.max, None, [Bo], [Bsm])
                        P.recip(sm[:, 0:4], sm[:, 0:4], [Bsm], [Bsm])
                        P.tt(sm[:, 4:8], sm[:, 0:4], gs[:, t, 4 * g:4 * g + 4], ALU.mult, [Bsm, Bgs[t]], [Bsm])
                        P.tt(oacc[:, t, :].rearrange("p (h d) -> p h d", d=64), pv[:, :, 0:64],
                             sm[:, 4:8].unsqueeze(2).broadcast_to([128, 4, 64]), ALU.mult, [Bo, Bsm], [Boacc[t]])
                        if C.dbg is not None:
                            P.load(C.dbg[0, t * 128:(t + 1) * 128, 256 * g:256 * g + 256], oacc[:, t, :], [Boacc[t]], [C.Bdbg])
                        P.tt(impall[:, t, :].rearrange("p (h j) -> p h j", j=32), pv[:, :, 65:97],
                             sm[:, 0:4].unsqueeze(2).broadcast_to([128, 4, 32]), ALU.mult, [Bo, Bsm], [Bimp[t]])

                def chain_A(t):
                    i2, Bi2 = imp2_ring.next()
                    P.op("dve", lambda e, o=i2[:, 0:32], i=impall[:, t, :].rearrange("p (h j) -> p j h", j=32): e.tensor_reduce(out=o, in_=i, axis=mybir.AxisListType.X, op=ALU.add),
                         [Bimp[t]], [Bi2])
                    a0 = C_A + 32 - 2 * t
                    b0 = C_B + 32 - 2 * t
                    P.tt(i2[:, 0:32], i2[:, 0:32], cst[:, a0:a0 + 32], ALU.mult, [Bi2, Bcst], [Bi2])
                    P.tt(i2[:, 0:32], i2[:, 0:32], cst[:, b0:b0 + 32], ALU.add, [Bi2, Bcst], [Bi2])
                    P.cp(i2[:, 0:1], cst[:, C_FORCE:C_FORCE + 1], [Bi2, Bcst], [Bi2])
                    P.op("dve", lambda e, o=i2[:, 32:40], i=i2[:, 0:32]: e.max(out=o, in_=i), [Bi2], [Bi2])
                    sb_, Bsb = selb_ring.next()
                    P.ts(sb_[:], i2[:, 0:32], i2[:, 39:40], NEG, ALU.is_lt, ALU.mult, [Bi2], [Bsb])
                    pendB[t] = (sb_, Bsb, t, t // 4)

                pendB = {}
                slots = [[("A", 0)]] + [[("A", u), ("B", u - 1)] for u in range(1, NT)] + [[("B", NT - 1)]]

                def run_slot(sl):
                    for kind, u in sl:
                        if kind == "A":
                            chain_A(u)
                        else:
                            deferred_B.append(pendB.pop(u))
                            flush_B()

                def flush_B():
                    if deferred_B:
                        sb_, Bsb, t, tb = deferred_B.pop(0)
                        ptr, Bptr = C.pst.next()
                        P.tr(ptr[0:32, 0:128], sb_[:, :], ident[:], [Bsb, Bid], [Bptr])
                        P.cp(selbT[0:32, t * 128:(t + 1) * 128], ptr[0:32, 0:128], [Bptr], [BselbT[tb]])

                deferred_B = []

                def flush_one():
                    if slots:
                        run_slot(slots.pop(0))

                deferred = slots
                items = []
                for branch in (2, 1):
                    for hl in range(4):
                        for tb in range(4):
                            j0 = 0 if branch == 1 else max(0, 4 * tb - 4)
                            for jt in range(j0, 4 * tb + 4):
                                items.append((hl, branch, tb, jt, jt == j0, jt == 4 * tb + 3))
                n_win = sum(1 for it_ in items if it_[1] == 2)
                state = {}

                def emit_scores(it):
                    hl, branch, tb, jt, first, last = it
                    cc, base = hl // 2, 64 * (hl % 2)
                    d = jt - 4 * tb
                    c0, c1 = 0, 512
                    if TRIM:
                        if d >= 0:
                            c0, c1 = 128 * d, 512
                        elif branch == 2:
                            c0, c1 = 0, 128 * (d + 4 + 1)
                    kt, Bkt = (kT["ks"], BkT["ks"]) if branch == 1 else (kT["kw"], BkT["kw"])
                    ps, Bps = C.psA.next()
                    q0 = tb * 512
                    P.mm(ps[:, c0:c1], kt[:, jt * 128:(jt + 1) * 128], qpad[:, hl, q0 + c0:q0 + c1], True, False,
                         [Bkt[jt // 4], BqT[hl][tb]], [Bps])
                    if branch == 1:
                        P.mm(ps[:, c0:c1], em[:, jt, :], selbT[:, q0 + c0:q0 + c1], False, d < 0, [Bem, BselbT[tb]], [Bps])
                        if d >= 0:
                            P.mm(ps[:, c0:c1], ident[:], mk[:, d, c0:c1], False, True, [Bid, Bmk], [Bps])
                    else:
                        mi = d if d >= 0 else (4 + d + 4)
                        P.mm(ps[:, c0:c1], ident[:], mk[:, mi, c0:c1], False, True, [Bid, Bmk], [Bps])
                    state[it] = (ps, Bps, c0, c1)

                def emit_rest(it):
                    hl, branch, tb, jt, first, last = it
                    h = 4 * g + hl
                    ps, Bps, c0, c1 = state.pop(it)
                    Vt, BVt = (Vs, BVs) if branch == 1 else (Vw, BVw)
                    if first:
                        pso, Bo = C.psB.next()
                        state[(hl, branch, tb)] = (pso, Bo)
                        P.mm(pso[:, 0:260], zer[:, :], mk[:, 0, 0:260], True, False, [Bzer, Bmk], [Bo], inc=False)
                    pso, Bo = state[(hl, branch, tb)]
                    pt, Bpt = pT_ring.next()
                    P.act(pt[:, c0:c1], ps[:, c0:c1], AF.Exp, [Bps], [Bpt], scale=0.125)
                    for qt in range(4):
                        T = 4 * tb + qt
                        lo = 0 if branch == 1 else max(0, T - 4)
                        if lo <= jt <= T:
                            assert c0 <= qt * 128 and (qt + 1) * 128 <= c1
                            P.mm(pso[:, qt * 65:(qt + 1) * 65], pt[:, qt * 128:(qt + 1) * 128], Vt[:, jt, :],
                                 False, (jt == 4 * tb + 3 and qt == 3), [Bpt, BVt[jt]], [Bo])
                    if last:
                        del state[(hl, branch, tb)]
                        pv = pso[:, 0:260].rearrange("p (q c) -> p q c", c=65)
                        sm, Bsm = small.next()
                        P.ts(sm[:, 0:4].unsqueeze(2), pv[:, :, 64:65], 1e-30, None, ALU.max, None, [Bo], [Bsm])
                        P.recip(sm[:, 0:4], sm[:, 0:4], [Bsm], [Bsm])
                        gcol = 16 * branch + h
                        P.tt(sm[:, 4:8].unsqueeze(2), sm[:, 0:4].unsqueeze(2), gs[:, 4 * tb:4 * tb + 4, gcol:gcol + 1], ALU.mult,
                             [Bsm] + [Bgs[4 * tb + i] for i in range(4)], [Bsm])
                        ot, Bot = otmp_ring.next()
                        P.tt(ot[:], pv[:, :, 0:64], sm[:, 4:8].unsqueeze(2).broadcast_to([128, 4, 64]), ALU.mult, [Bo, Bsm], [Bot])
                        ov = oacc[:, 4 * tb:4 * tb + 4, hl * 64:(hl + 1) * 64]
                        Bov = [Boacc[4 * tb + i] for i in range(4)]
                        P.tt(ov, ov, ot[:], ALU.add, Bov + [Bot], Bov, eng="pool")

                if PIPE:
                    for j in range(min(PIPE_DEPTH, len(items))):
                        emit_scores(items[j])
                    for i, it in enumerate(items):
                        if i + PIPE_DEPTH < len(items):
                            if i + PIPE_DEPTH >= n_win:
                                while deferred:
                                    flush_one()
                            emit_scores(items[i + PIPE_DEPTH])
                        emit_rest(it)
                        if i % 5 == 3:
                            flush_one()
                else:
                    while deferred:
                        flush_one()
                    for it in items:
                        emit_scores(it)
                        emit_rest(it)
                for t in range(NT):
                    ob, Bob = obf_ring.next()
                    P.act(ob[:], oacc[:, t, :], AF.Copy, [Boacc[t]], [Bob])
                    ptr, Bptr = C.pst.next()
                    for cc in range(2):
                        P.tr(ptr[:, cc * 128:(cc + 1) * 128], ob[:, cc * 128:(cc + 1) * 128], ident[:], [Bob, Bid], [Bptr], inc=(cc == 1))
                    ons, Bons = ons_ring.next()
                    P.cp(ons[:], ptr[:, 0:256].rearrange("p (c q) -> p c q", c=2), [Bptr], [Bons])
                    P.load(onsa_d[:, 2 * g:2 * g + 2, t * 128:(t + 1) * 128], ons[:], [Bons], [BonD[g][t]])

        o_poolT = P.sb("opoolT", [128, 4, S], BF16)
        BopT = [[Buf() for _ in range(4)] for _ in range(4)]
        with P.scope():
            wpin = P.sb("wpin", [128, KC, 512], BF16)
            Bwp = Buf()
            P.load(wpin[:], win_v[:, :, OFF_POOL:OFF_POOL + 512], (), [Bwp], eng="pool")
            pw = P.sb("pw", [128, 4, 128], BF16)
            Bpw = Buf()
            P.load(pw[:], W["pool_w"].rearrange("g c d -> c g d"), (), [Bpw], eng="pool")
            psc = P.sb("psc", [128, 4], F32)
            Bpsc = Buf()
            P.load_nc(psc[:], W["pool_scale"].rearrange("o (g p) -> p (o g)", p=128), (), [Bpsc])
            ub = [P.sb(f"ub{i}", [128, 16 + S], F32) for i in range(3)]
            Bub = [Buf() for _ in range(3)]
            for i in range(3):
                P.memset(ub[i][:, 0:16], 0.0, [Bub[i]], eng="dve")
            pl = P.sb("pl", [128, S], BF16)
            Bpl = Buf()
            ftmp = P.sb("ftmp", [128, 16], F32)
            Bft = Buf()
            for gi, w in enumerate((2, 4, 8, 16)):
                for tb in range(4):
                    ps, Bps = C.psA.next()
                    for k in range(KC):
                        P.mm(ps[:], wpin[:, k, gi * 128:(gi + 1) * 128], xnT[:, k, tb * 512:(tb + 1) * 512], k == 0, k == KC - 1, xr(tb) + [Bwp], [Bps])
                    P.act(ub[0][:, 16 + tb * 512:16 + (tb + 1) * 512], ps[:], AF.Copy, [Bps], [Bub[0]])
                cur = 0
                pp = [1, 2]
                step = 1
                for _ in range(gi + 1):
                    nxt = pp[0]
                    pp = pp[::-1]
                    P.tt(ub[nxt][:, 16:16 + S], ub[cur][:, 16:16 + S], ub[cur][:, 16 - step:16 - step + S], ALU.add, [Bub[cur]], [Bub[nxt]])
                    cur = nxt
                    step *= 2
                P.stt(pl[:], ub[cur][:, 16:16 + S], 1.0 / w, ub[0][:, 16:16 + S], ALU.mult, ALU.subtract, [Bub[cur], Bub[0]], [Bpl])
                P.tt(ftmp[:, 0:w - 1], ub[cur][:, 16:16 + w - 1], cst[:, C_RC:C_RC + w - 1], ALU.mult, [Bub[cur], Bcst], [Bft])
                P.tt(pl[:, 0:w - 1], ftmp[:, 0:w - 1], ub[0][:, 16:16 + w - 1], ALU.subtract, [Bft, Bub[0]], [Bpl])
                for tb in range(4):
                    ps, Bps = C.psA.next()
                    P.mm(ps[:], pw[:, gi, :], pl[:, tb * 512:(tb + 1) * 512], True, True, [Bpw, Bpl], [Bps])
                    P.act(o_poolT[:, gi, tb * 512:(tb + 1) * 512], ps[:], AF.Copy, [Bps, Bpsc], [BopT[gi][tb]], scale=psc[:, gi:gi + 1])

        with P.scope():
            o_nsaT = P.sb("onsaT", [128, KC, S], BF16)
            BonT = [Buf() for _ in range(4)]
            for tb_ in range(4):
                P.load(o_nsaT[:, :, tb_ * 512:(tb_ + 1) * 512], onsa_d[:, :, tb_ * 512:(tb_ + 1) * 512],
                       [BonD[g_][tb_ * 4 + i] for g_ in range(4) for i in range(4)], [BonT[tb_]])
            yT = P.sb("yT", [128, KC, S], BF16)
            ByT = [[Buf() for _ in range(4)] for _ in range(KC)]
            wab_v = W["w_attn_branch"].rearrange("(kc p) c -> p kc c", p=128)
            wpb_v = W["w_pool_branch"].rearrange("(kc p) c -> p kc c", p=128)
            with P.scope():
                wab_r = P.sb_ring("wab", [128, KC, 256], BF16, 2)
                wpb_r = P.sb_ring("wpb", [128, 4, 256], BF16, 2)
                wga_r = P.sb_ring("wga", [128, KC, 256], BF16, 2)
                wgp_r = P.sb_ring("wgp", [128, KC, 256], BF16, 2)
                sg_r = P.sb_ring("msg", [128, 512], F32, 4)
                t_r = P.sb_ring("mt", [128, 512], F32, 4)
                for dp in range(4):
                    wab, Bwab = wab_r.next()
                    wpb, Bwpb = wpb_r.next()
                    wga, Bwga = wga_r.next()
                    wgp, Bwgp = wgp_r.next()
                    cs = slice(dp * 256, (dp + 1) * 256)
                    P.load(wab[:], wab_v[:, :, cs], (), [Bwab], eng="pool")
                    P.load(wpb[:], wpb_v[:, :, cs], (), [Bwpb], eng="pool")
                    P.load(wga[:], win_v[:, :, OFF_MG + dp * 256:OFF_MG + (dp + 1) * 256], (), [Bwga], eng="pool")
                    P.load(wgp[:], win_v[:, :, OFF_MG + 1024 + dp * 256:OFF_MG + 1024 + (dp + 1) * 256], (), [Bwgp], eng="pool")
                    for dl in range(2):
                        dc = dp * 2 + dl
                        cl = slice(dl * 128, (dl + 1) * 128)
                        for tb in range(4):
                            tok = slice(tb * 512, (tb + 1) * 512)
                            pa, Bpa = C.psA.next()
                            pp_, Bpp = C.psA.next()
                            pga, Bpga = C.psA.next()
                            pgp, Bpgp = C.psA.next()
                            for k in range(KC):
                                P.mm(pa[:], wab[:, k, cl], o_nsaT[:, k, tok], k == 0, k == KC - 1,
                                     [Bwab, BonT[tb]], [Bpa])
                            for k in range(4):
                                P.mm(pp_[:], wpb[:, k, cl], o_poolT[:, k, tok], k == 0, k == 3, [Bwpb, BopT[k][tb]], [Bpp])
                            for k in range(KC):
                                P.mm(pga[:], wga[:, k, cl], xnT[:, k, tok], k == 0, k == KC - 1, xr(tb) + [Bwga], [Bpga])
                            for k in range(KC):
                                P.mm(pgp[:], wgp[:, k, cl], xnT[:, k, tok], k == 0, k == KC - 1, xr(tb) + [Bwgp], [Bpgp])
                            sa, Bsa = sg_r.next()
                            sp_, Bsp = sg_r.next()
                            P.act(sa[:], pga[:], AF.Sigmoid, [Bpga], [Bsa])
                            P.act(sp_[:], pgp[:], AF.Sigmoid, [Bpgp], [Bsp])
                            t1, Bt1 = t_r.next()
                            t2, Bt2 = t_r.next()
                            P.tt(t1[:], pa[:], sa[:], ALU.mult, [Bpa, Bsa], [Bt1])
                            P.tt(t2[:], pp_[:], sp_[:], ALU.mult, [Bpp, Bsp], [Bt2])
                            P.tt(yT[:, dc, tok], t1[:], t2[:], ALU.add, [Bt1, Bt2], [ByT[dc][tb]], eng="pool")
            with P.scope():
                wo = P.sb("wo", [128, KC, D], BF16)
                Bwo = Buf()
                P.load(wo[:], W["w_out"].rearrange("(kc p) c -> p kc c", p=128), (), [Bwo], eng="pool")
                gbc = P.sb("gbc3", [128, D], F32)
                Bg = Buf()
                P.load(gbc[:], W["g_mix_post"].partition_broadcast(128), (), [Bg])
                xt_ring = P.sb_ring("mxr", [128, D], F32, 2)
                st_ring = P.sb_ring("mst", [128, 8], F32, 3)
                tmp2_ring = P.sb_ring("mtmp", [128, 512], F32, 3)
                for t in range(NT):
                    xt, Bxt = xt_ring.next()
                    P.load(xt[:], src[t * 128:(t + 1) * 128, :], [Bsrc[t]], [Bxt])
                    pos_, Bpos = [], []
                    for dh in range(2):
                        po, Bpo = C.psA.next()
                        for k in range(KC):
                            P.mm(po[:], yT[:, k, t * 128:(t + 1) * 128], wo[:, k, dh * 512:(dh + 1) * 512], k == 0, k == KC - 1,
                                 [ByT[k][t // 4], Bwo], [Bpo])
                        pos_.append(po)
                        Bpos.append(Bpo)
                    post_norm_residual(P, C, pos_, Bpos, gbc, Bg, xt, Bxt, 1.0, dst[t * 128:(t + 1) * 128, :], Bdst[t], st_ring, tmp2_ring)


_NC_CACHE = {}


def kernel(**inputs):
    n = 8
    if "nc" not in _NC_CACHE:
        _NC_CACHE["nc"] = build_nc()[0]
    nc = _NC_CACHE["nc"]
    in_maps = []
    hc, hm, he = host_consts()
    for b in range(n):
        m = {"kconsts": hc, "kmasks": hm, "kemat": he}
        for k, v in inputs.items():
            v = np.asarray(v)
            if k == "x":
                m[k] = np.ascontiguousarray(v[b])
            elif k == "positions":
                m[k] = np.ascontiguousarray(v[b].reshape(1, S))
            else:
                a = v[0]
                if a.ndim == 1:
                    a = a.reshape(1, -1)
                m[k] = np.ascontiguousarray(a)
        in_maps.append(m)
    res = run_bass_kernel_spmd(nc, in_maps, core_ids=list(range(n)))
    return np.stack([np.asarray(r["out"]).reshape(S, D) for r in res.results], axis=0).astype(np.float32)
```

```python
import numpy as np
from contextlib import ExitStack, contextmanager
import concourse.bass as bass
import concourse.mybir as mybir
from concourse.bass_utils import run_bass_kernel_spmd

F32 = mybir.dt.float32
BF16 = mybir.dt.bfloat16
I32 = mybir.dt.int32
ALU = mybir.AluOpType
AF = mybir.ActivationFunctionType

ENGINES = ("pe", "act", "dve", "pool", "sp")
SEM_ROLL = 30000

D = 1024
S = 2048
DFF = 2816
NT = S // 128
NF = DFF // 128
KC = D // 128
NH = 16
HD = 64
NKV = 4
IN_TOTAL = 5168
EPS = 1e-6
NEG = -30000.0


class Buf:
    __slots__ = ("name", "w", "r")

    def __init__(self, name=""):
        self.name = name
        self.w = None
        self.r = []


class Ring:
    def __init__(self, items):
        self.items = items
        self.i = 0

    def next(self):
        it = self.items[self.i]
        self.i = (self.i + 1) % len(self.items)
        return it


class Prog:
    def __init__(self, nc, n_dma_sems=24):
        self.nc = nc
        self.es = ExitStack()
        self.scopes = [self.es]
        self.streams = {e: [] for e in ENGINES}
        self.sem = {}
        self.cnt = {}
        self.nsem = 0
        for e in ENGINES:
            self._new_engine_sem(e)
        self.known = {e: {} for e in ENGINES}
        self.dma_sems = [self.es.enter_context(nc.semaphore(f"dq{i}")) for i in range(n_dma_sems)]
        self.dma_cnt = [0] * n_dma_sems
        self.dma_rr = 0
        self.ninstr = 0
        self.uid = 0

    def _new_engine_sem(self, e):
        self.nsem += 1
        self.sem[e] = self.es.enter_context(self.nc.semaphore(f"s_{e}_{self.nsem}"))
        self.cnt[e] = 0

    @contextmanager
    def scope(self):
        es = ExitStack()
        self.scopes.append(es)
        try:
            yield
        finally:
            self.barrier()
            self.scopes.pop()
            es.close()

    def sb(self, name, shape, dt):
        self.uid += 1
        return self.scopes[-1].enter_context(self.nc.sbuf_tensor(f"{name}_{self.uid}", list(shape), dt))

    def ps(self, name, shape, dt):
        self.uid += 1
        return self.scopes[-1].enter_context(self.nc.psum_tensor(f"{name}_{self.uid}", list(shape), dt))

    def sb_ring(self, name, shape, dt, n):
        return Ring([(self.sb(f"{name}{i}", shape, dt), Buf(f"{name}{i}")) for i in range(n)])

    def _need(self, eng, tok, waits):
        if tok is None:
            return
        sem, val, teng = tok
        if teng == "pe" and eng == "pe":
            return
        sid = id(sem)
        if self.known[eng].get(sid, 0) >= val:
            return
        cur = waits.get(sid)
        if cur is None or cur[1] < val:
            waits[sid] = (sem, val)

    def _collect(self, eng, reads, writes):
        waits = {}
        for b in reads:
            self._need(eng, b.w, waits)
        for b in writes:
            self._need(eng, b.w, waits)
            for t in b.r:
                self._need(eng, t, waits)
        return waits

    def _emit_waits(self, eng, waits):
        st = self.streams[eng]
        for sid, (sem, val) in waits.items():
            st.append(lambda e, sem=sem, val=val: e.wait_ge(sem, val))
            self.known[eng][sid] = val

    def _commit(self, tok, reads, writes):
        for b in reads:
            b.r.append(tok)
            if len(b.r) > 48:
                best = {}
                for t in b.r:
                    k = id(t[0])
                    if k not in best or best[k][1] < t[1]:
                        best[k] = t
                b.r = list(best.values())
        for b in writes:
            b.w = tok
            b.r = []

    def op(self, eng, fn, reads=(), writes=(), inc=True):
        waits = self._collect(eng, reads, writes)
        self._emit_waits(eng, waits)
        self.ninstr += 1
        if inc:
            if self.cnt[eng] >= SEM_ROLL:
                self._new_engine_sem(eng)
            self.cnt[eng] += 1
            sem, val = self.sem[eng], self.cnt[eng]
            self.streams[eng].append(lambda e, fn=fn, sem=sem: fn(e).then_inc(sem, 1))
            tok = (sem, val, eng)
        else:
            self.streams[eng].append(lambda e, fn=fn: fn(e))
            tok = (self.sem[eng], self.cnt[eng] + 1, eng)
        self._commit(tok, reads, writes)
        return tok

    def dma(self, eng, fn, reads=(), writes=()):
        i = self.dma_rr
        self.dma_rr = (self.dma_rr + 1) % len(self.dma_sems)
        sem = self.dma_sems[i]
        waits = self._collect(eng, reads, writes)
        if self.dma_cnt[i] > 0:
            self._need(eng, (sem, self.dma_cnt[i], "dma"), waits)
        self._emit_waits(eng, waits)
        self.dma_cnt[i] += 16
        val = self.dma_cnt[i]
        self.streams[eng].append(lambda e, fn=fn, sem=sem: fn(e).then_inc(sem, 16))
        tok = (sem, val, "dma")
        self._commit(tok, reads, writes)
        self.ninstr += 1
        return tok

    def barrier(self):
        for eng in ENGINES:
            waits = {}
            for e2 in ENGINES:
                if e2 != eng and self.cnt[e2] > 0:
                    self._need(eng, (self.sem[e2], self.cnt[e2], e2), waits)
            if self.cnt[eng] > 0 and eng != "pe":
                self._need(eng, (self.sem[eng], self.cnt[eng], eng + "_self"), waits)
            for i, sem in enumerate(self.dma_sems):
                if self.dma_cnt[i] > 0:
                    self._need(eng, (sem, self.dma_cnt[i], "dma"), waits)
            self._emit_waits(eng, waits)

    def finish(self):
        nc = self.nc
        streams = self.streams
        self.barrier()
        with nc.Block() as block:
            @block.tensor
            def _(e):
                for f in streams["pe"]:
                    f(e)

            @block.scalar
            def _(e):
                for f in streams["act"]:
                    f(e)

            @block.vector
            def _(e):
                for f in streams["dve"]:
                    f(e)

            @block.gpsimd
            def _(e):
                for f in streams["pool"]:
                    f(e)

            @block.sync
            def _(e):
                for f in streams["sp"]:
                    f(e)
        self.es.close()

    def mm(self, out, lhsT, rhs, start, stop, reads, writes, inc=None):
        if inc is None:
            inc = stop
        return self.op("pe", lambda e: e.matmul(out, lhsT, rhs, start=start, stop=stop), reads, writes, inc=inc)

    def tr(self, out, in_, ident, reads, writes, inc=True):
        return self.op("pe", lambda e: e.transpose(out, in_, ident), reads, writes, inc=inc)

    def act(self, out, in_, func, reads, writes, **kw):
        return self.op("act", lambda e: e.activation(out=out, in_=in_, func=func, **kw), reads, writes)

    def tt(self, out, in0, in1, op, reads, writes, eng="dve"):
        return self.op(eng, lambda e: e.tensor_tensor(out=out, in0=in0, in1=in1, op=op), reads, writes)

    def ts(self, out, in0, s1, s2, op0, op1, reads, writes, eng="dve"):
        if op1 is None:
            return self.op(eng, lambda e: e.tensor_scalar(out=out, in0=in0, scalar1=s1, scalar2=None, op0=op0), reads, writes)
        return self.op(eng, lambda e: e.tensor_scalar(out=out, in0=in0, scalar1=s1, scalar2=s2, op0=op0, op1=op1), reads, writes)

    def stt(self, out, in0, scalar, in1, op0, op1, reads, writes):
        return self.op("dve", lambda e: e.scalar_tensor_tensor(out=out, in0=in0, scalar=scalar, in1=in1, op0=op0, op1=op1), reads, writes)

    def cp(self, out, in_, reads, writes, eng="dve"):
        return self.op(eng, lambda e: e.tensor_copy(out=out, in_=in_), reads, writes)

    def recip(self, out, in_, reads, writes):
        return self.op("dve", lambda e: e.reciprocal(out=out, in_=in_), reads, writes)

    def memset(self, out, val, writes, eng="pool"):
        return self.op(eng, lambda e: e.memset(out, val), (), writes)

    def load(self, out, in_, reads, writes, eng="sp"):
        return self.dma(eng, lambda e: e.dma_start(out=out, in_=in_), reads, writes)

    def load_nc(self, out, in_, reads, writes, eng="sp"):
        return self.dma(eng, lambda e: e.dma_start(out=out, in_=in_, allow_slow_non_contiguous=True), reads, writes)


class Ctx:
    pass


def norm_transpose_phase(P, C, src, Bsrc, g_dram, xnT, BxnT, tag):
    gbc = P.sb("gbc", [128, D], F32)
    Bg = Buf()
    P.load(gbc[:], g_dram.partition_broadcast(128), (), [Bg])
    xt_ring = P.sb_ring("xt", [128, D], F32, 2)
    xn_ring = P.sb_ring("xn", [128, D], BF16, 2)
    st_ring = P.sb_ring("st", [128, 4], F32, 3)
    for t in range(NT):
        xt, Bxt = xt_ring.next()
        xn, Bxn = xn_ring.next()
        st, Bst = st_ring.next()
        P.load(xt[:], src[t * 128:(t + 1) * 128, :], [Bsrc[t]], [Bxt])
        P.act(xn[:], xt[:], AF.Square, [Bxt], [Bxn, Bst], accum_out=st[:, 0:1])
        P.act(st[:, 1:2], st[:, 0:1], AF.Sqrt, [Bst], [Bst], scale=1.0 / D, bias=EPS)
        P.recip(st[:, 2:3], st[:, 1:2], [Bst], [Bst])
        P.stt(xn[:], xt[:], st[:, 2:3], gbc[:], ALU.mult, ALU.mult, [Bxt, Bst, Bg], [Bxn])
        pt, Bpt = C.pst.next()
        for k in range(KC):
            P.tr(pt[:, k * 128:(k + 1) * 128], xn[:, k * 128:(k + 1) * 128], C.ident[:], [Bxn, C.Bident], [Bpt], inc=(k == KC - 1))
        P.cp(xnT[:, :, t * 128:(t + 1) * 128], pt[:].rearrange("p (k c) -> p k c", k=KC), [Bpt], [BxnT[t]])


def post_norm_residual(P, C, po_list, Bpo_list, gbc, Bg, res_t, Bres, half_scale, out_dram, Bout, stt_ring, tmp_ring):
    st, Bst = stt_ring.next()
    junk, Bj = tmp_ring.next()
    P.act(junk[:], po_list[0][:], AF.Square, [Bpo_list[0]], [Bj, Bst], accum_out=st[:, 0:1])
    junk2, Bj2 = tmp_ring.next()
    P.act(junk2[:], po_list[1][:], AF.Square, [Bpo_list[1]], [Bj2, Bst], accum_out=st[:, 1:2])
    P.tt(st[:, 2:3], st[:, 0:1], st[:, 1:2], ALU.add, [Bst], [Bst])
    sc = 1.0 / (half_scale * half_scale)
    P.act(st[:, 3:4], st[:, 2:3], AF.Sqrt, [Bst], [Bst], scale=sc / D, bias=EPS * sc)
    P.recip(st[:, 4:5], st[:, 3:4], [Bst], [Bst])
    for hh in range(2):
        tmp, Bt = tmp_ring.next()
        P.stt(tmp[:], po_list[hh][:], st[:, 4:5], gbc[:, hh * 512:(hh + 1) * 512], ALU.mult, ALU.mult,
              [Bpo_list[hh], Bst, Bg], [Bt])
        P.tt(res_t[:, hh * 512:(hh + 1) * 512], res_t[:, hh * 512:(hh + 1) * 512], tmp[:], ALU.add, [Bres, Bt], [Bres])
    P.load(out_dram, res_t[:], [Bres], [Bout])


def ffn_block(P, C, src, Bsrc, dst, Bdst, g_pre, wg, wu, wd, g_post):
    FG = 2
    with P.scope():
        hT = P.sb("hT", [128, NF, S], BF16)
        BhT = [[Buf() for _ in range(4)] for _ in range(NF)]
        NA = 10
        WG = 2
        wd_v = wd.rearrange("(fc p) d -> p fc d", p=128)
        wdA = P.sb("wdA", [128, NA, D], BF16)
        Bwd = [Buf() for _ in range(NF // WG)]
        with P.scope():
            xnT = P.sb("xnT", [128, KC, S], BF16)
            BxnT = [Buf() for _ in range(NT)]
            norm_transpose_phase(P, C, src, Bsrc, g_pre, xnT, BxnT, "f")
            wg_ring = P.sb_ring("wg", [128, KC, FG * 128], BF16, 2)
            wu_ring = P.sb_ring("wu", [128, KC, FG * 128], BF16, 2)
            sg_ring = P.sb_ring("sg", [128, 512], F32, 2)
            wd_pending = list(range(NA // WG))
            wg_v = wg.rearrange("(kc p) f -> p kc f", p=128)
            wu_v = wu.rearrange("(kc p) f -> p kc f", p=128)
            for fg in range(NF // FG):
                wgt, Bwg = wg_ring.next()
                wut, Bwu = wu_ring.next()
                P.load(wgt[:], wg_v[:, :, fg * FG * 128:(fg + 1) * FG * 128], (), [Bwg], eng="pool")
                P.load(wut[:], wu_v[:, :, fg * FG * 128:(fg + 1) * FG * 128], (), [Bwu], eng="pool")
                if fg >= 2 and wd_pending:
                    i_ = wd_pending.pop(0)
                    P.load(wdA[:, i_ * WG:(i_ + 1) * WG, :], wd_v[:, i_ * WG:(i_ + 1) * WG, :], (), [Bwd[i_]], eng="pool")
                for tb in range(4):
                    for fc in range(FG):
                        f = fg * FG + fc
                        pg, Bpg = C.psA.next()
                        pu, Bpu = C.psA.next()
                        rd = [BxnT[tb * 4 + i] for i in range(4)]
                        for k in range(KC):
                            P.mm(pg[:], wgt[:, k, fc * 128:(fc + 1) * 128], xnT[:, k, tb * 512:(tb + 1) * 512],
                                 k == 0, k == KC - 1, rd + [Bwg], [Bpg])
                        for k in range(KC):
                            P.mm(pu[:], wut[:, k, fc * 128:(fc + 1) * 128], xnT[:, k, tb * 512:(tb + 1) * 512],
                                 k == 0, k == KC - 1, rd + [Bwu], [Bpu])
                        sg, Bsg = sg_ring.next()
                        P.act(sg[:], pg[:], AF.Silu, [Bpg], [Bsg])
                        P.tt(hT[:, f, tb * 512:(tb + 1) * 512], pu[:], sg[:], ALU.mult, [Bpu, Bsg], [BhT[f][tb]])
        with P.scope():
            wdB = P.sb("wdB", [128, NF - NA, D], BF16)
            for i in range(NA // WG, NF // WG):
                P.load(wdB[:, i * WG - NA:(i + 1) * WG - NA, :], wd_v[:, i * WG:(i + 1) * WG, :], (), [Bwd[i]], eng="pool")
            gbc = P.sb("gbc2", [128, D], F32)
            Bg = Buf()
            P.load(gbc[:], g_post.partition_broadcast(128), (), [Bg])
            xt_ring = P.sb_ring("xr", [128, D], F32, 2)
            st_ring = P.sb_ring("st2", [128, 8], F32, 3)
            tmp_ring = P.sb_ring("tmp", [128, 512], F32, 3)
            for t in range(NT):
                xt, Bxt = xt_ring.next()
                P.load(xt[:], src[t * 128:(t + 1) * 128, :], [Bsrc[t]], [Bxt])
                pos, Bpos = [], []
                for dh in range(2):
                    po, Bpo = C.psA.next()
                    for f in range(NF):
                        P.mm(po[:], hT[:, f, t * 128:(t + 1) * 128],
                             (wdA[:, f, dh * 512:(dh + 1) * 512] if f < NA else wdB[:, f - NA, dh * 512:(dh + 1) * 512]),
                             f == 0, f == NF - 1, [BhT[f][t // 4], Bwd[f // WG]], [Bpo])
                    pos.append(po)
                    Bpos.append(Bpo)
                post_norm_residual(P, C, pos, Bpos, gbc, Bg, xt, Bxt, 0.5, dst[t * 128:(t + 1) * 128, :], Bdst[t],
                                   st_ring, tmp_ring)


def setup_consts(P, C):
    C.ident = P.sb("ident", [128, 128], BF16)
    C.Bident = Buf()
    tmpi = P.sb("tmpi", [128, 128], F32)
    Bt = Buf()
    P.op("pool", lambda e: e.iota(tmpi[:], [[-1, 128]], base=0, channel_multiplier=1, allow_small_or_imprecise_dtypes=True), (), [Bt])
    P.op("dve", lambda e: e.tensor_single_scalar(out=C.ident[:], in_=tmpi[:], scalar=0.0, op=ALU.is_equal), [Bt], [C.Bident])
    C.psA = Ring([(P.ps(f"psA{i}", [128, 512], F32), Buf()) for i in range(4)])
    C.pst = Ring([(P.ps(f"pst{i}", [128, 1024], BF16), Buf()) for i in range(2)])
    C.psB = Ring([(P.ps(f"psB{i}", [128, 512], F32), Buf()) for i in range(2)])


STAGE = 3
DBG = False
PIPE = True
PIPE_DEPTH = 3
TRIM = True


def build_nc(stage=STAGE):
    nc = bass.Bass("TRN2", target_bir_lowering=False)

    def din(name, shape, dt=F32):
        return nc.dram_tensor(name, list(shape), dt, kind="ExternalInput").ap()

    x = din("x", [S, D])
    pos = din("positions", [1, S], I32)
    W = {}
    for n, shp in (("g_ffn1_pre", [1, D]), ("w_ffn1_gate", [D, DFF]), ("w_ffn1_up", [D, DFF]), ("w_ffn1_down", [DFF, D]),
                   ("g_ffn1_post", [1, D]), ("g_mix_pre", [1, D]), ("w_in", [D, IN_TOTAL]),
                   ("cmp_pe_k", [32, 64]), ("cmp_w1_k", [2048, 128]), ("cmp_w2_k", [128, 64]),
                   ("cmp_pe_v", [32, 64]), ("cmp_w1_v", [2048, 128]), ("cmp_w2_v", [128, 64]),
                   ("w_attn_branch", [D, D]), ("pool_w", [4, 128, 128]), ("pool_scale", [1, 512]),
                   ("w_pool_branch", [512, D]), ("w_out", [D, D]), ("g_mix_post", [1, D]),
                   ("g_ffn2_pre", [1, D]), ("w_ffn2_gate", [D, DFF]), ("w_ffn2_up", [D, DFF]), ("w_ffn2_down", [DFF, D]),
                   ("g_ffn2_post", [1, D])):
        W[n] = din(n, shp)
    consts = din("kconsts", [128, C_N])
    masks = din("kmasks", [128, 12, 512])
    emat = din("kemat", [128, 16, 128])
    out = nc.dram_tensor("out", [S, D], F32, kind="ExternalOutput").ap()
    x1 = nc.dram_tensor("x1_scratch", [S, D], F32).ap()
    x2 = nc.dram_tensor("x2_scratch", [S, D], F32).ap()
    onsa_dram = nc.dram_tensor("onsa_scratch", [128, KC, S], BF16).ap()

    P = Prog(nc)
    C = Ctx()
    C.dbg = None
    if DBG:
        C.dbg = nc.dram_tensor("dbg", [3, S, D], F32, kind="ExternalOutput").ap()
        C.Bdbg = Buf()
    setup_consts(P, C)
    C.onsa_d = onsa_dram
    Bx = [Buf() for _ in range(NT)]
    Bx1 = [Buf() for _ in range(NT)]
    Bx2 = [Buf() for _ in range(NT)]
    Bout = [Buf() for _ in range(NT)]
    if stage == 1:
        ffn_block(P, C, x, Bx, out, Bout, W["g_ffn1_pre"], W["w_ffn1_gate"], W["w_ffn1_up"], W["w_ffn1_down"], W["g_ffn1_post"])
    else:
        ffn_block(P, C, x, Bx, x1, Bx1, W["g_ffn1_pre"], W["w_ffn1_gate"], W["w_ffn1_up"], W["w_ffn1_down"], W["g_ffn1_post"])
        mixer_block(P, C, x1, Bx1, (out if stage == 2 else x2), (Bout if stage == 2 else Bx2), pos, W, consts, masks, emat)
        if stage >= 3:
            ffn_block(P, C, x2, Bx2, out, Bout, W["g_ffn2_pre"], W["w_ffn2_gate"], W["w_ffn2_up"], W["w_ffn2_down"], W["g_ffn2_post"])
    P.finish()
    return nc, P


OFF_Q, OFF_KC, OFF_VC, OFF_KS, OFF_VS, OFF_KW, OFF_VW, OFF_G, OFF_POOL, OFF_MG = 0, 1024, 1280, 1536, 1792, 2048, 2304, 2560, 2608, 3120
C_INV, C_RC, C_A, C_B, C_OV, C_FORCE, C_N = 0, 1, 17, 81, 145, 177, 178
PI = float(np.pi)
PI_SAFE = 3.1415


def host_consts():
    c = np.zeros((128, C_N), np.float32)
    p = np.arange(128)
    d = p % 64
    inv = 500000.0 ** (-(np.arange(8, dtype=np.float32)) * (2.0 / 16.0))
    c[:, C_INV] = np.where(d < 16, inv[d % 8], 0.0)
    c[:, C_RC:C_RC + 16] = 1.0 / (np.arange(16, dtype=np.float32) + 1.0)
    cur = (p // 64)[:, None]
    rel = (np.arange(64) - 32)[None, :]
    c[:, C_A:C_A + 64] = (rel < cur).astype(np.float32)
    c[:, C_B:C_B + 64] = np.where(rel == cur, 1e4, np.where(rel > cur, -1e30, 0.0))
    n = np.arange(128)[:, None]
    j = np.arange(32)[None, :]
    ov = np.minimum(16 * n + 32, 64 * j + 64) - np.maximum(16 * n, 64 * j)
    ov = np.clip(ov, 0, None).astype(np.float32) / 32.0
    ov[127, :] = 0.0
    c[:, C_OV:C_OV + 32] = ov
    c[:, C_FORCE] = 1e4
    masks = np.zeros((128, 12, 512), np.float32)
    s_ = np.arange(128)[:, None]
    tl = np.arange(512)[None, :]
    for jj in range(4):
        masks[:, jj, :] = np.where(128 * jj + s_ <= tl, 0.0, NEG)
        masks[:, 4 + jj, :] = np.where(128 * jj + s_ > tl, 0.0, NEG)
        masks[:, 8 + jj, :] = np.where(16 * s_ + 31 <= 512 * jj + tl, 0.0, NEG)
    em = np.zeros((128, 16, 128), np.float32)
    for jt in range(16):
        for s in range(128):
            em[2 * jt + s // 64, jt, s] = 1.0
    return c, masks, em


def mixer_block(P, C, src, Bsrc, dst, Bdst, pos, W, consts, masks, emat):
    win_v = W["w_in"].rearrange("(kc p) c -> p kc c", p=128)
    ident = C.ident
    Bid = C.Bident
    with P.scope():
        xnT = P.sb("mxnT", [128, KC, S], BF16)
        BxnT = [Buf() for _ in range(NT)]
        with P.scope():
            norm_transpose_phase(P, C, src, Bsrc, W["g_mix_pre"], xnT, BxnT, "m")
        onsa_d = C.onsa_d
        BonD = [[Buf() for _ in range(NT)] for _ in range(4)]
        cst = P.sb("cst", [128, C_N], F32)
        Bcst = Buf()
        P.load(cst[:], consts[:, :], (), [Bcst])

        def xr(tb):
            return [BxnT[tb * 4 + i] for i in range(4)]

        with P.scope():
            gs = P.sb("gs", [128, NT, 48], F32)
            Bgs = [Buf() for _ in range(NT)]
            mk = P.sb("mk", [128, 12, 512], BF16)
            Bmk = Buf()
            P.load(mk[:], masks[:, :, :], (), [Bmk], eng="pool")
            em = P.sb("em", [128, 16, 128], BF16)
            Bem = Buf()
            P.load(em[:], emat[:, :, :], (), [Bem], eng="pool")
            Ct = P.sb("Ct", [128, S], F32)
            St = P.sb("St", [128, S], F32)
            Btab = Buf()
            with P.scope():
                posi = P.sb("posi", [128, S], I32)
                Bp = Buf()
                P.load(posi[:], pos.partition_broadcast(128), (), [Bp])
                ang = P.sb("ang", [128, S], F32)
                Ba = Buf()
                P.cp(ang[:], posi[:], [Bp], [Ba])
                P.ts(ang[:], ang[:], cst[:, C_INV:C_INV + 1], None, ALU.mult, None, [Ba, Bcst], [Ba])
                kf = P.sb("kf", [128, S], F32)
                Bk = Buf()
                for tab, shift, bias in ((St, 0.0, 0.0), (Ct, 0.25, PI / 2)):
                    P.ts(posi[:], ang[:], 1.0 / (2 * PI), shift, ALU.mult, ALU.add, [Ba], [Bp])
                    P.cp(kf[:], posi[:], [Bp], [Bk])
                    P.stt(kf[:], kf[:], -2 * PI, ang[:], ALU.mult, ALU.add, [Bk, Ba], [Bk])
                    P.ts(kf[:], kf[:], PI_SAFE - bias, -PI_SAFE - bias, ALU.min, ALU.max, [Bk], [Bk])
                    if bias == 0.0:
                        P.act(tab[:], kf[:], AF.Sin, [Bk], [Btab])
                    else:
                        hp = P.sb("hp", [128, 1], F32)
                        Bhp = Buf()
                        P.memset(hp[:], bias, [Bhp])
                        P.act(tab[:], kf[:], AF.Sin, [Bk, Bhp], [Btab], bias=hp[:, 0:1])
            wgate = P.sb("wgate", [128, KC, 48], BF16)
            Bwgt = Buf()
            P.load(wgate[:], win_v[:, :, OFF_G:OFF_G + 48], (), [Bwgt], eng="pool")
            for t in range(NT):
                ps, Bps = C.psB.next()
                for k in range(KC):
                    P.mm(ps[:, 0:48], xnT[:, k, t * 128:(t + 1) * 128], wgate[:, k, :], k == 0, k == KC - 1, [BxnT[t], Bwgt], [Bps])
                P.act(gs[:, t, :], ps[:, 0:48], AF.Sigmoid, [Bps], [Bgs[t]])
            W1 = {}
            BW1 = Buf()
            cbias = {}
            Bcb = Buf()
            pe16 = P.sb("pe16", [16, 128], F32)
            pe16b = P.sb("pe16b", [16, 128], BF16)
            peb = P.sb("peb", [128, 16], BF16)
            Bpe = Buf()
            for kv in ("k", "v"):
                W1[kv] = P.sb("W1" + kv, [128, 16, 128], BF16)
                P.load(W1[kv][:], W[f"cmp_w1_{kv}"].rearrange("(m p) h -> p m h", p=128), (), [BW1], eng="pool")
                P.load(pe16[:], W[f"cmp_pe_{kv}"].rearrange("(m lp) d -> m (lp d)", lp=2), (), [Bpe])
                P.cp(pe16b[:], pe16[:], [Bpe], [Bpe])
                ptr, Bptr = C.pst.next()
                P.tr(ptr[:, 0:16], pe16b[:, :], ident[0:16, 0:16], [Bpe, Bid], [Bptr])
                P.cp(peb[:], ptr[:, 0:16], [Bptr], [Bpe])
                ps, Bps = C.psB.next()
                for m in range(16):
                    P.mm(ps[:, 0:1], W1[kv][:, m, :], peb[:, m:m + 1], m == 0, m == 15, [BW1, Bpe], [Bps])
                cbias[kv] = P.sb("cb" + kv, [128, 1], F32)
                P.cp(cbias[kv][:], ps[:, 0:1], [Bps], [Bcb])
            w2k = P.sb("w2k", [128, 128], BF16)
            w2kr = P.sb("w2kr", [128, 128], BF16)
            w2v = P.sb("w2v", [128, 64], BF16)
            Bw2 = Buf()
            P.load(w2k[:, 0:64], W["cmp_w2_k"][:, :], (), [Bw2], eng="pool")
            P.load(w2k[:, 64:128], W["cmp_w2_k"][:, :], (), [Bw2], eng="pool")
            P.load(w2v[:], W["cmp_w2_v"][:, :], (), [Bw2], eng="pool")
            P.memset(w2kr[:], 0.0, [Bw2], eng="dve")
            w2k3 = w2k[:].rearrange("p (r d) -> p r d", d=64)
            w2kr3 = w2kr[:].rearrange("p (r d) -> p r d", d=64)
            P.ts(w2kr3[:, :, 0:8], w2k3[:, :, 8:16], -1.0, None, ALU.mult, None, [Bw2], [Bw2])
            P.cp(w2kr3[:, :, 8:16], w2k3[:, :, 0:8], [Bw2], [Bw2])

            wq = P.sb("wq", [128, KC, 256], BF16)
            wqr = P.sb("wqr", [128, KC, 256], BF16)
            Bwq = Buf()
            Bwqr = Buf()
            P.memset(wqr[:], 0.0, [Bwqr], eng="dve")
            wdup = {}
            Bwdup = {}
            for nm in ("kc", "vc", "ks", "kw"):
                wdup[nm] = P.sb("wd_" + nm, [128, KC, 128], BF16)
                Bwdup[nm] = Buf()
            wrot = {}
            Bwrot = {}
            for nm in ("ks", "kw"):
                wrot[nm] = P.sb("wr_" + nm, [128, KC, 128], BF16)
                Bwrot[nm] = Buf()
                P.memset(wrot[nm][:], 0.0, [Bwrot[nm]], eng="dve")
            wv = P.sb("wv", [128, KC, 128], BF16)
            Bwv = Buf()
            qpad = P.sb("qpad", [128, 4, S], BF16)
            BqT = [[Buf() for _ in range(4)] for _ in range(4)]
            for hl_ in range(4):
                P.memset(qpad[:, hl_, :], 0.0, BqT[hl_], eng="dve")
            kT = {}
            BkT = {}
            for nm in ("ks", "kw"):
                kT[nm] = P.sb("kT_" + nm, [128, S], BF16)
                BkT[nm] = [Buf() for _ in range(4)]
            KK = {}
            BKK = {}
            for nm in ("kc", "vc"):
                KK[nm] = P.sb("KK_" + nm, [128, S], BF16)
                BKK[nm] = [Buf() for _ in range(4)]
                P.memset(KK[nm][:], 0.0, BKK[nm], eng="dve")
            Vs = P.sb("Vs", [128, NT, 65], BF16)
            Vw = P.sb("Vw", [128, NT, 65], BF16)
            BVs = [Buf() for _ in range(NT)]
            BVw = [Buf() for _ in range(NT)]
            P.memset(Vs[:], 1.0, BVs, eng="dve")
            P.memset(Vw[:], 1.0, BVw, eng="dve")
            HT = P.sb("HT", [128, 128], BF16)
            BHT = Buf()
            kcT = P.sb("kcT", [128, 128], BF16)
            BkcT = Buf()
            P.memset(kcT[:], 0.0, [BkcT], eng="dve")
            VC = P.sb("VC", [128, 97], BF16)
            BVC = Buf()
            P.memset(VC[:], 0.0, [BVC], eng="dve")
            P.memset(VC[:, 64:65], 1.0, [BVC], eng="dve")
            P.cp(VC[:, 65:97], cst[:, C_OV:C_OV + 32], [Bcst], [BVC])
            zer = P.sb("zer", [128, 128], BF16)
            Bzer = Buf()
            P.memset(zer[:], 0.0, [Bzer], eng="dve")
            oacc = P.sb("oacc", [128, NT, 256], F32)
            Boacc = [Buf() for _ in range(NT)]
            selbT = P.sb("selbT", [128, S], BF16)
            BselbT = [Buf() for _ in range(4)]
            P.memset(selbT[:], 0.0, BselbT, eng="dve")
            pT_ring = P.sb_ring("pT", [128, 512], BF16, 6)
            tmp_ring = P.sb_ring("rt", [128, 512], F32, 4)
            small = P.sb_ring("sm", [128, 8], F32, 4)
            imp_ring = P.sb_ring("imp", [128, 4, 32], F32, 2)
            imp2_ring = P.sb_ring("imp2", [128, 40], F32, 2)
            selb_ring = P.sb_ring("selb", [128, 32], BF16, 3)
            otmp_ring = P.sb_ring("otmp", [128, 4, 64], F32, 3)
            obf_ring = P.sb_ring("obf", [128, 256], BF16, 2)
            impall = P.sb("impall", [128, NT, 128], F32)
            Bimp = [Buf() for _ in range(NT)]
            ons_ring = P.sb_ring("ons", [128, 2, 128], BF16, 2)

            def make_rot(dst, Bd, srct, Bs):
                dv = dst[:].rearrange("p k (r d) -> p (k r) d", d=64)
                sv = srct[:].rearrange("p k (r d) -> p (k r) d", d=64)
                P.ts(dv[:, :, 0:8], sv[:, :, 8:16], -1.0, None, ALU.mult, None, [Bs], [Bd])
                P.cp(dv[:, :, 8:16], sv[:, :, 0:8], [Bs], [Bd])

            def rope_evac(ps1, B1, ps2, B2, c0, c1, out_ap, Bouts, cstep=None, split=None):
                t1, Bt1 = tmp_ring.next()
                t2, Bt2 = tmp_ring.next()
                n = (out_ap if split is None else split[0][0]).shape[-1]
                if cstep is None:
                    ca, sa = Ct[:, c0:c1], St[:, c0:c1]
                else:
                    ca, sa = Ct[:, c0:c1:cstep], St[:, c0:c1:cstep]
                P.tt(t1[:, 0:n], ps1, ca, ALU.mult, [B1, Btab], [Bt1])
                P.tt(t2[:, 0:n], ps2, sa, ALU.mult, [B2, Btab], [Bt2])
                if split is None:
                    P.tt(out_ap, t1[:, 0:n], t2[:, 0:n], ALU.add, [Bt1, Bt2], Bouts, eng="pool")
                else:
                    for (oap, Bo_), (r0, r1) in zip(split, ((0, 64), (64, 128))):
                        P.tt(oap, t1[r0:r1, 0:n], t2[r0:r1, 0:n], ALU.add, [Bt1, Bt2], [Bo_], eng="dve")

            for g in range(NKV):
                P.load(wq[:], win_v[:, :, OFF_Q + 256 * g:OFF_Q + 256 * g + 256], (), [Bwq], eng="pool")
                make_rot(wqr, Bwqr, wq, Bwq)
                for nm, off in (("kc", OFF_KC), ("vc", OFF_VC), ("ks", OFF_KS), ("kw", OFF_KW)):
                    for r in range(2):
                        P.load(wdup[nm][:, :, r * 64:(r + 1) * 64], win_v[:, :, off + 64 * g:off + 64 * g + 64], (), [Bwdup[nm]], eng="pool")
                for nm in ("ks", "kw"):
                    make_rot(wrot[nm], Bwrot[nm], wdup[nm], Bwdup[nm])
                P.load(wv[:, :, 0:64], win_v[:, :, OFF_VS + 64 * g:OFF_VS + 64 * g + 64], (), [Bwv], eng="pool")
                P.load(wv[:, :, 64:128], win_v[:, :, OFF_VW + 64 * g:OFF_VW + 64 * g + 64], (), [Bwv], eng="pool")
                for tb in range(4):
                    tok = slice(tb * 512, (tb + 1) * 512)
                    for cc in range(2):
                        ps1, B1 = C.psA.next()
                        ps2, B2 = C.psA.next()
                        for k in range(KC):
                            P.mm(ps1[:], wq[:, k, cc * 128:(cc + 1) * 128], xnT[:, k, tok], k == 0, k == KC - 1, xr(tb) + [Bwq], [B1])
                        for k in range(KC):
                            P.mm(ps2[:], wqr[:, k, cc * 128:(cc + 1) * 128], xnT[:, k, tok], k == 0, k == KC - 1, xr(tb) + [Bwqr], [B2])
                        rope_evac(ps1[:], B1, ps2[:], B2, tb * 512, (tb + 1) * 512, None, None,
                                  split=[(qpad[0:64, 2 * cc, tok], BqT[2 * cc][tb]), (qpad[64:128, 2 * cc + 1, tok], BqT[2 * cc + 1][tb])])
                    for nm in ("ks", "kw"):
                        ps1, B1 = C.psA.next()
                        ps2, B2 = C.psA.next()
                        for k in range(KC):
                            P.mm(ps1[:], wdup[nm][:, k, :], xnT[:, k, tok], k == 0, k == KC - 1, xr(tb) + [Bwdup[nm]], [B1])
                        for k in range(KC):
                            P.mm(ps2[:], wrot[nm][:, k, :], xnT[:, k, tok], k == 0, k == KC - 1, xr(tb) + [Bwrot[nm]], [B2])
                        rope_evac(ps1[:], B1, ps2[:], B2, tb * 512, (tb + 1) * 512, kT[nm][:, tok], [BkT[nm][tb]])
                    for nm in ("kc", "vc"):
                        ps1, B1 = C.psA.next()
                        for k in range(KC):
                            P.mm(ps1[:], wdup[nm][:, k, :], xnT[:, k, tok], k == 0, k == KC - 1, xr(tb) + [Bwdup[nm]], [B1])
                        P.act(KK[nm][0:64, tok], ps1[0:64, :], AF.Copy, [B1], [BKK[nm][tb]])
                        if tb == 0:
                            P.cp(KK[nm][64:128, 0:511], ps1[64:128, 1:512], [B1], [BKK[nm][0]])
                        else:
                            P.cp(KK[nm][64:128, tb * 512 - 1:tb * 512 + 511], ps1[64:128, :], [B1], [BKK[nm][tb], BKK[nm][tb - 1]])
                for t in range(NT):
                    ps, Bps = C.psB.next()
                    for k in range(KC):
                        P.mm(ps[:, 0:128], xnT[:, k, t * 128:(t + 1) * 128], wv[:, k, :], k == 0, k == KC - 1, [BxnT[t], Bwv], [Bps])
                    P.cp(Vs[:, t, 0:64], ps[:, 0:64], [Bps], [BVs[t]])
                    P.act(Vw[:, t, 0:64], ps[:, 64:128], AF.Copy, [Bps], [BVw[t]])
                for kv, nm in (("k", "kc"), ("v", "vc")):
                    psz, Bz = C.psA.next()
                    for m in range(16):
                        P.mm(psz[:, 0:127], W1[kv][:, m, :], KK[nm][:, 2 * m:2 * m + 2017:16], m == 0, m == 15, BKK[nm] + [BW1], [Bz])
                    P.act(HT[:, 0:127], psz[:, 0:127], AF.Gelu_apprx_tanh, [Bz, Bcb], [BHT], bias=cbias[kv][:, 0:1])
                    if kv == "k":
                        ps1, B1 = C.psA.next()
                        ps2, B2 = C.psA.next()
                        P.mm(ps1[:, 0:127], w2k[:, :], HT[:, 0:127], True, True, [Bw2, BHT], [B1])
                        P.mm(ps2[:, 0:127], w2kr[:, :], HT[:, 0:127], True, True, [Bw2, BHT], [B2])
                        rope_evac(ps1[:, 0:127], B1, ps2[:, 0:127], B2, 31, 2048, kcT[:, 0:127], [BkcT], cstep=16)
                    else:
                        ps, Bps = C.psB.next()
                        P.mm(ps[0:127, 0:64], HT[:, 0:127], w2v[:, :], True, True, [BHT, Bw2], [Bps])
                        P.cp(VC[0:127, 0:64], ps[0:127, 0:64], [Bps], [BVC])
                for tb in range(4):
                    tok = slice(tb * 512, (tb + 1) * 512)
                    PT = []
                    for hl in range(4):
                        cc, base = hl // 2, 64 * (hl % 2)
                        ps, Bps = C.psA.next()
                        P.mm(ps[:], kcT[:, :], qpad[:, hl, tok], True, False, [BkcT, BqT[hl][tb]], [Bps])
                        P.mm(ps[:], ident[:], mk[:, 8 + tb, :], False, True, [Bid, Bmk], [Bps])
                        pt, Bpt = pT_ring.next()
                        P.act(pt[:], ps[:], AF.Exp, [Bps], [Bpt], scale=0.125)
                        PT.append((pt, Bpt))
                    for qt in range(4):
                        t = 4 * tb + qt
                        pso, Bo = C.psB.next()
                        for hl in range(4):
                            P.mm(pso[:, hl * 97:(hl + 1) * 97], PT[hl][0][:, qt * 128:(qt + 1) * 128], VC[:, :], True, True,
                                 [PT[hl][1], BVC], [Bo], inc=(hl == 3))
                        pv = pso[:, 0:388].rearrange("p (h c) -> p h c", c=97)
                        sm, Bsm = small.next()
                        P.ts(sm[:, 0:4].unsqueeze(2), pv[:, :, 64:65], 1e-30, None, ALU.max, None, [Bo], [Bsm])
                        P.recip(sm[:, 0:4], sm[:, 0:4], [Bsm], [Bsm])
                        P.tt(sm[:, 4:8], sm[:, 0:4], gs[:, t, 4 * g:4 * g + 4], ALU.mult, [Bsm, Bgs[t]], [Bsm])
                        P.tt(oacc[:, t, :].rearrange("p (h d) -> p h d", d=64), pv[:, :, 0:64],
                             sm[:, 4:8].unsqueeze(2).broadcast_to([128, 4, 64]), ALU.mult, [Bo, Bsm], [Boacc[t]])
                        if C.dbg is not None:
                            P.load(C.dbg[0, t * 128:(t + 1) * 128, 256 * g:256 * g + 256], oacc[:, t, :], [Boacc[t]], [C.Bdbg])
                        P.tt(impall[:, t, :].rearrange("p (h j) -> p h j", j=32), pv[:, :, 65:97],
                             sm[:, 0:4].unsqueeze(2).broadcast_to([128, 4, 32]), ALU.mult, [Bo, Bsm], [Bimp[t]])

                def chain_A(t):
                    i2, Bi2 = imp2_ring.next()
                    P.op("dve", lambda e, o=i2[:, 0:32], i=impall[:, t, :].rearrange("p (h j) -> p j h", j=32): e.tensor_reduce(out=o, in_=i, axis=mybir.AxisListType.X, op=ALU.add),
                         [Bimp[t]], [Bi2])
                    a0 = C_A + 32 - 2 * t
                    b0 = C_B + 32 - 2 * t
                    P.tt(i2[:, 0:32], i2[:, 0:32], cst[:, a0:a0 + 32], ALU.mult, [Bi2, Bcst], [Bi2])
                    P.tt(i2[:, 0:32], i2[:, 0:32], cst[:, b0:b0 + 32], ALU.add, [Bi2, Bcst], [Bi2])
                    P.cp(i2[:, 0:1], cst[:, C_FORCE:C_FORCE + 1], [Bi2, Bcst], [Bi2])
                    P.op("dve", lambda e, o=i2[:, 32:40], i=i2[:, 0:32]: e.max(out=o, in_=i), [Bi2], [Bi2])
                    sb_, Bsb = selb_ring.next()
                    P.ts(sb_[:], i2[:, 0:32], i2[:, 39:40], NEG, ALU.is_lt, ALU.mult, [Bi2], [Bsb])
                    pendB[t] = (sb_, Bsb, t, t // 4)

                pendB = {}
                slots = [[("A", 0)]] + [[("A", u), ("B", u - 1)] for u in range(1, NT)] + [[("B", NT - 1)]]

                def run_slot(sl):
                    for kind, u in sl:
                        if kind == "A":
                            chain_A(u)
                        else:
                            deferred_B.append(pendB.pop(u))
                            flush_B()

                def flush_B():
                    if deferred_B:
                        sb_, Bsb, t, tb = deferred_B.pop(0)
                        ptr, Bptr = C.pst.next()
                        P.tr(ptr[0:32, 0:128], sb_[:, :], ident[:], [Bsb, Bid], [Bptr])
                        P.cp(selbT[0:32, t * 128:(t + 1) * 128], ptr[0:32, 0:128], [Bptr], [BselbT[tb]])

                deferred_B = []

                def flush_one():
                    if slots:
                        run_slot(slots.pop(0))

                deferred = slots
                items = []
                for branch in (2, 1):
                    for hl in range(4):
                        for tb in range(4):
                            j0 = 0 if branch == 1 else max(0, 4 * tb - 4)
                            for jt in range(j0, 4 * tb + 4):
                                items.append((hl, branch, tb, jt, jt == j0, jt == 4 * tb + 3))
                n_win = sum(1 for it_ in items if it_[1] == 2)
                state = {}

                def emit_scores(it):
                    hl, branch, tb, jt, first, last = it
                    cc, base = hl // 2, 64 * (hl % 2)
                    d = jt - 4 * tb
                    c0, c1 = 0, 512
                    if TRIM:
                        if d >= 0:
                            c0, c1 = 128 * d, 512
                        elif branch == 2:
                            c0, c1 = 0, 128 * (d + 4 + 1)
                    kt, Bkt = (kT["ks"], BkT["ks"]) if branch == 1 else (kT["kw"], BkT["kw"])
                    ps, Bps = C.psA.next()
                    q0 = tb * 512
                    P.mm(ps[:, c0:c1], kt[:, jt * 128:(jt + 1) * 128], qpad[:, hl, q0 + c0:q0 + c1], True, False,
                         [Bkt[jt // 4], BqT[hl][tb]], [Bps])
                    if branch == 1:
                        if tb > 0:
                            P.mm(ps[:, c0:c1], em[:, jt, :], selbT[:, q0 + c0:q0 + c1], False, d < 0, [Bem, BselbT[tb]], [Bps])
                        if d >= 0:
                            P.mm(ps[:, c0:c1], ident[:], mk[:, d, c0:c1], False, True, [Bid, Bmk], [Bps])
                    else:
                        mi = d if d >= 0 else (4 + d + 4)
                        P.mm(ps[:, c0:c1], ident[:], mk[:, mi, c0:c1], False, True, [Bid, Bmk], [Bps])
                    state[it] = (ps, Bps, c0, c1)

                def emit_rest(it):
                    hl, branch, tb, jt, first, last = it
                    h = 4 * g + hl
                    ps, Bps, c0, c1 = state.pop(it)
                    Vt, BVt = (Vs, BVs) if branch == 1 else (Vw, BVw)
                    if first:
                        pso, Bo = C.psB.next()
                        state[(hl, branch, tb)] = (pso, Bo)
                        P.mm(pso[:, 0:260], zer[:, :], mk[:, 0, 0:260], True, False, [Bzer, Bmk], [Bo], inc=False)
                    pso, Bo = state[(hl, branch, tb)]
                    pt, Bpt = pT_ring.next()
                    P.act(pt[:, c0:c1], ps[:, c0:c1], AF.Exp, [Bps], [Bpt], scale=0.125)
                    for qt in range(4):
                        T = 4 * tb + qt
                        lo = 0 if branch == 1 else max(0, T - 4)
                        if lo <= jt <= T:
                            assert c0 <= qt * 128 and (qt + 1) * 128 <= c1
                            P.mm(pso[:, qt * 65:(qt + 1) * 65], pt[:, qt * 128:(qt + 1) * 128], Vt[:, jt, :],
                                 False, (jt == 4 * tb + 3 and qt == 3), [Bpt, BVt[jt]], [Bo])
                    if last:
                        del state[(hl, branch, tb)]
                        pv = pso[:, 0:260].rearrange("p (q c) -> p q c", c=65)
                        sm, Bsm = small.next()
                        P.ts(sm[:, 0:4].unsqueeze(2), pv[:, :, 64:65], 1e-30, None, ALU.max, None, [Bo], [Bsm])
                        P.recip(sm[:, 0:4], sm[:, 0:4], [Bsm], [Bsm])
                        gcol = 16 * branch + h
                        P.tt(sm[:, 4:8].unsqueeze(2), sm[:, 0:4].unsqueeze(2), gs[:, 4 * tb:4 * tb + 4, gcol:gcol + 1], ALU.mult,
                             [Bsm] + [Bgs[4 * tb + i] for i in range(4)], [Bsm])
                        ot, Bot = otmp_ring.next()
                        P.tt(ot[:], pv[:, :, 0:64], sm[:, 4:8].unsqueeze(2).broadcast_to([128, 4, 64]), ALU.mult, [Bo, Bsm], [Bot])
                        ov = oacc[:, 4 * tb:4 * tb + 4, hl * 64:(hl + 1) * 64]
                        Bov = [Boacc[4 * tb + i] for i in range(4)]
                        P.tt(ov, ov, ot[:], ALU.add, Bov + [Bot], Bov, eng="pool")

                if PIPE:
                    for j in range(min(PIPE_DEPTH, len(items))):
                        emit_scores(items[j])
                    for i, it in enumerate(items):
                        if i + PIPE_DEPTH < len(items):
                            if i + PIPE_DEPTH >= n_win:
                                while deferred:
                                    flush_one()
                            emit_scores(items[i + PIPE_DEPTH])
                        emit_rest(it)
                        if i % 5 == 3:
                            flush_one()
                else:
                    while deferred:
                        flush_one()
                    for it in items:
                        emit_scores(it)
                        emit_rest(it)
                for t in range(NT):
                    ob, Bob = obf_ring.next()
                    P.act(ob[:], oacc[:, t, :], AF.Copy, [Boacc[t]], [Bob])
                    ptr, Bptr = C.pst.next()
                    for cc in range(2):
                        P.tr(ptr[:, cc * 128:(cc + 1) * 128], ob[:, cc * 128:(cc + 1) * 128], ident[:], [Bob, Bid], [Bptr], inc=(cc == 1))
                    ons, Bons = ons_ring.next()
                    P.cp(ons[:], ptr[:, 0:256].rearrange("p (c q) -> p c q", c=2), [Bptr], [Bons])
                    P.load(onsa_d[:, 2 * g:2 * g + 2, t * 128:(t + 1) * 128], ons[:], [Bons], [BonD[g][t]])

        o_poolT = P.sb("opoolT", [128, 4, S], BF16)
        BopT = [[Buf() for _ in range(4)] for _ in range(4)]
        with P.scope():
            wpin = P.sb("wpin", [128, KC, 512], BF16)
            Bwp = Buf()
            P.load(wpin[:], win_v[:, :, OFF_POOL:OFF_POOL + 512], (), [Bwp], eng="pool")
            pw = P.sb("pw", [128, 4, 128], BF16)
            Bpw = Buf()
            P.load(pw[:], W["pool_w"].rearrange("g c d -> c g d"), (), [Bpw], eng="pool")
            psc = P.sb("psc", [128, 4], F32)
            Bpsc = Buf()
            P.load_nc(psc[:], W["pool_scale"].rearrange("o (g p) -> p (o g)", p=128), (), [Bpsc])
            ub = [P.sb(f"ub{i}", [128, 16 + S], F32) for i in range(3)]
            Bub = [Buf() for _ in range(3)]
            for i in range(3):
                P.memset(ub[i][:, 0:16], 0.0, [Bub[i]], eng="dve")
            pl = P.sb("pl", [128, S], BF16)
            Bpl = Buf()
            ftmp = P.sb("ftmp", [128, 16], F32)
            Bft = Buf()
            for gi, w in enumerate((2, 4, 8, 16)):
                for tb in range(4):
                    ps, Bps = C.psA.next()
                    for k in range(KC):
                        P.mm(ps[:], wpin[:, k, gi * 128:(gi + 1) * 128], xnT[:, k, tb * 512:(tb + 1) * 512], k == 0, k == KC - 1, xr(tb) + [Bwp], [Bps])
                    P.act(ub[0][:, 16 + tb * 512:16 + (tb + 1) * 512], ps[:], AF.Copy, [Bps], [Bub[0]])
                cur = 0
                pp = [1, 2]
                step = 1
                for _ in range(gi + 1):
                    nxt = pp[0]
                    pp = pp[::-1]
                    P.tt(ub[nxt][:, 16:16 + S], ub[cur][:, 16:16 + S], ub[cur][:, 16 - step:16 - step + S], ALU.add, [Bub[cur]], [Bub[nxt]])
                    cur = nxt
                    step *= 2
                P.stt(pl[:], ub[cur][:, 16:16 + S], 1.0 / w, ub[0][:, 16:16 + S], ALU.mult, ALU.subtract, [Bub[cur], Bub[0]], [Bpl])
                P.tt(ftmp[:, 0:w - 1], ub[cur][:, 16:16 + w - 1], cst[:, C_RC:C_RC + w - 1], ALU.mult, [Bub[cur], Bcst], [Bft])
                P.tt(pl[:, 0:w - 1], ftmp[:, 0:w - 1], ub[0][:, 16:16 + w - 1], ALU.subtract, [Bft, Bub[0]], [Bpl])
                for tb in range(4):
                    ps, Bps = C.psA.next()
                    P.mm(ps[:], pw[:, gi, :], pl[:, tb * 512:(tb + 1) * 512], True, True, [Bpw, Bpl], [Bps])
                    P.act(o_poolT[:, gi, tb * 512:(tb + 1) * 512], ps[:], AF.Copy, [Bps, Bpsc], [BopT[gi][tb]], scale=psc[:, gi:gi + 1])

        with P.scope():
            o_nsaT = P.sb("onsaT", [128, KC, S], BF16)
            BonT = [Buf() for _ in range(4)]
            for tb_ in range(4):
                P.load(o_nsaT[:, :, tb_ * 512:(tb_ + 1) * 512], onsa_d[:, :, tb_ * 512:(tb_ + 1) * 512],
                       [BonD[g_][tb_ * 4 + i] for g_ in range(4) for i in range(4)], [BonT[tb_]])
            yT = P.sb("yT", [128, KC, S], BF16)
            ByT = [[Buf() for _ in range(4)] for _ in range(KC)]
            wab_v = W["w_attn_branch"].rearrange("(kc p) c -> p kc c", p=128)
            wpb_v = W["w_pool_branch"].rearrange("(kc p) c -> p kc c", p=128)
            with P.scope():
                wab_r = P.sb_ring("wab", [128, KC, 256], BF16, 2)
                wpb_r = P.sb_ring("wpb", [128, 4, 256], BF16, 2)
                wga_r = P.sb_ring("wga", [128, KC, 256], BF16, 2)
                wgp_r = P.sb_ring("wgp", [128, KC, 256], BF16, 2)
                sg_r = P.sb_ring("msg", [128, 512], F32, 4)
                t_r = P.sb_ring("mt", [128, 512], F32, 4)
                for dp in range(4):
                    wab, Bwab = wab_r.next()
                    wpb, Bwpb = wpb_r.next()
                    wga, Bwga = wga_r.next()
                    wgp, Bwgp = wgp_r.next()
                    cs = slice(dp * 256, (dp + 1) * 256)
                    P.load(wab[:], wab_v[:, :, cs], (), [Bwab], eng="pool")
                    P.load(wpb[:], wpb_v[:, :, cs], (), [Bwpb], eng="pool")
                    P.load(wga[:], win_v[:, :, OFF_MG + dp * 256:OFF_MG + (dp + 1) * 256], (), [Bwga], eng="pool")
                    P.load(wgp[:], win_v[:, :, OFF_MG + 1024 + dp * 256:OFF_MG + 1024 + (dp + 1) * 256], (), [Bwgp], eng="pool")
                    for dl in range(2):
                        dc = dp * 2 + dl
                        cl = slice(dl * 128, (dl + 1) * 128)
                        for tb in range(4):
                            tok = slice(tb * 512, (tb + 1) * 512)
                            pa, Bpa = C.psA.next()
                            pp_, Bpp = C.psA.next()
                            pga, Bpga = C.psA.next()
                            pgp, Bpgp = C.psA.next()
                            for k in range(KC):
                                P.mm(pa[:], wab[:, k, cl], o_nsaT[:, k, tok], k == 0, k == KC - 1,
                                     [Bwab, BonT[tb]], [Bpa])
                            for k in range(4):
                                P.mm(pp_[:], wpb[:, k, cl], o_poolT[:, k, tok], k == 0, k == 3, [Bwpb, BopT[k][tb]], [Bpp])
                            for k in range(KC):
                                P.mm(pga[:], wga[:, k, cl], xnT[:, k, tok], k == 0, k == KC - 1, xr(tb) + [Bwga], [Bpga])
                            for k in range(KC):
                                P.mm(pgp[:], wgp[:, k, cl], xnT[:, k, tok], k == 0, k == KC - 1, xr(tb) + [Bwgp], [Bpgp])
                            sa, Bsa = sg_r.next()
                            sp_, Bsp = sg_r.next()
                            P.act(sa[:], pga[:], AF.Sigmoid, [Bpga], [Bsa])
                            P.act(sp_[:], pgp[:], AF.Sigmoid, [Bpgp], [Bsp])
                            t1, Bt1 = t_r.next()
                            t2, Bt2 = t_r.next()
                            P.tt(t1[:], pa[:], sa[:], ALU.mult, [Bpa, Bsa], [Bt1])
                            P.tt(t2[:], pp_[:], sp_[:], ALU.mult, [Bpp, Bsp], [Bt2])
                            P.tt(yT[:, dc, tok], t1[:], t2[:], ALU.add, [Bt1, Bt2], [ByT[dc][tb]], eng="pool")
            with P.scope():
                wo = P.sb("wo", [128, KC, D], BF16)
                Bwo = Buf()
                P.load(wo[:], W["w_out"].rearrange("(kc p) c -> p kc c", p=128), (), [Bwo], eng="pool")
                gbc = P.sb("gbc3", [128, D], F32)
                Bg = Buf()
                P.load(gbc[:], W["g_mix_post"].partition_broadcast(128), (), [Bg])
                xt_ring = P.sb_ring("mxr", [128, D], F32, 2)
                st_ring = P.sb_ring("mst", [128, 8], F32, 3)
                tmp2_ring = P.sb_ring("mtmp", [128, 512], F32, 3)
                for t in range(NT):
                    xt, Bxt = xt_ring.next()
                    P.load(xt[:], src[t * 128:(t + 1) * 128, :], [Bsrc[t]], [Bxt])
                    pos_, Bpos = [], []
                    for dh in range(2):
                        po, Bpo = C.psA.next()
                        for k in range(KC):
                            P.mm(po[:], yT[:, k, t * 128:(t + 1) * 128], wo[:, k, dh * 512:(dh + 1) * 512], k == 0, k == KC - 1,
                                 [ByT[k][t // 4], Bwo], [Bpo])
                        pos_.append(po)
                        Bpos.append(Bpo)
                    post_norm_residual(P, C, pos_, Bpos, gbc, Bg, xt, Bxt, 1.0, dst[t * 128:(t + 1) * 128, :], Bdst[t], st_ring, tmp2_ring)


_NC_CACHE = {}


def kernel(**inputs):
    n = 8
    if "nc" not in _NC_CACHE:
        _NC_CACHE["nc"] = build_nc()[0]
    nc = _NC_CACHE["nc"]
    in_maps = []
    hc, hm, he = host_consts()
    for b in range(n):
        m = {"kconsts": hc, "kmasks": hm, "kemat": he}
        for k, v in inputs.items():
            v = np.asarray(v)
            if k == "x":
                m[k] = np.ascontiguousarray(v[b])
            elif k == "positions":
                m[k] = np.ascontiguousarray(v[b].reshape(1, S))
            else:
                a = v[0]
                if a.ndim == 1:
                    a = a.reshape(1, -1)
                m[k] = np.ascontiguousarray(a)
        in_maps.append(m)
    res = run_bass_kernel_spmd(nc, in_maps, core_ids=list(range(n)))
    return np.stack([np.asarray(r["out"]).reshape(S, D) for r in res.results], axis=0).astype(np.float32)
```
